# Optimizing a Trainium2 kernel written in Bass

```python
import math
import jax, jax.numpy as jnp
from jax import lax
import numpy as np

D_MODEL = 4096
BATCH = 4
SEQ = 2048
DEPTH = 4
DEC_BATCH = 8
DEC_SEQ = 16
PAST_LEN = 2048

CHUNK = 64
EPS = 1e-6
N_AB_LAYERS = (DEPTH + 1) // 2
N_C_LAYERS = DEPTH // 2
QUERY_BLOCK = 128
MLA_HEADS = 16
MLA_NOPE = 128
MLA_ROPE = 64
MLA_QK = MLA_NOPE + MLA_ROPE
MLA_V = 128
Q_RANK = 768
KV_RANK = 512
ROPE_THETA = 10000.0
A_WIDTH = MLA_HEADS * MLA_V
B_WIDTH = D_MODEL - A_WIDTH
SSM_GROUP = 16
SSM_GROUPS = B_WIDTH // SSM_GROUP
SSM_STATE = 64
DT_MIN = 1e-3
DT_MAX = 1e-1
SWA_HEADS = 64
SWA_KV_HEADS = 8
SWA_GQ = SWA_HEADS // SWA_KV_HEADS
SWA_HEAD_DIM = D_MODEL // SWA_HEADS
C_WIDTH = SWA_HEADS * SWA_HEAD_DIM
SWA_KV_DIM = SWA_KV_HEADS * SWA_HEAD_DIM
WINDOW = 128
WIN_CHUNKS = WINDOW // CHUNK
REL_BUCKETS = 32
REL_MAX_DIST = 128
AB_IN = Q_RANK + KV_RANK + MLA_ROPE + A_WIDTH + B_WIDTH + B_WIDTH
AB_SPLITS = (Q_RANK, Q_RANK + KV_RANK, Q_RANK + KV_RANK + MLA_ROPE,
             Q_RANK + KV_RANK + MLA_ROPE + A_WIDTH,
             Q_RANK + KV_RANK + MLA_ROPE + A_WIDTH + B_WIDTH)
C_IN = C_WIDTH + 2 * SWA_KV_DIM + C_WIDTH
C_SPLITS = (C_WIDTH, C_WIDTH + SWA_KV_DIM, C_WIDTH + 2 * SWA_KV_DIM)

kernel_name = "hybrid_streaming_mla_s5_swa_step"


def rms_norm(x, g):
    xf = x.astype(jnp.float32)
    y = xf * lax.rsqrt(jnp.mean(xf * xf, axis=-1, keepdims=True) + EPS)
    return (y * g.astype(jnp.float32)).astype(x.dtype)


def rope(x, pos):
    half = x.shape[-1] // 2
    inv = ROPE_THETA ** (-jnp.arange(half, dtype=jnp.float32) / half)
    ang = pos.astype(jnp.float32)[:, None, None] * inv
    cos, sin = jnp.cos(ang), jnp.sin(ang)
    xf = x.astype(jnp.float32)
    x1, x2 = xf[..., :half], xf[..., half:]
    return jnp.concatenate([x1 * cos - x2 * sin, x1 * sin + x2 * cos], axis=-1).astype(x.dtype)


def chunk_causal_mask(q_pos, k_pos):
    return (k_pos[None, :] // CHUNK) <= (q_pos[:, None] // CHUNK)


def rel_bucket(rel):
    nb = REL_BUCKETS // 2
    max_exact = nb // 2
    ret = jnp.where(rel > 0, nb, 0)
    n = jnp.abs(rel)
    nf = jnp.maximum(n, 1).astype(jnp.float32)
    large = max_exact + (jnp.log(nf / max_exact) / math.log(REL_MAX_DIST / max_exact)
                         * (nb - max_exact)).astype(jnp.int32)
    large = jnp.minimum(large, nb - 1)
    return ret + jnp.where(n < max_exact, n, large)


def rel_bias(table, rel):
    b = jnp.take(table, rel_bucket(rel), axis=0).astype(jnp.float32)
    return jnp.transpose(b, (2, 0, 1)).reshape(SWA_KV_HEADS, SWA_GQ, rel.shape[0], rel.shape[1])


def mla_attention(q, k, v, mask):
    s = jnp.einsum('bqhd,bkhd->bhqk', q, k, preferred_element_type=jnp.float32) * (MLA_QK ** -0.5)
    s = jnp.where(mask, s, -1e30)
    p = jax.nn.softmax(s, axis=-1)
    return jnp.einsum('bhqk,bkhd->bqhd', p.astype(v.dtype), v)


def mla_prompt_attention(q, k, v):
    bsz, L = q.shape[:2]
    nqb = L // QUERY_BLOCK
    qb = jnp.moveaxis(q.reshape(bsz, nqb, QUERY_BLOCK, MLA_HEADS, MLA_QK), 1, 0)
    k_pos = jnp.arange(L)

    def one_block(args):
        q_blk, start = args
        mask = chunk_causal_mask(start + jnp.arange(QUERY_BLOCK), k_pos)
        return mla_attention(q_blk, k, v, mask)

    o = lax.map(one_block, (qb, jnp.arange(nqb) * QUERY_BLOCK))
    return jnp.moveaxis(o, 0, 1).reshape(bsz, L, MLA_HEADS, MLA_V)


def mla_keys(lat, k_rope, w_ukv, gk):
    bsz, T = lat.shape[:2]
    kv = (lat @ w_ukv).reshape(bsz, T, MLA_HEADS, MLA_NOPE + MLA_V)
    k_pe = jnp.broadcast_to(k_rope[:, :, None, :], (bsz, T, MLA_HEADS, MLA_ROPE))
    k = rms_norm(jnp.concatenate([kv[..., :MLA_NOPE], k_pe], axis=-1), gk)
    return k, kv[..., MLA_NOPE:]


def s5_discretize(lam_re, lam_im, b_re, b_im, log_dt):
    lam_re = jnp.minimum(lam_re.astype(jnp.float32), -1e-4)
    lam_im = lam_im.astype(jnp.float32)
    dt = jnp.exp(log_dt.astype(jnp.float32))[:, None]
    mag = jnp.exp(lam_re * dt)
    lb_re, lb_im = mag * jnp.cos(lam_im * dt), mag * jnp.sin(lam_im * dt)
    nr, ni = lb_re - 1.0, lb_im
    den = lam_re * lam_re + lam_im * lam_im
    f_re = (nr * lam_re + ni * lam_im) / den
    f_im = (ni * lam_re - nr * lam_im) / den
    b_re, b_im = b_re.astype(jnp.float32), b_im.astype(jnp.float32)
    bb_re = f_re[..., None] * b_re - f_im[..., None] * b_im
    bb_im = f_re[..., None] * b_im + f_im[..., None] * b_re
    return lb_re, lb_im, bb_re, bb_im


def s5_combine(e1, e2):
    a1r, a1i, b1r, b1i = e1
    a2r, a2i, b2r, b2i = e2
    return (a2r * a1r - a2i * a1i, a2r * a1i + a2i * a1r,
            a2r * b1r - a2i * b1i + b2r, a2r * b1i + a2i * b1r + b2i)


def s5_block(u, h0_re, h0_im, disc):
    lb_re, lb_im, bb_re, bb_im = disc
    bu_re = jnp.einsum('btgc,gpc->btgp', u, bb_re)
    bu_im = jnp.einsum('btgc,gpc->btgp', u, bb_im)
    bu_re = bu_re.at[:, 0].add(lb_re * h0_re - lb_im * h0_im)
    bu_im = bu_im.at[:, 0].add(lb_re * h0_im + lb_im * h0_re)
    a_re = jnp.broadcast_to(lb_re, bu_re.shape)
    a_im = jnp.broadcast_to(lb_im, bu_re.shape)
    _, _, hr, hi = lax.associative_scan(s5_combine, (a_re, a_im, bu_re, bu_im), axis=1)
    return hr, hi


def s5_readout(hr, hi, c_re, c_im):
    return (jnp.einsum('btgp,gcp->btgc', hr, c_re.astype(jnp.float32))
            - jnp.einsum('btgp,gcp->btgc', hi, c_im.astype(jnp.float32)))


def s5_prompt(u, disc, c_re, c_im):
    bsz, L = u.shape[:2]
    nc = L // CHUNK
    uc = jnp.moveaxis(u.reshape(bsz, nc, CHUNK, SSM_GROUPS, SSM_GROUP), 1, 0)
    h0 = jnp.zeros((bsz, SSM_GROUPS, SSM_STATE), jnp.float32)

    def step(carry, u_blk):
        hr, hi = s5_block(u_blk, carry[0], carry[1], disc)
        return (hr[:, -1], hi[:, -1]), s5_readout(hr, hi, c_re, c_im)

    (hr, hi), ys = lax.scan(step, (h0, h0), uc)
    return jnp.moveaxis(ys, 0, 1).reshape(bsz, L, SSM_GROUPS, SSM_GROUP), hr, hi


def ab_layer(x, past, params):
    (norm_g, w_in, q_lora_g, kv_lora_g, w_uq, w_ukv, gq, gk, lam_re, lam_im,
     b_re, b_im, c_re, c_im, log_dt, d_skip, w_glu, w_out) = params
    bsz, T, _ = x.shape
    h = rms_norm(x, norm_g)
    c_q, c_kv, k_rope, gate_a, u, gate_b = jnp.split(h @ w_in, AB_SPLITS, axis=-1)
    c_q = rms_norm(c_q, q_lora_g)
    c_kv = rms_norm(c_kv, kv_lora_g)
    past_len = 0 if past is None else past[0].shape[1]
    pos = past_len + jnp.arange(T)
    q = (c_q @ w_uq).reshape(bsz, T, MLA_HEADS, MLA_QK)
    q = rms_norm(jnp.concatenate([q[..., :MLA_NOPE], rope(q[..., MLA_NOPE:], pos)], axis=-1), gq)
    k_rope = rope(k_rope[:, :, None, :], pos)[:, :, 0]
    uf = u.astype(jnp.float32).reshape(bsz, T, SSM_GROUPS, SSM_GROUP)
    disc = s5_discretize(lam_re, lam_im, b_re, b_im, log_dt)
    if past is None:
        k, v = mla_keys(c_kv, k_rope, w_ukv, gk)
        o = mla_prompt_attention(q, k, v)
        y, h_re, h_im = s5_prompt(uf, disc, c_re, c_im)
    else:
        lat_c, kr_c, h0_re, h0_im = past
        k, v = mla_keys(jnp.concatenate([lat_c, c_kv], axis=1),
                        jnp.concatenate([kr_c, k_rope], axis=1), w_ukv, gk)
        o = mla_attention(q, k, v, chunk_causal_mask(pos, jnp.arange(past_len + T)))
        hr, hi = s5_block(uf, h0_re, h0_im, disc)
        y = s5_readout(hr, hi, c_re, c_im)
        h_re, h_im = hr[:, -1], hi[:, -1]
    a_out = o.reshape(bsz, T, A_WIDTH) * jax.nn.silu(gate_a)
    y = y.reshape(bsz, T, B_WIDTH) + d_skip.astype(jnp.float32) * uf.reshape(bsz, T, B_WIDTH)
    y = jax.nn.gelu(y).astype(x.dtype)
    g_a, g_b = jnp.split(y @ w_glu, 2, axis=-1)
    b_out = g_a * jax.nn.sigmoid(g_b) * jax.nn.silu(gate_b)
    out = jnp.concatenate([a_out, b_out], axis=-1) @ w_out
    return x + out, (c_kv, k_rope, h_re, h_im)


def sink_attention(q, k, v, bias, mask, sinks):
    s = jnp.einsum('...qhgd,...khd->...hgqk', q, k, preferred_element_type=jnp.float32) * (SWA_HEAD_DIM ** -0.5)
    s = jnp.where(mask, s + bias, -1e30)
    sink = sinks.astype(jnp.float32).reshape(SWA_KV_HEADS, SWA_GQ, 1, 1)
    m = jnp.maximum(jnp.max(s, axis=-1, keepdims=True), sink)
    e = jnp.exp(s - m)
    p = e / (jnp.sum(e, axis=-1, keepdims=True) + jnp.exp(sink - m))
    return jnp.einsum('...hgqk,...khd->...qhgd', p.astype(v.dtype), v)


def swa_prompt(q, k, v, table, sinks):
    bsz, L = q.shape[:2]
    nc = L // CHUNK
    qc = q.reshape(bsz, nc, CHUNK, SWA_KV_HEADS, SWA_GQ, SWA_HEAD_DIM)

    def windows(t):
        tp = jnp.pad(t, ((0, 0), (WINDOW, 0), (0, 0), (0, 0)))
        tp = tp.reshape(bsz, nc + WIN_CHUNKS, CHUNK, SWA_KV_HEADS, SWA_HEAD_DIM)
        return jnp.concatenate([tp[:, j:j + nc] for j in range(WIN_CHUNKS + 1)], axis=2)

    kw, vw = windows(k), windows(v)
    qi = jnp.arange(CHUNK)
    kj = jnp.arange(WINDOW + CHUNK)
    bias = rel_bias(table, kj[None, :] - WINDOW - qi[:, None])
    valid = (jnp.arange(nc)[:, None] * CHUNK - WINDOW + kj[None, :]) >= 0
    o = sink_attention(qc, kw, vw, bias, valid[None, :, None, None, None, :], sinks)
    return o.reshape(bsz, L, C_WIDTH)


def c_layer(x, past, past_len, params, table):
    norm_g, w_in, gq, gk, sinks, w_out = params
    bsz, T, _ = x.shape
    h = rms_norm(x, norm_g)
    q, k, v, gate = jnp.split(h @ w_in, C_SPLITS, axis=-1)
    q = rms_norm(q.reshape(bsz, T, SWA_KV_HEADS, SWA_GQ, SWA_HEAD_DIM), gq)
    k = rms_norm(k.reshape(bsz, T, SWA_KV_HEADS, SWA_HEAD_DIM), gk)
    v = v.reshape(bsz, T, SWA_KV_HEADS, SWA_HEAD_DIM)
    if past is None:
        o = swa_prompt(q, k, v, table, sinks)
        k_buf, v_buf = k[:, -WINDOW:], v[:, -WINDOW:]
    else:
        kc = jnp.concatenate([past[0], k], axis=1)
        vc = jnp.concatenate([past[1], v], axis=1)
        q_pos = past_len + jnp.arange(T)
        k_pos = past_len - WINDOW + jnp.arange(WINDOW + T)
        qch, kch = q_pos[:, None] // CHUNK, k_pos[None, :] // CHUNK
        mask = (kch <= qch) & (kch >= qch - WIN_CHUNKS)
        bias = rel_bias(table, k_pos[None, :] - q_pos[:, None])
        o = sink_attention(q, kc, vc, bias, mask, sinks).reshape(bsz, T, C_WIDTH)
        k_buf, v_buf = kc[:, -WINDOW:], vc[:, -WINDOW:]
    out = (o * jax.nn.silu(gate)) @ w_out
    return x + out, (k_buf, v_buf)


def setup_inputs(seed: int = 0) -> dict:
    key = jax.random.key(seed)
    keys = jax.random.split(key, 40)

    def nrm(i, shape, scale=1.0):
        return scale * jax.random.normal(keys[i], shape, jnp.float32)

    def gain(i, shape):
        return 1.0 + nrm(i, shape, 0.01)

    n = jnp.arange(SSM_STATE, dtype=jnp.float32)
    ssm_shape = (N_AB_LAYERS, SSM_GROUPS, SSM_STATE)
    return {
        "x_prompt": nrm(0, (BATCH, SEQ, D_MODEL)),
        "x_sample": nrm(1, (DEC_BATCH, DEC_SEQ, D_MODEL)),
        "cache_mla_latent": nrm(2, (N_AB_LAYERS, DEC_BATCH, PAST_LEN, KV_RANK)),
        "cache_mla_krope": nrm(3, (N_AB_LAYERS, DEC_BATCH, PAST_LEN, MLA_ROPE)),
        "state_ssm_re": nrm(4, (N_AB_LAYERS, DEC_BATCH, SSM_GROUPS, SSM_STATE), 0.3),
        "state_ssm_im": nrm(5, (N_AB_LAYERS, DEC_BATCH, SSM_GROUPS, SSM_STATE), 0.3),
        "cache_swa_k": nrm(6, (N_C_LAYERS, DEC_BATCH, WINDOW, SWA_KV_HEADS, SWA_HEAD_DIM)),
        "cache_swa_v": nrm(7, (N_C_LAYERS, DEC_BATCH, WINDOW, SWA_KV_HEADS, SWA_HEAD_DIM)),
        "rel_bias_table": nrm(8, (REL_BUCKETS, SWA_HEADS), 0.5),
        "ab_norm": gain(9, (N_AB_LAYERS, D_MODEL)),
        "ab_w_in": nrm(10, (N_AB_LAYERS, D_MODEL, AB_IN), D_MODEL ** -0.5),
        "ab_q_lora_norm": gain(11, (N_AB_LAYERS, Q_RANK)),
        "ab_kv_lora_norm": gain(12, (N_AB_LAYERS, KV_RANK)),
        "ab_w_uq": nrm(13, (N_AB_LAYERS, Q_RANK, MLA_HEADS * MLA_QK), Q_RANK ** -0.5),
        "ab_w_ukv": nrm(14, (N_AB_LAYERS, KV_RANK, MLA_HEADS * (MLA_NOPE + MLA_V)), KV_RANK ** -0.5),
        "ab_q_norm": gain(15, (N_AB_LAYERS, MLA_QK)),
        "ab_k_norm": gain(16, (N_AB_LAYERS, MLA_QK)),
        "ssm_lambda_re": -0.5 + nrm(17, ssm_shape, 0.01),
        "ssm_lambda_im": math.pi * n + nrm(18, ssm_shape, 0.01),
        "ssm_b_re": nrm(19, (N_AB_LAYERS, SSM_GROUPS, SSM_STATE, SSM_GROUP), (2 * SSM_GROUP) ** -0.5),
        "ssm_b_im": nrm(20, (N_AB_LAYERS, SSM_GROUPS, SSM_STATE, SSM_GROUP), (2 * SSM_GROUP) ** -0.5),
        "ssm_c_re": nrm(21, (N_AB_LAYERS, SSM_GROUPS, SSM_GROUP, SSM_STATE), SSM_STATE ** -0.5),
        "ssm_c_im": nrm(22, (N_AB_LAYERS, SSM_GROUPS, SSM_GROUP, SSM_STATE), SSM_STATE ** -0.5),
        "ssm_log_dt": jax.random.uniform(keys[23], (N_AB_LAYERS, SSM_GROUPS), jnp.float32,
                                         minval=math.log(DT_MIN), maxval=math.log(DT_MAX)),
        "ssm_d": nrm(24, (N_AB_LAYERS, B_WIDTH)),
        "ssm_w_glu": nrm(25, (N_AB_LAYERS, B_WIDTH, 2 * B_WIDTH), B_WIDTH ** -0.5),
        "ab_w_out": nrm(26, (N_AB_LAYERS, A_WIDTH + B_WIDTH, D_MODEL), D_MODEL ** -0.5),
        "c_norm": gain(27, (N_C_LAYERS, D_MODEL)),
        "c_w_in": nrm(28, (N_C_LAYERS, D_MODEL, C_IN), D_MODEL ** -0.5),
        "c_q_norm": gain(29, (N_C_LAYERS, SWA_HEAD_DIM)),
        "c_k_norm": gain(30, (N_C_LAYERS, SWA_HEAD_DIM)),
        "c_sinks": nrm(31, (N_C_LAYERS, SWA_HEADS), 0.5),
        "c_w_out": nrm(32, (N_C_LAYERS, C_WIDTH, D_MODEL), C_WIDTH ** -0.5),
    }


def reference(x_prompt, x_sample, cache_mla_latent, cache_mla_krope, state_ssm_re, state_ssm_im,
              cache_swa_k, cache_swa_v, rel_bias_table, ab_norm, ab_w_in, ab_q_lora_norm,
              ab_kv_lora_norm, ab_w_uq, ab_w_ukv, ab_q_norm, ab_k_norm, ssm_lambda_re, ssm_lambda_im,
              ssm_b_re, ssm_b_im, ssm_c_re, ssm_c_im, ssm_log_dt, ssm_d, ssm_w_glu, ab_w_out,
              c_norm, c_w_in, c_q_norm, c_k_norm, c_sinks, c_w_out):
    xp, xs = x_prompt, x_sample
    past_len = cache_mla_latent.shape[2]
    lat_p, kr_p, sre_p, sim_p, k_p, v_p = [], [], [], [], [], []
    lat_s, kr_s, sre_s, sim_s, k_s, v_s = [], [], [], [], [], []
    for layer in range(DEPTH):
        i = layer // 2
        if layer % 2 == 0:
            prm = (ab_norm[i], ab_w_in[i], ab_q_lora_norm[i], ab_kv_lora_norm[i], ab_w_uq[i], ab_w_ukv[i],
                   ab_q_norm[i], ab_k_norm[i], ssm_lambda_re[i], ssm_lambda_im[i], ssm_b_re[i], ssm_b_im[i],
                   ssm_c_re[i], ssm_c_im[i], ssm_log_dt[i], ssm_d[i], ssm_w_glu[i], ab_w_out[i])
            xp, (a, b, c, d) = ab_layer(xp, None, prm)
            lat_p.append(a); kr_p.append(b); sre_p.append(c); sim_p.append(d)
            past = (cache_mla_latent[i], cache_mla_krope[i], state_ssm_re[i], state_ssm_im[i])
            xs, (a, b, c, d) = ab_layer(xs, past, prm)
            lat_s.append(a); kr_s.append(b); sre_s.append(c); sim_s.append(d)
        else:
            prm = (c_norm[i], c_w_in[i], c_q_norm[i], c_k_norm[i], c_sinks[i], c_w_out[i])
            xp, (a, b) = c_layer(xp, None, 0, prm, rel_bias_table)
            k_p.append(a); v_p.append(b)
            xs, (a, b) = c_layer(xs, (cache_swa_k[i], cache_swa_v[i]), past_len, prm, rel_bias_table)
            k_s.append(a); v_s.append(b)
    return (xp, xs,
            jnp.stack(lat_p), jnp.stack(kr_p), jnp.stack(sre_p), jnp.stack(sim_p),
            jnp.stack(k_p), jnp.stack(v_p),
            jnp.stack(lat_s), jnp.stack(kr_s), jnp.stack(sre_s), jnp.stack(sim_s),
            jnp.stack(k_s), jnp.stack(v_s))
```

```python
import math
from contextlib import ExitStack

import numpy as np
import concourse.bass as bass
import concourse.mybir as mybir
from concourse.bass_utils import run_bass_kernel_spmd

F32 = mybir.dt.float32
BF16 = mybir.dt.bfloat16
AF = mybir.ActivationFunctionType
ALU = mybir.AluOpType

P = 128
D = 4096
SEQ = 2048
TS = 16
NT = SEQ + TS
DEPTH = 4
EPS = 1e-6
TT = [(0, 512), (512, 512), (1024, 512), (1536, 512), (2048, 16)]
GROUPS = [[0, 1], [2, 3, 4]]
AB_IN = 7488
C_IN = 9216
MAGIC = 12582912.0


class Buf:
    __slots__ = ("name", "t", "wr", "rd", "dsem", "merge")

    def __init__(self, name, t=None, merge=False):
        self.name = name
        self.t = t
        self.wr = {}
        self.rd = {}
        self.dsem = None
        self.merge = merge

    def __getitem__(self, idx):
        return self.t[idx]


class Eng:
    def __init__(self, kb, name, h, compute=True):
        self.kb = kb
        self.name = name
        self.h = h
        self.sem = kb.new_sem("e_" + name) if compute else None
        self.cnt = 0
        self.waited = {}
        self.pend_r = []
        self.pend_w = []

    def wait(self, sem, val):
        k = id(sem)
        if self.waited.get(k, 0) >= val:
            return
        self.h.wait_ge(sem, val)
        self.waited[k] = val


class KB:
    def __init__(self):
        self.nc = bass.Bass("TRN2", target_bir_lowering=False)
        nc = self.nc
        self.stack = ExitStack()
        self.sems = {}
        self.pe = Eng(self, "pe", nc.tensor)
        self.act = Eng(self, "act", nc.scalar)
        self.dve = Eng(self, "dve", nc.vector)
        self.pool = Eng(self, "pool", nc.gpsimd)
        self.sp = Eng(self, "sp", nc.sync, compute=False)
        self.engs = [self.pe, self.act, self.dve, self.pool, self.sp]
        self.dma_sems = [[self.new_sem(f"d{i}"), 0] for i in range(84)]
        self.dma_free = list(range(len(self.dma_sems)))
        self.phase_bufs = []
        self.all_events = {}
        self.psum = [Buf(f"ps{i}", self.stack.enter_context(nc.psum_tensor(f"ps{i}", [P, 512], F32)))
                     for i in range(8)]
        self.ps_i = 0
        self.uid = 0
        self.dram = {}

    def new_sem(self, name):
        s = self.stack.enter_context(self.nc.semaphore(name))
        self.sems[id(s)] = s
        return s

    def next_ps(self):
        b = self.psum[self.ps_i % 8]
        self.ps_i += 1
        return b

    def dram_in(self, name, shape, dtype=F32):
        t = self.nc.dram_tensor(name, list(shape), dtype, kind="ExternalInput")
        self.dram[name] = Buf(name, t, merge=True)
        return self.dram[name]

    def dram_out(self, name, shape, dtype=F32):
        t = self.nc.dram_tensor(name, list(shape), dtype, kind="ExternalOutput")
        self.dram[name] = Buf(name, t, merge=True)
        return self.dram[name]

    def dram_tmp(self, name, shape, dtype):
        t = self.nc.dram_tensor(name, list(shape), dtype, kind="Internal")
        self.dram[name] = Buf(name, t, merge=True)
        return self.dram[name]

    def phase_begin(self):
        self.pstack = ExitStack()
        self.phase_bufs = []

    def sb(self, name, shape, dtype):
        self.uid += 1
        t = self.pstack.enter_context(self.nc.sbuf_tensor(f"{name}_{self.uid}", list(shape), dtype))
        b = Buf(name, t)
        self.phase_bufs.append(b)
        return b

    def phase_end(self):
        self.barrier()
        for b in self.phase_bufs:
            if b.dsem is not None:
                self.dma_free.append(b.dsem)
                b.dsem = None
        self.pstack.close()
        self.phase_bufs = []

    def barrier(self):
        for e in self.engs:
            assert not e.pend_r and not e.pend_w, f"pending accesses on {e.name} at barrier"
        for e in self.engs:
            for k, (sem, val) in self.all_events.items():
                if e.sem is not None and sem is e.sem and e is self.pe:
                    continue
                e.wait(sem, val)

    def _note(self, sem, val):
        self.all_events[id(sem)] = (sem, val)

    def _waits(self, eng, reads, writes):
        for e2 in self.engs:
            if e2 is eng:
                continue
            if e2.pend_r or e2.pend_w:
                for b in list(reads) + list(writes):
                    if any(b is x for x in e2.pend_w):
                        raise RuntimeError(f"buffer {b.name} has uncommitted write on {e2.name}")
                for b in writes:
                    if any(b is x for x in e2.pend_r):
                        raise RuntimeError(f"buffer {b.name} has uncommitted read on {e2.name}")
        for b in reads:
            for k, (sem, val) in b.wr.items():
                if eng is self.pe and sem is eng.sem:
                    continue
                eng.wait(sem, val)
        for b in writes:
            if not b.merge:
                for k, (sem, val) in b.wr.items():
                    if eng is self.pe and sem is eng.sem:
                        continue
                    eng.wait(sem, val)
            for k, (sem, val) in b.rd.items():
                if eng is self.pe and sem is eng.sem:
                    continue
                eng.wait(sem, val)

    def _commit(self, sem, val, reads, writes):
        for b in reads:
            b.rd[id(sem)] = (sem, val)
        for b in writes:
            if b.merge:
                b.wr[id(sem)] = (sem, val)
            else:
                b.wr = {id(sem): (sem, val)}
                b.rd = {}
        self._note(sem, val)

    def op(self, eng, fn, reads=(), writes=(), inc=True):
        self._waits(eng, reads, writes)
        ins = fn()
        if not inc:
            eng.pend_r.extend(reads)
            eng.pend_w.extend(writes)
            return ins
        eng.cnt += 1
        ins.then_inc(eng.sem, 1)
        self._commit(eng.sem, eng.cnt, list(reads) + eng.pend_r, list(writes) + eng.pend_w)
        eng.pend_r = []
        eng.pend_w = []
        return ins

    def dma(self, q, out_ap, in_ap, reads, writes, sbuf):
        if sbuf.dsem is None:
            sbuf.dsem = self.dma_free.pop()
        ent = self.dma_sems[sbuf.dsem]
        saved = []
        for b in writes:
            if not b.merge and id(ent[0]) in b.wr:
                saved.append((b, b.wr.pop(id(ent[0]))))
        self._waits(q, reads, writes)
        for b, v in saved:
            b.wr[id(ent[0])] = v
        ent[1] += 16
        q.h.dma_start(out=out_ap, in_=in_ap).then_inc(ent[0], 16)
        sem, val = ent
        for b in reads:
            b.rd[id(sem)] = (sem, val)
        for b in writes:
            if not b.merge:
                stale = [k for k in b.wr if k != id(sem)]
                for k in stale:
                    del b.wr[k]
                b.rd = {}
            b.wr[id(sem)] = (sem, val)
        self._note(sem, val)

    def _is_dma_sem(self, k):
        return any(id(e[0]) == k for e in self.dma_sems)


class Rot:
    def __init__(self, bufs):
        self.bufs = bufs
        self.i = 0

    def next(self):
        b = self.bufs[self.i % len(self.bufs)]
        self.i += 1
        return b


def load_consts(kb, C):
    nc = kb.nc
    C.ones = kb.sb("ones_bf", [P, P], BF16)
    kb.op(kb.dve, lambda: nc.vector.memset(C.ones[:], 1.0), writes=[C.ones])
    C.eps = kb.sb("eps_col", [P, 1], F32)
    kb.op(kb.dve, lambda: nc.vector.memset(C.eps[:], EPS), writes=[C.eps])
    C.halfpi = kb.sb("halfpi", [P, 1], F32)
    kb.op(kb.dve, lambda: nc.vector.memset(C.halfpi[:], math.pi / 2), writes=[C.halfpi])


class NS:
    pass


def rstd_from_ss(kb, C, ss_ps, n, inv_count, out_buf, out_sl, tmp):
    nc = kb.nc
    kb.op(kb.act, lambda: nc.scalar.activation(tmp[:, 0:n], ss_ps[:, 0:n], AF.Sqrt, bias=C.eps[:, 0:1],
                                               scale=inv_count), reads=[ss_ps, C.eps], writes=[tmp])
    kb.op(kb.dve, lambda: nc.vector.reciprocal(out_buf[:, out_sl], tmp[:, 0:n]), reads=[tmp], writes=[out_buf])


def rmsnorm_resident(kb, C, X, KC, gcol, A, pre):
    nc = kb.nc
    xa = X.t.ap().rearrange("(c p) t -> p c t", p=P)
    xin = Rot([kb.sb(f"{pre}_xin{i}", [P, 512], F32) for i in range(4)])
    sq = Rot([kb.sb(f"{pre}_sq{i}", [P, 512], BF16) for i in range(3)])
    rstd = kb.sb(f"{pre}_rstd", [P, NT], F32)
    tmp = kb.sb(f"{pre}_sd", [P, 512], F32)
    for (t0, n) in TT:
        ss = kb.next_ps()
        for c in range(KC):
            xb = xin.next()
            kb.dma(kb.sp, xb[:, 0:n], xa[:, c, t0:t0 + n], reads=[X], writes=[xb], sbuf=xb)
            sb_ = sq.next()
            kb.op(kb.act, lambda: nc.scalar.activation(sb_[:, 0:n], xb[:, 0:n], AF.Square), reads=[xb], writes=[sb_])
            kb.op(kb.pe, lambda: nc.tensor.matmul(ss[:, 0:n], C.ones[:], sb_[:, 0:n], start=(c == 0),
                                                  stop=(c == KC - 1)),
                  reads=[C.ones, sb_], writes=[ss], inc=True)
        rstd_from_ss(kb, C, ss, n, 1.0 / (KC * P), rstd, slice(t0, t0 + n), tmp)
        for c in range(KC):
            xb = xin.next()
            kb.dma(kb.sp, xb[:, 0:n], xa[:, c, t0:t0 + n], reads=[X], writes=[xb], sbuf=xb)
            kb.op(kb.dve, lambda: nc.vector.scalar_tensor_tensor(A[:, c, t0:t0 + n], xb[:, 0:n], gcol[:, c:c + 1],
                                                                  rstd[:, t0:t0 + n], ALU.mult, ALU.mult),
                  reads=[xb, gcol, rstd], writes=[A])


def linear(kb, A, KC, W_ap, jobs, epilogue, pre, nslab=3):
    nc = kb.nc
    wv = W_ap.rearrange("(k p) m -> p k m", p=P)
    slabs = Rot([kb.sb(f"{pre}_w{i}", [P, KC, P], BF16) for i in range(nslab)])
    for ji, (m0, msz, tag) in enumerate(jobs):
        sl = slabs.next()
        kb.dma(kb.pool, sl[:, :, 0:msz], wv[:, :, m0:m0 + msz], reads=[], writes=[sl], sbuf=sl)
        for grp in GROUPS:
            pss = {ti: kb.next_ps() for ti in grp}
            for k in range(KC):
                for ti in grp:
                    t0, n = TT[ti]
                    last = (k == KC - 1)
                    kb.op(kb.pe, lambda: nc.tensor.matmul(pss[ti][0:msz, 0:n], sl[:, k, 0:msz], A[:, k, t0:t0 + n],
                                                          start=(k == 0), stop=last),
                          reads=[sl, A], writes=[pss[ti]], inc=last)
            for ti in grp:
                epilogue(tag, ji, ti, pss[ti], msz)


def ab_jobs():
    jobs = []
    for i in range(6):
        jobs.append((i * 128, 128, ("cq", i)))
    for i in range(4):
        jobs.append((768 + i * 128, 128, ("ckv", i)))
    jobs.append((1280, 64, ("kr", 0)))
    for i in range(16):
        jobs.append((1344 + i * 128, 128, ("ga", i)))
    for i in range(16):
        jobs.append((3392 + i * 128, 128, ("u", i)))
    for i in range(16):
        jobs.append((5440 + i * 128, 128, ("gb", i)))
    return jobs


def ab_phase1(kb, G, li, X):
    nc = kb.nc
    kb.phase_begin()
    C = NS()
    load_consts(kb, C)
    gcol = kb.sb("gcol", [P, 32], F32)
    kb.dma(kb.sp, gcol[:], G.ab_norm.t.ap()[li], reads=[G.ab_norm], writes=[gcol], sbuf=gcol)
    A = kb.sb("A", [P, 32, NT], BF16)
    rmsnorm_resident(kb, C, X, 32, gcol, A, "n1")
    st32 = Rot([kb.sb(f"st32_{i}", [P, 512], F32) for i in range(3)])
    st16 = Rot([kb.sb(f"st16_{i}", [P, 512], BF16) for i in range(3)])
    flip = [0]

    def epi(tag, ji, ti, ps, msz):
        t0, n = TT[ti]
        kind, i = tag
        if kind in ("cq", "ckv", "kr"):
            dst = {"cq": G.CQ, "ckv": G.CKV, "kr": G.KRAW}[kind]
            s = st32.next()
            flip[0] ^= 1
            if flip[0]:
                kb.op(kb.dve, lambda: nc.vector.tensor_copy(s[0:msz, 0:n], ps[0:msz, 0:n]), reads=[ps], writes=[s])
            else:
                kb.op(kb.act, lambda: nc.scalar.copy(s[0:msz, 0:n], ps[0:msz, 0:n]), reads=[ps], writes=[s])
            kb.dma(kb.sp, dst.t.ap()[i * 128:i * 128 + msz, t0:t0 + n], s[0:msz, 0:n], reads=[s], writes=[dst], sbuf=s)
        else:
            dst = {"ga": G.GA, "u": G.U, "gb": G.GB}[kind]
            s = st16.next()
            if kind == "u":
                kb.op(kb.dve, lambda: nc.vector.tensor_copy(s[:, 0:n], ps[:, 0:n]), reads=[ps], writes=[s])
            else:
                kb.op(kb.act, lambda: nc.scalar.activation(s[:, 0:n], ps[:, 0:n], AF.Silu), reads=[ps], writes=[s])
            kb.dma(kb.sp, dst.t.ap()[i * 128:(i + 1) * 128, t0:t0 + n], s[:, 0:n], reads=[s], writes=[dst], sbuf=s)

    linear(kb, A, 32, G.ab_w_in.t.ap()[li], ab_jobs(), epi, "l1")
    kb.phase_end()

    kb.phase_begin()
    C = NS()
    load_consts(kb, C)
    wuq = kb.sb("wuq", [P, 6, 3072], BF16)
    wv = G.ab_w_uq.t.ap()[li].rearrange("(k p) m -> p k m", p=P)
    for k in range(6):
        kb.dma(kb.pool, wuq[:, k, :], wv[:, k, :], reads=[], writes=[wuq], sbuf=wuq)
    glq = kb.sb("glq", [P, 6], F32)
    kb.dma(kb.sp, glq[:], G.ab_q_lora_norm.t.ap()[li], reads=[], writes=[glq], sbuf=glq)
    glkv = kb.sb("glkv", [P, 4], F32)
    kb.dma(kb.sp, glkv[:], G.ab_kv_lora_norm.t.ap()[li], reads=[], writes=[glkv], sbuf=glkv)
    gq = kb.sb("gq", [P, 2], F32)
    kb.dma(kb.sp, gq[:], G.ab_q_norm.t.ap()[li], reads=[], writes=[gq], sbuf=gq)
    cs = kb.sb("ropecs", [64, NT], F32)
    sn = kb.sb("ropesn", [64, NT], F32)
    kb.dma(kb.sp, cs[:], G.rope_cos.t.ap(), reads=[], writes=[cs], sbuf=cs)
    kb.dma(kb.sp, sn[:], G.rope_sin.t.ap(), reads=[], writes=[sn], sbuf=sn)

    cq = Rot([kb.sb(f"cq{i}", [P, 6, 512], F32) for i in range(2)])
    cqsq = kb.sb("cqsq", [P, 6, 512], BF16)
    cqn = Rot([kb.sb(f"cqn{i}", [P, 6, 512], BF16) for i in range(2)])
    ckv = Rot([kb.sb(f"ckv{i}", [P, 4, 512], F32) for i in range(2)])
    ckvsq = kb.sb("ckvsq", [P, 4, 512], BF16)
    latn = Rot([kb.sb(f"latn{i}", [P, 4, 512], F32) for i in range(2)])
    kraw = Rot([kb.sb(f"kraw{i}", [64, 512], F32) for i in range(2)])
    krt = kb.sb("krt", [64, 512], F32)
    kro = Rot([kb.sb(f"kro{i}", [64, 512], F32) for i in range(2)])
    rsq = kb.sb("rsq", [P, 512], F32)
    rskv = kb.sb("rskv", [P, 512], F32)
    tmp = kb.sb("tmp", [P, 512], F32)
    sqa = Rot([kb.sb(f"sqa{i}", [P, 512], BF16) for i in range(2)])
    sqb = Rot([kb.sb(f"sqb{i}", [64, 512], BF16) for i in range(2)])
    rsh = Rot([kb.sb(f"rsh{i}", [P, 512], F32) for i in range(2)])
    rt = Rot([kb.sb(f"rt{i}", [64, 512], F32) for i in range(2)])
    rr = Rot([kb.sb(f"rr{i}", [64, 512], F32) for i in range(2)])
    qn = Rot([kb.sb(f"qn{i}", [P, 512], BF16) for i in range(3)])
    qr = Rot([kb.sb(f"qr{i}", [64, 512], BF16) for i in range(3)])

    def rope(src, sl_src, dst_t, dst, n, t0, src_bufs):
        kb.op(kb.dve, lambda: nc.vector.tensor_tensor(dst_t[0:32, 0:n], src[32:64, sl_src], sn[32:64, t0:t0 + n], ALU.mult),
              reads=src_bufs + [sn], writes=[dst_t])
        kb.op(kb.dve, lambda: nc.vector.tensor_tensor(dst_t[32:64, 0:n], src[0:32, sl_src], sn[0:32, t0:t0 + n], ALU.mult),
              reads=src_bufs + [sn], writes=[dst_t])
        kb.op(kb.dve, lambda: nc.vector.tensor_tensor(dst[0:64, 0:n], src[0:64, sl_src], cs[0:64, t0:t0 + n], ALU.mult),
              reads=src_bufs + [cs], writes=[dst])
        kb.op(kb.dve, lambda: nc.vector.tensor_tensor(dst[0:64, 0:n], dst[0:64, 0:n], dst_t[0:64, 0:n], ALU.add),
              reads=[dst, dst_t], writes=[dst])

    cqa = G.CQ.t.ap().rearrange("(c p) t -> p c t", p=P)
    ckva = G.CKV.t.ap().rearrange("(c p) t -> p c t", p=P)
    lata = G.lat_out[li].t.ap().rearrange("(c p) t -> p c t", p=P)
    for (t0, n) in TT:
        kv = ckv.next()
        kb.dma(kb.sp, kv[:, :, 0:n], ckva[:, :, t0:t0 + n], reads=[G.CKV], writes=[kv], sbuf=kv)
        kb.op(kb.act, lambda: nc.scalar.activation(ckvsq[:, :, 0:n], kv[:, :, 0:n], AF.Square), reads=[kv], writes=[ckvsq])
        ss = kb.next_ps()
        for c in range(4):
            kb.op(kb.pe, lambda: nc.tensor.matmul(ss[:, 0:n], C.ones[:], ckvsq[:, c, 0:n], start=(c == 0), stop=(c == 3)),
                  reads=[C.ones, ckvsq], writes=[ss], inc=(c == 3))
        rstd_from_ss(kb, C, ss, n, 1.0 / 512, rskv, slice(0, n), tmp)
        ln = latn.next()
        for c in range(4):
            kb.op(kb.dve, lambda: nc.vector.scalar_tensor_tensor(ln[:, c, 0:n], kv[:, c, 0:n], glkv[:, c:c + 1],
                                                                  rskv[:, 0:n], ALU.mult, ALU.mult),
                  reads=[kv, glkv, rskv], writes=[ln])
        kb.dma(kb.sp, lata[:, :, t0:t0 + n], ln[:, :, 0:n], reads=[ln], writes=[G.lat_out[li]], sbuf=ln)
        kr = kraw.next()
        kb.dma(kb.sp, kr[:, 0:n], G.KRAW.t.ap()[:, t0:t0 + n], reads=[G.KRAW], writes=[kr], sbuf=kr)
        ko = kro.next()
        rope(kr, slice(0, n), krt, ko, n, t0, [kr])
        kb.dma(kb.sp, G.kr_out[li].t.ap()[:, t0:t0 + n], ko[:, 0:n], reads=[ko], writes=[G.kr_out[li]], sbuf=ko)
        q_ = cq.next()
        kb.dma(kb.sp, q_[:, :, 0:n], cqa[:, :, t0:t0 + n], reads=[G.CQ], writes=[q_], sbuf=q_)
        kb.op(kb.act, lambda: nc.scalar.activation(cqsq[:, :, 0:n], q_[:, :, 0:n], AF.Square), reads=[q_], writes=[cqsq])
        ss = kb.next_ps()
        for c in range(6):
            kb.op(kb.pe, lambda: nc.tensor.matmul(ss[:, 0:n], C.ones[:], cqsq[:, c, 0:n], start=(c == 0), stop=(c == 5)),
                  reads=[C.ones, cqsq], writes=[ss], inc=(c == 5))
        rstd_from_ss(kb, C, ss, n, 1.0 / 768, rsq, slice(0, n), tmp)
        qn_ = cqn.next()
        for c in range(6):
            kb.op(kb.dve, lambda: nc.vector.scalar_tensor_tensor(qn_[:, c, 0:n], q_[:, c, 0:n], glq[:, c:c + 1],
                                                                  rsq[:, 0:n], ALU.mult, ALU.mult),
                  reads=[q_, glq, rsq], writes=[qn_])
        for h in range(16):
            psA = kb.next_ps()
            psB = kb.next_ps()
            for k in range(6):
                kb.op(kb.pe, lambda: nc.tensor.matmul(psA[:, 0:n], wuq[:, k, h * 192:h * 192 + 128], qn_[:, k, 0:n],
                                                      start=(k == 0), stop=(k == 5)),
                      reads=[wuq, qn_], writes=[psA], inc=(k == 5))
            for k in range(6):
                kb.op(kb.pe, lambda: nc.tensor.matmul(psB[0:64, 0:n], wuq[:, k, h * 192 + 128:h * 192 + 192], qn_[:, k, 0:n],
                                                      start=(k == 0), stop=(k == 5)),
                      reads=[wuq, qn_], writes=[psB], inc=(k == 5))
            a2 = sqa.next()
            b2 = sqb.next()
            kb.op(kb.act, lambda: nc.scalar.activation(a2[:, 0:n], psA[:, 0:n], AF.Square), reads=[psA], writes=[a2])
            kb.op(kb.act, lambda: nc.scalar.activation(b2[0:64, 0:n], psB[0:64, 0:n], AF.Square), reads=[psB], writes=[b2])
            ss = kb.next_ps()
            kb.op(kb.pe, lambda: nc.tensor.matmul(ss[:, 0:n], C.ones[:], a2[:, 0:n], start=True, stop=False),
                  reads=[C.ones, a2], writes=[ss], inc=False)
            kb.op(kb.pe, lambda: nc.tensor.matmul(ss[:, 0:n], C.ones[0:64, :], b2[0:64, 0:n], start=False, stop=True),
                  reads=[C.ones, b2], writes=[ss], inc=True)
            rh = rsh.next()
            rstd_from_ss(kb, C, ss, n, 1.0 / 192, rh, slice(0, n), tmp)
            t_ = rt.next()
            r_ = rr.next()
            rope(psB, slice(0, n), t_, r_, n, t0, [psB])
            o1 = qn.next()
            kb.op(kb.dve, lambda: nc.vector.scalar_tensor_tensor(o1[:, 0:n], psA[:, 0:n], gq[:, 0:1], rh[:, 0:n],
                                                                  ALU.mult, ALU.mult),
                  reads=[psA, gq, rh], writes=[o1])
            kb.dma(kb.sp, G.QT.t.ap()[h * 192:h * 192 + 128, t0:t0 + n], o1[:, 0:n], reads=[o1], writes=[G.QT], sbuf=o1)
            o2 = qr.next()
            kb.op(kb.dve, lambda: nc.vector.scalar_tensor_tensor(o2[0:64, 0:n], r_[0:64, 0:n], gq[0:64, 1:2], rh[0:64, 0:n],
                                                                  ALU.mult, ALU.mult),
                  reads=[r_, gq, rh], writes=[o2])
            kb.dma(kb.sp, G.QT.t.ap()[h * 192 + 128:h * 192 + 192, t0:t0 + n], o2[0:64, 0:n], reads=[o2], writes=[G.QT],
                   sbuf=o2)
    kb.phase_end()


def declare(kb):
    G = NS()
    G.x0 = kb.dram_in("x0", [D, NT])
    G.ab_norm = kb.dram_in("ab_norm", [2, P, 32])
    G.ab_w_in = kb.dram_in("ab_w_in", [2, D, AB_IN])
    G.ab_q_lora_norm = kb.dram_in("ab_q_lora_norm", [2, P, 6])
    G.ab_kv_lora_norm = kb.dram_in("ab_kv_lora_norm", [2, P, 4])
    G.ab_w_uq = kb.dram_in("ab_w_uq", [2, 768, 3072])
    G.ab_q_norm = kb.dram_in("ab_q_norm", [2, P, 2])
    G.rope_cos = kb.dram_in("rope_cos", [64, NT])
    G.rope_sin = kb.dram_in("rope_sin", [64, NT])
    G.lat_out = [kb.dram_out(f"lat_out{i}", [512, NT]) for i in range(2)]
    G.kr_out = [kb.dram_out(f"kr_out{i}", [64, NT]) for i in range(2)]
    G.CQ = kb.dram_tmp("CQ", [768, NT], F32)
    G.CKV = kb.dram_tmp("CKV", [512, NT], F32)
    G.KRAW = kb.dram_tmp("KRAW", [64, NT], F32)
    G.GA = kb.dram_tmp("GA", [2048, NT], BF16)
    G.U = kb.dram_tmp("U", [2048, NT], BF16)
    G.GB = kb.dram_tmp("GB", [2048, NT], BF16)
    G.QT = kb.dram_tmp("QT", [3072, NT], BF16)
    G.s5B = kb.dram_in("s5B", [2, 5, P, 4096])
    G.s5S = kb.dram_in("s5S", [2, 3, P, 64])
    G.s5C = kb.dram_in("s5C", [2, 2, P, 4096])
    G.ssm_d = kb.dram_in("ssm_d", [2, P, 16])
    G.iota512 = kb.dram_in("iota512", [P, 512])
    G.h0 = kb.dram_in("h0", [2, 2, P, 64])
    G.SPAR = kb.dram_tmp("SPAR", [6, P, 64], F32)
    G.BWRE = kb.dram_tmp("BWRE", [P, 4096], BF16)
    G.BWIM = kb.dram_tmp("BWIM", [P, 4096], BF16)
    G.CW = kb.dram_tmp("CW", [3, P, 4096], BF16)
    G.YG = kb.dram_tmp("YG", [2048, NT], BF16)
    G.hp_out = [kb.dram_out(f"hp_out{i}", [2, P, 64]) for i in range(2)]
    G.hs_out = [kb.dram_out(f"hs_out{i}", [2, P, 64]) for i in range(2)]
    G.ab_k_norm = kb.dram_in("ab_k_norm", [2, P, 2])
    G.latc = kb.dram_in("latc", [2, 512, SEQ])
    G.krc = kb.dram_in("krc", [2, 64, SEQ])
    G.ab_w_ukv = kb.dram_in("ab_w_ukv", [2, 512, 4096])
    G.ssm_w_glu = kb.dram_in("ssm_w_glu", [2, 2048, 4096])
    G.ab_w_out = kb.dram_in("ab_w_out", [2, D, D])
    G.AO = kb.dram_tmp("AO", [D, NT], BF16)
    G.XA = kb.dram_tmp("XA", [D, NT], F32)
    G.XB = kb.dram_tmp("XB", [D, NT], F32)
    G.y_out = kb.dram_out("y_out", [D, NT])
    G.c_norm = kb.dram_in("c_norm", [2, P, 32])
    G.c_qk_norm = kb.dram_in("c_qk_norm", [2, P, 2])
    G.c_w_in = kb.dram_in("c_w_in", [2, D, C_IN])
    G.c_w_out = kb.dram_in("c_w_out", [2, D, D])
    G.kcache = kb.dram_in("kcache", [2, 512, 128])
    G.vcacheT = kb.dram_in("vcacheT", [2, 512, 128])
    G.vcache = kb.dram_in("vcache", [2, 128, 512])
    G.QS = kb.dram_tmp("QS", [D, NT], BF16)
    G.KS = kb.dram_tmp("KS", [512, NT], F32)
    G.VS = kb.dram_tmp("VS", [512, NT], F32)
    G.GC = kb.dram_tmp("GC", [D, NT], BF16)
    G.k_out = [kb.dram_out(f"k_out{i}", [512, 256]) for i in range(2)]
    G.v_out = [kb.dram_out(f"v_out{i}", [512, 256]) for i in range(2)]
    G.rel_table = kb.dram_in("rel_table", [32, 64])
    G.onehot = kb.dram_in("onehot", [32, LB])
    G.ident = kb.dram_in("ident", [P, P])
    G.sinks = kb.dram_in("sinks", [2, P, 64])
    G.GD = kb.dram_tmp("GD", [64, LB], F32)
    G.GT = kb.dram_tmp("GT", [64, NCOPY, LB], F32)
    G.EBP = kb.dram_tmp("EBP", [64, P, 256], F32)
    G.EBS = kb.dram_tmp("EBS", [64, P, 32], F32)
    return G


def build(parts=None, depth=DEPTH):
    kb = KB()
    G = declare(kb)
    if parts is not None:
        if "p1" in parts:
            ab_phase1(kb, G, 0, G.x0)
        if "s5s" in parts:
            s5_setup(kb, G, 0)
        if "s5m" in parts:
            s5_main(kb, G, 0)
        if "mla" in parts:
            mla_phase(kb, G, 0)
        if "glu" in parts:
            glu_phase(kb, G, 0)
        if "out" in parts:
            out_proj_phase(kb, G, G.ab_w_out.t.ap()[0], G.x0, G.XA)
        if "c1" in parts:
            c_phase1(kb, G, 0, G.XA)
        if "bias" in parts:
            bias_setup(kb, G)
        if "c2" in parts:
            c_phase2(kb, G, 0)
        if "cout" in parts:
            out_proj_phase(kb, G, G.c_w_out.t.ap()[0], G.XA, G.XB)
        kb.barrier()
        return kb
    bias_setup(kb, G)
    chain = [G.x0, G.XA, G.XB, G.XA, G.y_out]
    for layer in range(depth):
        li = layer // 2
        Xin, Xout = chain[layer], chain[layer + 1]
        if layer == depth - 1:
            Xout = G.y_out
        if layer % 2 == 0:
            ab_phase1(kb, G, li, Xin)
            mla_phase(kb, G, li)
            s5_setup(kb, G, li)
            s5_main(kb, G, li)
            glu_phase(kb, G, li)
            out_proj_phase(kb, G, G.ab_w_out.t.ap()[li], Xin, Xout)
        else:
            c_phase1(kb, G, li, Xin)
            c_phase2(kb, G, li)
            out_proj_phase(kb, G, G.c_w_out.t.ap()[li], Xin, Xout)
    kb.barrier()
    return kb


def colmajor(v, nchunk):
    return np.ascontiguousarray(v.reshape(nchunk, P).T)


def rope_tables():
    half = 32
    inv = 10000.0 ** (-np.arange(half, dtype=np.float32) / half)
    ang = np.arange(NT, dtype=np.float32)[None, :] * inv[:, None].astype(np.float32)
    ang = ang.astype(np.float32)
    cos = np.cos(ang).astype(np.float32)
    sin = np.sin(ang).astype(np.float32)
    cs = np.concatenate([cos, cos], 0)
    sn = np.concatenate([sin, -sin], 0)
    return np.ascontiguousarray(cs), np.ascontiguousarray(sn)


def prep_core(inp, c, shared):
    pb = c % 4
    m = dict(shared)
    m["x0"] = np.ascontiguousarray(np.concatenate([inp["x_prompt"][pb].T, inp["x_sample"][c].T], axis=1))
    m["latc"] = np.ascontiguousarray(inp["cache_mla_latent"][:, c].transpose(0, 2, 1))
    m["krc"] = np.ascontiguousarray(inp["cache_mla_krope"][:, c].transpose(0, 2, 1))
    kc = inp["cache_swa_k"][:, c].reshape(2, 128, 512)
    vc = inp["cache_swa_v"][:, c].reshape(2, 128, 512)
    m["kcache"] = np.ascontiguousarray(kc.transpose(0, 2, 1))
    m["vcacheT"] = np.ascontiguousarray(vc.transpose(0, 2, 1))
    m["vcache"] = np.ascontiguousarray(vc)
    m["h0"] = np.stack([np.stack([s_layout(inp["state_ssm_re"][i, c]), s_layout(inp["state_ssm_im"][i, c])]) for i in range(2)])
    return m


def prep_shared(inp):
    m = {}
    m["ab_norm"] = np.stack([colmajor(inp["ab_norm"][i], 32) for i in range(2)])
    m["ab_w_in"] = inp["ab_w_in"]
    m["ab_q_lora_norm"] = np.stack([colmajor(inp["ab_q_lora_norm"][i], 6) for i in range(2)])
    m["ab_kv_lora_norm"] = np.stack([colmajor(inp["ab_kv_lora_norm"][i], 4) for i in range(2)])
    m["ab_w_uq"] = inp["ab_w_uq"]
    gq = np.zeros((2, P, 2), np.float32)
    gq[:, :, 0] = inp["ab_q_norm"][:, 0:128]
    gq[:, 0:64, 1] = inp["ab_q_norm"][:, 128:192]
    m["ab_q_norm"] = gq
    m["rope_cos"], m["rope_sin"] = rope_tables()
    m.update(s5_layouts(inp))
    gk = np.zeros((2, P, 2), np.float32)
    gk[:, :, 0] = inp["ab_k_norm"][:, 0:128]
    gk[:, 0:64, 1] = inp["ab_k_norm"][:, 128:192]
    m["ab_k_norm"] = gk
    for nm in ("ab_w_ukv", "ssm_w_glu", "ab_w_out", "c_w_in", "c_w_out"):
        m[nm] = inp[nm]
    m["c_norm"] = np.stack([colmajor(inp["c_norm"][i], 32) for i in range(2)])
    qk = np.zeros((2, P, 2), np.float32)
    qk[:, :, 0] = np.tile(inp["c_q_norm"], (1, 2))
    qk[:, :, 1] = np.tile(inp["c_k_norm"], (1, 2))
    m["c_qk_norm"] = qk
    m["rel_table"] = inp["rel_bias_table"]
    rel = 127 - np.arange(LB)
    bk = rel_bucket_np(rel)
    m["onehot"] = np.ascontiguousarray((bk[None, :] == np.arange(32)[:, None]).astype(np.float32))
    m["ident"] = np.eye(P, dtype=np.float32)
    m["sinks"] = np.ascontiguousarray(np.broadcast_to(inp["c_sinks"][:, None, :], (2, P, 64)))
    m["ssm_d"] = np.stack([colmajor(inp["ssm_d"][i], 16) for i in range(2)])
    m["iota512"] = np.ascontiguousarray(np.broadcast_to(np.arange(512, dtype=np.float32), (P, 512)))
    return m


def s_layout(a):
    return np.ascontiguousarray(a.reshape(64, 2, 64).transpose(1, 2, 0).reshape(P, 64))


def s_unlayout(a):
    return np.ascontiguousarray(a.reshape(2, 64, 64).transpose(2, 0, 1).reshape(P, 64))


def s5_layouts(inp):
    B = np.zeros((2, 5, P, 32, P), np.float32)
    S = np.zeros((2, 3, P, 64), np.float32)
    Cc = np.zeros((2, 2, P, 64, 64), np.float32)
    for i in range(2):
        lre, lim, ldt = inp["ssm_lambda_re"][i], inp["ssm_lambda_im"][i], inp["ssm_log_dt"][i]
        bre, bim = inp["ssm_b_re"][i], inp["ssm_b_im"][i]
        cre, cim = inp["ssm_c_re"][i], inp["ssm_c_im"][i]
        ldtb = np.broadcast_to(ldt[:, None], (128, 64))
        S[i, 0], S[i, 1], S[i, 2] = s_layout(lre), s_layout(lim), s_layout(ldtb)
        for hf in range(2):
            rows = slice(64 * hf, 64 * hf + 64)
            for j in range(16):
                for s_ in range(2):
                    js = 2 * j + s_
                    gb = 8 * j + 2 * (2 * hf + s_)
                    for g1 in range(2):
                        cols = slice(64 * g1, 64 * g1 + 64)
                        B[i, 0, rows, js, cols] = lre[gb + g1][None, :]
                        B[i, 1, rows, js, cols] = lim[gb + g1][None, :]
                        B[i, 2, rows, js, cols] = ldt[gb + g1]
                        r0 = 64 * hf + 32 * s_ + 16 * g1
                        B[i, 3, r0:r0 + 16, js, cols] = bre[gb + g1].T
                        B[i, 4, r0:r0 + 16, js, cols] = bim[gb + g1].T
        for pi in range(64):
            s_ = pi % 2
            for g1 in range(2):
                g = 2 * pi + g1
                c0 = 32 * s_ + 16 * g1
                Cc[i, 0, 64 * g1:64 * g1 + 64, pi, c0:c0 + 16] = cre[g].T
                Cc[i, 1, 64 * g1:64 * g1 + 64, pi, c0:c0 + 16] = cim[g].T
    return {"s5B": B.reshape(2, 5, P, 4096), "s5S": S, "s5C": Cc.reshape(2, 2, P, 4096)}


def cis_tables(kb, C, x, n, COS, SIN, W, rows=P):
    nc = kb.nc
    t1, k_, fr = W
    r = slice(0, rows)
    kb.op(kb.pool, lambda: nc.gpsimd.tensor_scalar(t1[r, 0:n], x[r, 0:n], MAGIC, None, ALU.add), reads=[x], writes=[t1])
    kb.op(kb.pool, lambda: nc.gpsimd.tensor_scalar(k_[r, 0:n], t1[r, 0:n], MAGIC, None, ALU.subtract), reads=[t1], writes=[k_])
    kb.op(kb.pool, lambda: nc.gpsimd.tensor_tensor(fr[r, 0:n], x[r, 0:n], k_[r, 0:n], ALU.subtract), reads=[x, k_], writes=[fr])
    kb.op(kb.act, lambda: nc.scalar.activation(t1[r, 0:n], fr[r, 0:n], AF.Sin, scale=math.pi), reads=[fr], writes=[t1])
    kb.op(kb.act, lambda: nc.scalar.activation(k_[r, 0:n], fr[r, 0:n], AF.Sin, bias=C.halfpi[r, 0:1], scale=math.pi),
          reads=[fr, C.halfpi], writes=[k_])
    kb.op(kb.act, lambda: nc.scalar.activation(fr[r, 0:n], t1[r, 0:n], AF.Square, scale=math.sqrt(2.0)), reads=[t1], writes=[fr])
    kb.op(kb.dve, lambda: nc.vector.scalar_tensor_tensor(SIN[r, 0:n], t1[r, 0:n], 2.0, k_[r, 0:n], ALU.mult, ALU.mult),
          reads=[t1, k_], writes=[SIN])
    kb.op(kb.dve, lambda: nc.vector.tensor_scalar(COS[r, 0:n], fr[r, 0:n], -1.0, 1.0, ALU.mult, ALU.add), reads=[fr], writes=[COS])


def s5_setup(kb, G, li):
    nc = kb.nc
    INV2PI = 1.0 / (2.0 * math.pi)
    for which in ("B", "S"):
        kb.phase_begin()
        C = NS()
        load_consts(kb, C)
        N = 4096 if which == "B" else 64
        pre = "sb" if which == "B" else "ss"
        src = G.s5B if which == "B" else G.s5S
        T = {nm: kb.sb(f"{pre}_{nm}", [P, N], F32) for nm in
             ("lre", "lim", "ldt", "mag", "x", "cos", "sin", "w0", "w1", "w2", "den")}

        def ld(dst, idx):
            kb.dma(kb.sp, dst[:], src.t.ap()[li, idx], reads=[src], writes=[dst], sbuf=dst)
        ld(T["lre"], 0)
        ld(T["lim"], 1)
        ld(T["ldt"], 2)
        dt = T["ldt"]
        kb.op(kb.act, lambda: nc.scalar.activation(dt[:], dt[:], AF.Exp), reads=[dt], writes=[dt])
        lr = T["lre"]
        kb.op(kb.dve, lambda: nc.vector.tensor_scalar(lr[:], lr[:], -1e-4, None, ALU.min), reads=[lr], writes=[lr])
        mag = T["mag"]
        kb.op(kb.dve, lambda: nc.vector.tensor_tensor(mag[:], lr[:], dt[:], ALU.mult), reads=[lr, dt], writes=[mag])
        kb.op(kb.act, lambda: nc.scalar.activation(mag[:], mag[:], AF.Exp), reads=[mag], writes=[mag])
        x = T["x"]
        kb.op(kb.dve, lambda: nc.vector.scalar_tensor_tensor(x[:], T["lim"][:], INV2PI, dt[:], ALU.mult, ALU.mult),
              reads=[T["lim"], dt], writes=[x])
        cis_tables(kb, C, x, N, T["cos"], T["sin"], (T["w0"], T["w1"], T["w2"]))
        if which == "S":
            def st(srcb, idx):
                kb.dma(kb.sp, G.SPAR.t.ap()[idx], srcb[:], reads=[srcb], writes=[G.SPAR], sbuf=srcb)
            st(mag, 0)
            st(x, 1)
            st(T["cos"], 2)
            st(T["sin"], 3)
            x5 = T["den"]
            kb.op(kb.dve, lambda: nc.vector.tensor_scalar(x5[:], x[:], 512.0, None, ALU.mult), reads=[x], writes=[x5])
            c5 = kb.sb("ss_c5", [P, N], F32)
            s5 = kb.sb("ss_s5", [P, N], F32)
            w3 = [kb.sb(f"ss_w3{i}", [P, N], F32) for i in range(3)]
            cis_tables(kb, C, x5, N, c5, s5, w3)
            st(c5, 4)
            st(s5, 5)
            kb.phase_end()
            continue
        lbre, lbim = T["cos"], T["sin"]
        kb.op(kb.dve, lambda: nc.vector.tensor_tensor(lbre[:], lbre[:], mag[:], ALU.mult), reads=[lbre, mag], writes=[lbre])
        kb.op(kb.dve, lambda: nc.vector.tensor_tensor(lbim[:], lbim[:], mag[:], ALU.mult), reads=[lbim, mag], writes=[lbim])
        kb.op(kb.dve, lambda: nc.vector.tensor_scalar(lbre[:], lbre[:], -1.0, None, ALU.add), reads=[lbre], writes=[lbre])
        lim = T["lim"]
        den, w0, w1, w2 = T["den"], T["w0"], T["w1"], T["w2"]
        kb.op(kb.dve, lambda: nc.vector.tensor_tensor(den[:], lr[:], lr[:], ALU.mult), reads=[lr], writes=[den])
        kb.op(kb.dve, lambda: nc.vector.tensor_tensor(w0[:], lim[:], lim[:], ALU.mult), reads=[lim], writes=[w0])
        kb.op(kb.dve, lambda: nc.vector.tensor_tensor(den[:], den[:], w0[:], ALU.add), reads=[den, w0], writes=[den])
        kb.op(kb.dve, lambda: nc.vector.reciprocal(den[:], den[:]), reads=[den], writes=[den])
        kb.op(kb.dve, lambda: nc.vector.tensor_tensor(w0[:], lbre[:], lr[:], ALU.mult), reads=[lbre, lr], writes=[w0])
        kb.op(kb.dve, lambda: nc.vector.tensor_tensor(w2[:], lbim[:], lim[:], ALU.mult), reads=[lbim, lim], writes=[w2])
        kb.op(kb.dve, lambda: nc.vector.tensor_tensor(w0[:], w0[:], w2[:], ALU.add), reads=[w0, w2], writes=[w0])
        kb.op(kb.dve, lambda: nc.vector.tensor_tensor(w0[:], w0[:], den[:], ALU.mult), reads=[w0, den], writes=[w0])
        kb.op(kb.dve, lambda: nc.vector.tensor_tensor(w1[:], lbim[:], lr[:], ALU.mult), reads=[lbim, lr], writes=[w1])
        kb.op(kb.dve, lambda: nc.vector.tensor_tensor(w2[:], lbre[:], lim[:], ALU.mult), reads=[lbre, lim], writes=[w2])
        kb.op(kb.dve, lambda: nc.vector.tensor_tensor(w1[:], w1[:], w2[:], ALU.subtract), reads=[w1, w2], writes=[w1])
        kb.op(kb.dve, lambda: nc.vector.tensor_tensor(w1[:], w1[:], den[:], ALU.mult), reads=[w1, den], writes=[w1])
        bre, bim = T["lre"], T["ldt"]
        ld(bre, 3)
        ld(bim, 4)
        o_re = kb.sb("sb_ore", [P, N], BF16)
        o_im = kb.sb("sb_oim", [P, N], BF16)
        kb.op(kb.dve, lambda: nc.vector.tensor_tensor(w2[:], w0[:], bre[:], ALU.mult), reads=[w0, bre], writes=[w2])
        kb.op(kb.dve, lambda: nc.vector.tensor_tensor(den[:], w1[:], bim[:], ALU.mult), reads=[w1, bim], writes=[den])
        kb.op(kb.dve, lambda: nc.vector.tensor_tensor(o_re[:], w2[:], den[:], ALU.subtract), reads=[w2, den], writes=[o_re])
        kb.op(kb.dve, lambda: nc.vector.tensor_tensor(w2[:], w0[:], bim[:], ALU.mult), reads=[w0, bim], writes=[w2])
        kb.op(kb.dve, lambda: nc.vector.tensor_tensor(den[:], w1[:], bre[:], ALU.mult), reads=[w1, bre], writes=[den])
        kb.op(kb.dve, lambda: nc.vector.tensor_tensor(o_im[:], w2[:], den[:], ALU.add), reads=[w2, den], writes=[o_im])
        kb.dma(kb.sp, G.BWRE.t.ap(), o_re[:], reads=[o_re], writes=[G.BWRE], sbuf=o_re)
        kb.dma(kb.sp, G.BWIM.t.ap(), o_im[:], reads=[o_im], writes=[G.BWIM], sbuf=o_im)
        kb.phase_end()
    kb.phase_begin()
    cre = kb.sb("sc_cre", [P, 4096], F32)
    cim = kb.sb("sc_cim", [P, 4096], F32)
    kb.dma(kb.sp, cre[:], G.s5C.t.ap()[li, 0], reads=[G.s5C], writes=[cre], sbuf=cre)
    kb.dma(kb.sp, cim[:], G.s5C.t.ap()[li, 1], reads=[G.s5C], writes=[cim], sbuf=cim)
    o = [kb.sb(f"sc_o{i}", [P, 4096], BF16) for i in range(3)]
    kb.op(kb.dve, lambda: nc.vector.tensor_copy(o[0][:], cre[:]), reads=[cre], writes=[o[0]])
    kb.op(kb.dve, lambda: nc.vector.tensor_scalar(o[1][:], cre[:], -1.0, None, ALU.mult), reads=[cre], writes=[o[1]])
    kb.op(kb.dve, lambda: nc.vector.tensor_scalar(o[2][:], cim[:], -1.0, None, ALU.mult), reads=[cim], writes=[o[2]])
    for i in range(3):
        kb.dma(kb.sp, G.CW.t.ap()[i], o[i][:], reads=[o[i]], writes=[G.CW], sbuf=o[i])
    kb.phase_end()


def s5_main(kb, G, li):
    nc = kb.nc
    kb.phase_begin()
    C = NS()
    load_consts(kb, C)
    UT = kb.sb("UT", [P, 16, NT], BF16)
    ua = G.U.t.ap().rearrange("(c p) t -> p c t", p=P)
    for c in range(16):
        kb.dma(kb.sp, UT[:, c, :], ua[:, c, :], reads=[G.U], writes=[UT], sbuf=UT)
    BWre = kb.sb("BWre", [P, 32, P], BF16)
    BWim = kb.sb("BWim", [P, 32, P], BF16)
    kb.dma(kb.sp, BWre[:], G.BWRE.t.ap().rearrange("p (a b) -> p a b", b=P), reads=[G.BWRE], writes=[BWre], sbuf=BWre)
    kb.dma(kb.sp, BWim[:], G.BWIM.t.ap().rearrange("p (a b) -> p a b", b=P), reads=[G.BWIM], writes=[BWim], sbuf=BWim)
    CW = [kb.sb(f"CW{i}", [P, 64, 64], BF16) for i in range(3)]
    for i in range(3):
        kb.dma(kb.sp, CW[i][:], G.CW.t.ap()[i].rearrange("p (a b) -> p a b", b=64), reads=[G.CW], writes=[CW[i]], sbuf=CW[i])
    SPR = kb.sb("SPR", [P, 6, 64], F32)
    kb.dma(kb.sp, SPR[:], G.SPAR.t.ap().rearrange("k p n -> p k n"), reads=[G.SPAR], writes=[SPR], sbuf=SPR)
    dcol = kb.sb("dcol", [P, 16], F32)
    kb.dma(kb.sp, dcol[:], G.ssm_d.t.ap()[li], reads=[], writes=[dcol], sbuf=dcol)
    IOTA = kb.sb("iota", [P, 512], F32)
    kb.dma(kb.sp, IOTA[:], G.iota512.t.ap(), reads=[], writes=[IOTA], sbuf=IOTA)
    H0 = kb.sb("h0", [P, 2, 64], F32)
    kb.dma(kb.sp, H0[:], G.h0.t.ap()[li].rearrange("k p n -> p k n"), reads=[], writes=[H0], sbuf=H0)
    INIT = kb.sb("init", [P, 2, 64], F32)
    kb.op(kb.dve, lambda: nc.vector.memset(INIT[:], 0.0), writes=[INIT])
    HP = kb.sb("HP", [P, 2, 64], F32)
    HS = kb.sb("HS", [P, 2, 64], F32)
    SINI = kb.sb("SINI", [P, 2, 64], F32)
    ctmp = kb.sb("ctmp", [P, 4], F32)
    COS = [kb.sb(f"COS{q}", [P, 512], F32) for q in range(4)]
    SIN = [kb.sb(f"SIN{q}", [P, 512], F32) for q in range(4)]
    XW = kb.sb("xw", [P, 512], F32)
    TW = [kb.sb(f"tw{i}", [P, 512], F32) for i in range(3)]
    bR = Rot([kb.sb(f"bR{i}", [P, 512], F32) for i in range(2)])
    bI = Rot([kb.sb(f"bI{i}", [P, 512], F32) for i in range(2)])
    m1 = kb.sb("m1", [P, 512], F32)
    m2 = kb.sb("m2", [P, 512], F32)
    m3 = kb.sb("m3", [P, 512], F32)
    m4 = kb.sb("m4", [P, 512], F32)
    btre = Rot([kb.sb(f"btre{i}", [P, 512], F32) for i in range(2)])
    btim = Rot([kb.sb(f"btim{i}", [P, 512], F32) for i in range(2)])
    gre = Rot([kb.sb(f"gre{i}", [P, 512], F32) for i in range(2)])
    gim = Rot([kb.sb(f"gim{i}", [P, 512], F32) for i in range(2)])
    PR = [Rot([kb.sb(f"pr{a}_{i}", [P, 512], BF16) for i in range(2)]) for a in range(4)]
    yv = Rot([kb.sb(f"yv{i}", [P, 512], F32) for i in range(2)])
    gw = [kb.sb(f"gw{i}", [P, 512], F32) for i in range(2)]
    yo = Rot([kb.sb(f"yo{i}", [P, 512], BF16) for i in range(2)])
    psW = Rot([kb.psum[i] for i in range(6)])
    psY = Rot([kb.psum[6], kb.psum[7]])

    def cmul(o_re, o_im, a_re, a_im, g_re, g_im, rd, wr):
        kb.op(kb.dve, lambda: nc.vector.tensor_tensor(ctmp[:, 0:1], a_im, g_im, ALU.mult), reads=rd, writes=[ctmp])
        kb.op(kb.dve, lambda: nc.vector.scalar_tensor_tensor(o_re, g_re, a_re, ctmp[:, 0:1], ALU.mult, ALU.subtract),
              reads=rd + [ctmp], writes=wr)
        kb.op(kb.dve, lambda: nc.vector.tensor_tensor(ctmp[:, 1:2], a_im, g_re, ALU.mult), reads=rd, writes=[ctmp])
        kb.op(kb.dve, lambda: nc.vector.scalar_tensor_tensor(o_im, g_im, a_re, ctmp[:, 1:2], ALU.mult, ALU.add),
              reads=rd + [ctmp], writes=wr)

    for pi in range(64):
        cmul(SINI[:, 0, pi:pi + 1], SINI[:, 1, pi:pi + 1], SPR[:, 2, pi:pi + 1], SPR[:, 3, pi:pi + 1],
             H0[:, 0, pi:pi + 1], H0[:, 1, pi:pi + 1], [SPR, H0], [SINI])

    for j in range(16):
        for q in range(4):
            pi = 4 * j + q
            kb.op(kb.pool, lambda: nc.gpsimd.tensor_scalar(XW[:], IOTA[:], SPR[:, 1, pi:pi + 1], None, ALU.mult),
                  reads=[IOTA, SPR], writes=[XW])
            cis_tables(kb, C, XW, 512, COS[q], SIN[q], TW)
        for ti, (t0, n) in enumerate(TT):
            Yps = psY.next()
            for q in range(4):
                pi = 4 * j + q
                hf, s = q // 2, q % 2
                js = 2 * j + s
                rows = slice(64 * hf, 64 * hf + 64)
                psR = psW.next()
                psI = psW.next()
                kb.op(kb.pe, lambda: nc.tensor.matmul(psR[:, 0:n], BWre[rows, js, :], UT[rows, j, t0:t0 + n], start=True, stop=True),
                      reads=[BWre, UT], writes=[psR])
                kb.op(kb.pe, lambda: nc.tensor.matmul(psI[:, 0:n], BWim[rows, js, :], UT[rows, j, t0:t0 + n], start=True, stop=True),
                      reads=[BWim, UT], writes=[psI])
                r_ = bR.next()
                i_ = bI.next()
                kb.op(kb.act, lambda: nc.scalar.copy(r_[:, 0:n], psR[:, 0:n]), reads=[psR], writes=[r_])
                kb.op(kb.act, lambda: nc.scalar.copy(i_[:, 0:n], psI[:, 0:n]), reads=[psI], writes=[i_])
                cq_, sq_ = COS[q], SIN[q]
                tr, tim = btre.next(), btim.next()
                kb.op(kb.dve, lambda: nc.vector.tensor_tensor(m1[:, 0:n], cq_[:, 0:n], r_[:, 0:n], ALU.mult), reads=[cq_, r_], writes=[m1])
                kb.op(kb.dve, lambda: nc.vector.tensor_tensor(m2[:, 0:n], sq_[:, 0:n], i_[:, 0:n], ALU.mult), reads=[sq_, i_], writes=[m2])
                kb.op(kb.dve, lambda: nc.vector.tensor_tensor(tr[:, 0:n], m1[:, 0:n], m2[:, 0:n], ALU.add), reads=[m1, m2], writes=[tr])
                kb.op(kb.pool, lambda: nc.gpsimd.tensor_tensor(m3[:, 0:n], cq_[:, 0:n], i_[:, 0:n], ALU.mult), reads=[cq_, i_], writes=[m3])
                kb.op(kb.pool, lambda: nc.gpsimd.tensor_tensor(m4[:, 0:n], sq_[:, 0:n], r_[:, 0:n], ALU.mult), reads=[sq_, r_], writes=[m4])
                kb.op(kb.pool, lambda: nc.gpsimd.tensor_tensor(tim[:, 0:n], m3[:, 0:n], m4[:, 0:n], ALU.subtract), reads=[m3, m4], writes=[tim])
                g_r, g_i = gre.next(), gim.next()
                rcol = SPR[:, 0, pi:pi + 1]
                if ti == 0:
                    ini_r, ini_i, ini_rd = 0.0, 0.0, []
                elif ti < 4:
                    ini_r, ini_i, ini_rd = INIT[:, 0, pi:pi + 1], INIT[:, 1, pi:pi + 1], [INIT]
                else:
                    ini_r, ini_i, ini_rd = SINI[:, 0, pi:pi + 1], SINI[:, 1, pi:pi + 1], [SINI]
                kb.op(kb.dve, lambda: nc.vector.tensor_tensor_scan(g_r[:, 0:n], rcol.to_broadcast([P, n]), tr[:, 0:n], ini_r,
                                                                    ALU.mult, ALU.add), reads=[SPR, tr] + ini_rd, writes=[g_r])
                kb.op(kb.dve, lambda: nc.vector.tensor_tensor_scan(g_i[:, 0:n], rcol.to_broadcast([P, n]), tim[:, 0:n], ini_i,
                                                                    ALU.mult, ALU.add), reads=[SPR, tim] + ini_rd, writes=[g_i])
                lr_, li_ = g_r[:, n - 1:n], g_i[:, n - 1:n]
                if ti < 3:
                    cmul(INIT[:, 0, pi:pi + 1], INIT[:, 1, pi:pi + 1], SPR[:, 4, pi:pi + 1], SPR[:, 5, pi:pi + 1], lr_, li_,
                         [SPR, g_r, g_i], [INIT])
                elif ti == 3:
                    cmul(HP[:, 0, pi:pi + 1], HP[:, 1, pi:pi + 1], cq_[:, 511:512], sq_[:, 511:512], lr_, li_,
                         [cq_, sq_, g_r, g_i], [HP])
                else:
                    cmul(HS[:, 0, pi:pi + 1], HS[:, 1, pi:pi + 1], cq_[:, n - 1:n], sq_[:, n - 1:n], lr_, li_,
                         [cq_, sq_, g_r, g_i], [HS])
                p1, p2, p3, p4 = (PR[a].next() for a in range(4))
                kb.op(kb.dve, lambda: nc.vector.tensor_tensor(p1[:, 0:n], cq_[:, 0:n], g_r[:, 0:n], ALU.mult), reads=[cq_, g_r], writes=[p1])
                kb.op(kb.pool, lambda: nc.gpsimd.tensor_tensor(p2[:, 0:n], sq_[:, 0:n], g_i[:, 0:n], ALU.mult), reads=[sq_, g_i], writes=[p2])
                kb.op(kb.dve, lambda: nc.vector.tensor_tensor(p3[:, 0:n], sq_[:, 0:n], g_r[:, 0:n], ALU.mult), reads=[sq_, g_r], writes=[p3])
                kb.op(kb.pool, lambda: nc.gpsimd.tensor_tensor(p4[:, 0:n], cq_[:, 0:n], g_i[:, 0:n], ALU.mult), reads=[cq_, g_i], writes=[p4])
                for a, (pp, cw) in enumerate(((p1, CW[0]), (p2, CW[1]), (p3, CW[2]), (p4, CW[2]))):
                    first = (s == 0 and a == 0)
                    last = (s == 1 and a == 3)
                    kb.op(kb.pe, lambda: nc.tensor.matmul(Yps[rows, 0:n], cw[:, pi, :], pp[:, 0:n], start=first, stop=last),
                          reads=[cw, pp], writes=[Yps], inc=True)
            y = yv.next()
            kb.op(kb.dve, lambda: nc.vector.scalar_tensor_tensor(y[:, 0:n], UT[:, j, t0:t0 + n], dcol[:, j:j + 1], Yps[:, 0:n],
                                                                  ALU.mult, ALU.add), reads=[UT, dcol, Yps], writes=[y])
            kb.op(kb.pool, lambda: nc.gpsimd.tensor_tensor(gw[0][:, 0:n], y[:, 0:n], y[:, 0:n], ALU.mult), reads=[y], writes=[gw[0]])
            kb.op(kb.pool, lambda: nc.gpsimd.tensor_scalar(gw[0][:, 0:n], gw[0][:, 0:n], 0.044715, 1.0, ALU.mult, ALU.add),
                  reads=[gw[0]], writes=[gw[0]])
            kb.op(kb.pool, lambda: nc.gpsimd.tensor_tensor(gw[1][:, 0:n], gw[0][:, 0:n], y[:, 0:n], ALU.mult), reads=[gw[0], y], writes=[gw[1]])
            kb.op(kb.act, lambda: nc.scalar.activation(gw[1][:, 0:n], gw[1][:, 0:n], AF.Sigmoid, scale=1.5957691216057308),
                  reads=[gw[1]], writes=[gw[1]])
            o_ = yo.next()
            kb.op(kb.pool, lambda: nc.gpsimd.tensor_tensor(o_[:, 0:n], gw[1][:, 0:n], y[:, 0:n], ALU.mult), reads=[gw[1], y], writes=[o_])
            kb.dma(kb.sp, G.YG.t.ap()[j * P:(j + 1) * P, t0:t0 + n], o_[:, 0:n], reads=[o_], writes=[G.YG], sbuf=o_)
    kb.dma(kb.sp, G.hp_out[li].t.ap().rearrange("k p n -> p k n"), HP[:], reads=[HP], writes=[G.hp_out[li]], sbuf=HP)
    kb.dma(kb.sp, G.hs_out[li].t.ap().rearrange("k p n -> p k n"), HS[:], reads=[HS], writes=[G.hs_out[li]], sbuf=HS)
    kb.phase_end()


def mla_phase(kb, G, li):
    nc = kb.nc
    kb.phase_begin()
    C = NS()
    load_consts(kb, C)
    SCALE = 192.0 ** -0.5
    gk = kb.sb("gk", [P, 2], F32)
    kb.dma(kb.sp, gk[:], G.ab_k_norm.t.ap()[li], reads=[], writes=[gk], sbuf=gk)
    lat_src = G.lat_out[li].t.ap().rearrange("(c p) t -> p c t", p=P)
    latc_src = G.latc.t.ap()[li].rearrange("(c p) t -> p c t", p=P)
    NKS = [SEQ, NT]
    LAT = [kb.sb("LATP", [P, 4, SEQ], BF16), kb.sb("LATS", [P, 4, NT], BF16)]
    for c in range(4):
        kb.dma(kb.pool, LAT[0][:, c, :], lat_src[:, c, 0:SEQ], reads=[G.lat_out[li]], writes=[LAT[0]], sbuf=LAT[0])
        kb.dma(kb.pool, LAT[1][:, c, 0:SEQ], latc_src[:, c, :], reads=[G.latc], writes=[LAT[1]], sbuf=LAT[1])
        kb.dma(kb.pool, LAT[1][:, c, SEQ:NT], lat_src[:, c, SEQ:NT], reads=[G.lat_out[li]], writes=[LAT[1]], sbuf=LAT[1])
    KRF = [kb.sb("KRP", [64, SEQ], F32), kb.sb("KRS", [64, NT], F32)]
    kb.dma(kb.sp, KRF[0][:], G.kr_out[li].t.ap()[:, 0:SEQ], reads=[G.kr_out[li]], writes=[KRF[0]], sbuf=KRF[0])
    kb.dma(kb.sp, KRF[1][:, 0:SEQ], G.krc.t.ap()[li], reads=[G.krc], writes=[KRF[1]], sbuf=KRF[1])
    kb.dma(kb.sp, KRF[1][:, SEQ:NT], G.kr_out[li].t.ap()[:, SEQ:NT], reads=[G.kr_out[li]], writes=[KRF[1]], sbuf=KRF[1])
    psW = Rot([kb.psum[i] for i in range(4)])
    psO = Rot([kb.psum[4], kb.psum[5]])
    psL = Rot([kb.psum[6], kb.psum[7]])
    sq16 = Rot([kb.sb(f"msq{i}", [P, 512], BF16) for i in range(2)])
    SSR = [kb.sb("SSRP", [P, SEQ], F32), kb.sb("SSRS", [P, NT], F32)]
    ktiles = [[(0, 512), (512, 512), (1024, 512), (1536, 512)], [(0, 512), (512, 512), (1024, 512), (1536, 512), (2048, 16)]]
    for s_ in range(2):
        for (k0, kn) in ktiles[s_]:
            b2 = sq16.next()
            kb.op(kb.act, lambda: nc.scalar.activation(b2[0:64, 0:kn], KRF[s_][0:64, k0:k0 + kn], AF.Square), reads=[KRF[s_]], writes=[b2])
            ps = psW.next()
            kb.op(kb.pe, lambda: nc.tensor.matmul(ps[:, 0:kn], C.ones[0:64, :], b2[0:64, 0:kn], start=True, stop=True),
                  reads=[C.ones, b2], writes=[ps])
            kb.op(kb.dve, lambda: nc.vector.tensor_copy(SSR[s_][:, k0:k0 + kn], ps[:, 0:kn]), reads=[ps], writes=[SSR[s_]])
    Wh = Rot([kb.sb(f"Wh{i}", [P, 4, 256], BF16) for i in range(2)])
    KN = [kb.sb("KNP", [P, SEQ], BF16), kb.sb("KNS", [P, NT], BF16)]
    KR = [kb.sb("KRBP", [64, SEQ], BF16), kb.sb("KRBS", [64, NT], BF16)]
    V = [kb.sb("VP", [P, 16, P], BF16), kb.sb("VS", [P, 17, P], BF16)]
    QN = Rot([kb.sb(f"QN{i}", [P, NT], BF16) for i in range(2)])
    QR = Rot([kb.sb(f"QR{i}", [64, NT], BF16) for i in range(2)])
    tot = kb.sb("tot", [P, 512], F32)
    sd = kb.sb("msd", [P, 512], F32)
    rs = kb.sb("mrs", [P, 512], F32)
    PT = Rot([kb.sb(f"PT{i}", [P, 512], BF16) for i in range(3)])
    rec = kb.sb("rec", [P, 512], F32)
    on = kb.sb("on", [P, 512], F32)
    gat = Rot([kb.sb(f"gat{i}", [P, 512], BF16) for i in range(2)])
    ao = Rot([kb.sb(f"ao{i}", [P, 512], BF16) for i in range(2)])
    wsrc = G.ab_w_ukv.t.ap()[li].rearrange("(k p) m -> p k m", p=P)

    def finish(Ops, Lps, ncol, h, q0):
        kb.op(kb.dve, lambda: nc.vector.reciprocal(rec[:, 0:ncol], Lps[:, 0:ncol]), reads=[Lps], writes=[rec])
        kb.op(kb.dve, lambda: nc.vector.tensor_tensor(on[:, 0:ncol], Ops[:, 0:ncol], rec[:, 0:ncol], ALU.mult), reads=[Ops, rec], writes=[on])
        g_ = gat.next()
        kb.dma(kb.sp, g_[:, 0:ncol], G.GA.t.ap()[h * P:(h + 1) * P, q0:q0 + ncol], reads=[G.GA], writes=[g_], sbuf=g_)
        a_ = ao.next()
        kb.op(kb.pool, lambda: nc.gpsimd.tensor_tensor(a_[:, 0:ncol], on[:, 0:ncol], g_[:, 0:ncol], ALU.mult), reads=[on, g_], writes=[a_])
        kb.dma(kb.sp, G.AO.t.ap()[h * P:(h + 1) * P, q0:q0 + ncol], a_[:, 0:ncol], reads=[a_], writes=[G.AO], sbuf=a_)

    for h in range(16):
        w = Wh.next()
        kb.dma(kb.pool, w[:], wsrc[:, :, h * 256:(h + 1) * 256], reads=[], writes=[w], sbuf=w)
        qn, qr = QN.next(), QR.next()
        kb.dma(kb.sp, qn[:], G.QT.t.ap()[h * 192:h * 192 + 128, :], reads=[G.QT], writes=[qn], sbuf=qn)
        kb.dma(kb.sp, qr[:], G.QT.t.ap()[h * 192 + 128:h * 192 + 192, :], reads=[G.QT], writes=[qr], sbuf=qr)
        for s_ in range(2):
            for (k0, kn) in ktiles[s_]:
                psK = psW.next()
                for c in range(4):
                    kb.op(kb.pe, lambda: nc.tensor.matmul(psK[:, 0:kn], w[:, c, 0:128], LAT[s_][:, c, k0:k0 + kn], start=(c == 0), stop=(c == 3)),
                          reads=[w, LAT[s_]], writes=[psK], inc=(c == 3))
                b2 = sq16.next()
                kb.op(kb.act, lambda: nc.scalar.activation(b2[:, 0:kn], psK[:, 0:kn], AF.Square), reads=[psK], writes=[b2])
                pss = psW.next()
                kb.op(kb.pe, lambda: nc.tensor.matmul(pss[:, 0:kn], C.ones[:], b2[:, 0:kn], start=True, stop=True),
                      reads=[C.ones, b2], writes=[pss])
                kb.op(kb.dve, lambda: nc.vector.tensor_tensor(tot[:, 0:kn], pss[:, 0:kn], SSR[s_][:, k0:k0 + kn], ALU.add),
                      reads=[pss, SSR[s_]], writes=[tot])
                kb.op(kb.act, lambda: nc.scalar.activation(sd[:, 0:kn], tot[:, 0:kn], AF.Sqrt, bias=C.eps[:, 0:1], scale=1.0 / 192),
                      reads=[tot, C.eps], writes=[sd])
                kb.op(kb.dve, lambda: nc.vector.reciprocal(rs[:, 0:kn], sd[:, 0:kn]), reads=[sd], writes=[rs])
                kb.op(kb.dve, lambda: nc.vector.scalar_tensor_tensor(KN[s_][:, k0:k0 + kn], psK[:, 0:kn], gk[:, 0:1], rs[:, 0:kn], ALU.mult, ALU.mult),
                      reads=[psK, gk, rs], writes=[KN[s_]])
                kb.op(kb.dve, lambda: nc.vector.scalar_tensor_tensor(KR[s_][0:64, k0:k0 + kn], KRF[s_][0:64, k0:k0 + kn], gk[0:64, 1:2],
                                                                      rs[0:64, 0:kn], ALU.mult, ALU.mult),
                      reads=[KRF[s_], gk, rs], writes=[KR[s_]])
            nkt = 16 if s_ == 0 else 17
            for kt4 in range(0, nkt, 4):
                psV = psW.next()
                kts = list(range(kt4, min(kt4 + 4, nkt)))
                for kt in kts:
                    kk = P if kt < 16 else 16
                    for c in range(4):
                        kb.op(kb.pe, lambda: nc.tensor.matmul(psV[0:kk, (kt - kt4) * P:(kt - kt4 + 1) * P], LAT[s_][:, c, kt * P:kt * P + kk],
                                                              w[:, c, 128:256], start=(c == 0), stop=(c == 3)),
                              reads=[LAT[s_], w], writes=[psV], inc=(c == 3 and kt == kts[-1]))
                if len(kts) == 4:
                    kb.op(kb.act, lambda: nc.scalar.copy(V[s_][:, kt4:kt4 + 4, :], psV[:, :].rearrange("p (a b) -> p a b", b=P)),
                          reads=[psV], writes=[V[s_]])
                else:
                    kb.op(kb.act, lambda: nc.scalar.copy(V[s_][0:16, 16, :], psV[0:16, 0:P]), reads=[psV], writes=[V[s_]])
        for qt in range(4):
            q0 = qt * 512
            Ops, Lps = psO.next(), psL.next()
            nkt = 4 * (qt + 1)
            for kt in range(nkt):
                col0 = max(0, kt - 4 * qt) * P
                ncols = 512 - col0
                Sps = psW.next()
                kb.op(kb.pe, lambda: nc.tensor.matmul(Sps[:, 0:ncols], KN[0][:, kt * P:(kt + 1) * P], qn[:, q0 + col0:q0 + 512], start=True, stop=False),
                      reads=[KN[0], qn], writes=[Sps], inc=False)
                kb.op(kb.pe, lambda: nc.tensor.matmul(Sps[:, 0:ncols], KR[0][0:64, kt * P:(kt + 1) * P], qr[0:64, q0 + col0:q0 + 512], start=False, stop=True),
                      reads=[KR[0], qr], writes=[Sps], inc=True)
                pt = PT.next()
                kb.op(kb.act, lambda: nc.scalar.activation(pt[:, 0:ncols], Sps[:, 0:ncols], AF.Exp, scale=SCALE), reads=[Sps], writes=[pt])
                if kt >= 4 * qt:
                    kb.op(kb.dve, lambda: nc.vector.memset(pt[64:128, 0:64], 0.0), writes=[pt])
                last = (kt == nkt - 1)
                kb.op(kb.pe, lambda: nc.tensor.matmul(Ops[:, col0:512], V[0][:, kt, :], pt[:, 0:ncols], start=(kt == 0), stop=last),
                      reads=[V[0], pt], writes=[Ops], inc=False)
                kb.op(kb.pe, lambda: nc.tensor.matmul(Lps[:, col0:512], C.ones[:], pt[:, 0:ncols], start=(kt == 0), stop=last),
                      reads=[C.ones, pt], writes=[Lps], inc=True)
            finish(Ops, Lps, 512, h, q0)
        Ops, Lps = psO.next(), psL.next()
        for kt in range(17):
            kk = P if kt < 16 else 16
            Sps = psW.next()
            kb.op(kb.pe, lambda: nc.tensor.matmul(Sps[0:kk, 0:TS], KN[1][:, kt * P:kt * P + kk], qn[:, SEQ:NT], start=True, stop=False),
                  reads=[KN[1], qn], writes=[Sps], inc=False)
            kb.op(kb.pe, lambda: nc.tensor.matmul(Sps[0:kk, 0:TS], KR[1][0:64, kt * P:kt * P + kk], qr[0:64, SEQ:NT], start=False, stop=True),
                  reads=[KR[1], qr], writes=[Sps], inc=True)
            pt = PT.next()
            kb.op(kb.act, lambda: nc.scalar.activation(pt[0:kk, 0:TS], Sps[0:kk, 0:TS], AF.Exp, scale=SCALE), reads=[Sps], writes=[pt])
            kb.op(kb.pe, lambda: nc.tensor.matmul(Ops[:, 0:TS], V[1][0:kk, kt, :], pt[0:kk, 0:TS], start=(kt == 0), stop=(kt == 16)),
                  reads=[V[1], pt], writes=[Ops], inc=False)
            kb.op(kb.pe, lambda: nc.tensor.matmul(Lps[:, 0:TS], C.ones[0:kk, :], pt[0:kk, 0:TS], start=(kt == 0), stop=(kt == 16)),
                  reads=[C.ones, pt], writes=[Lps], inc=True)
        finish(Ops, Lps, TS, h, SEQ)
    kb.phase_end()


def load_resident(kb, src, KC, A):
    sa = src.t.ap().rearrange("(c p) t -> p c t", p=P)
    for c in range(KC):
        kb.dma(kb.sp, A[:, c, :], sa[:, c, :], reads=[src], writes=[A], sbuf=A)


def glu_phase(kb, G, li):
    nc = kb.nc
    kb.phase_begin()
    A = kb.sb("Aglu", [P, 16, NT], BF16)
    load_resident(kb, G.YG, 16, A)
    hold = [kb.sb(f"ghold{i}", [P, 512], F32) for i in range(5)]
    sig = Rot([kb.sb(f"gsig{i}", [P, 512], F32) for i in range(2)])
    gbt = Rot([kb.sb(f"ggb{i}", [P, 512], BF16) for i in range(2)])
    outb = Rot([kb.sb(f"gout{i}", [P, 512], BF16) for i in range(2)])
    jobs = []
    for m in range(16):
        jobs.append((m * P, P, ("a", m)))
        jobs.append((2048 + m * P, P, ("b", m)))

    def epi(tag, ji, ti, ps, msz):
        t0, n = TT[ti]
        kind, m = tag
        if kind == "a":
            kb.op(kb.dve, lambda: nc.vector.tensor_copy(hold[ti][:, 0:n], ps[:, 0:n]), reads=[ps], writes=[hold[ti]])
            return
        s = sig.next()
        kb.op(kb.act, lambda: nc.scalar.activation(s[:, 0:n], ps[:, 0:n], AF.Sigmoid), reads=[ps], writes=[s])
        g_ = gbt.next()
        kb.dma(kb.sp, g_[:, 0:n], G.GB.t.ap()[m * P:(m + 1) * P, t0:t0 + n], reads=[G.GB], writes=[g_], sbuf=g_)
        kb.op(kb.pool, lambda: nc.gpsimd.tensor_tensor(s[:, 0:n], s[:, 0:n], hold[ti][:, 0:n], ALU.mult), reads=[s, hold[ti]], writes=[s])
        o = outb.next()
        kb.op(kb.dve, lambda: nc.vector.tensor_tensor(o[:, 0:n], s[:, 0:n], g_[:, 0:n], ALU.mult), reads=[s, g_], writes=[o])
        kb.dma(kb.sp, G.AO.t.ap()[2048 + m * P:2048 + (m + 1) * P, t0:t0 + n], o[:, 0:n], reads=[o], writes=[G.AO], sbuf=o)

    linear(kb, A, 16, G.ssm_w_glu.t.ap()[li], jobs, epi, "lg")
    kb.phase_end()


def out_proj_phase(kb, G, W_ap, Xin, Xout):
    nc = kb.nc
    kb.phase_begin()
    A = kb.sb("Aout", [P, 32, NT], BF16)
    load_resident(kb, G.AO, 32, A)
    xt = Rot([kb.sb(f"oxt{i}", [P, 512], F32) for i in range(3)])
    ot = Rot([kb.sb(f"oot{i}", [P, 512], F32) for i in range(3)])
    jobs = [(m * P, P, ("o", m)) for m in range(32)]

    def epi(tag, ji, ti, ps, msz):
        t0, n = TT[ti]
        m = tag[1]
        x_ = xt.next()
        kb.dma(kb.sp, x_[:, 0:n], Xin.t.ap()[m * P:(m + 1) * P, t0:t0 + n], reads=[Xin], writes=[x_], sbuf=x_)
        o = ot.next()
        kb.op(kb.dve, lambda: nc.vector.tensor_tensor(o[:, 0:n], ps[:, 0:n], x_[:, 0:n], ALU.add), reads=[ps, x_], writes=[o])
        kb.dma(kb.sp, Xout.t.ap()[m * P:(m + 1) * P, t0:t0 + n], o[:, 0:n], reads=[o], writes=[Xout], sbuf=o)

    linear(kb, A, 32, W_ap, jobs, epi, "lo")
    kb.phase_end()


def c_phase1(kb, G, li, X):
    nc = kb.nc
    kb.phase_begin()
    C = NS()
    load_consts(kb, C)
    bd = kb.sb("bdones", [P, P], BF16)
    kb.op(kb.dve, lambda: nc.vector.memset(bd[:], 0.0), writes=[bd])
    kb.op(kb.dve, lambda: nc.vector.memset(bd[0:64, 0:64], 1.0), writes=[bd])
    kb.op(kb.dve, lambda: nc.vector.memset(bd[64:128, 64:128], 1.0), writes=[bd])
    gcol = kb.sb("cgcol", [P, 32], F32)
    kb.dma(kb.sp, gcol[:], G.c_norm.t.ap()[li], reads=[], writes=[gcol], sbuf=gcol)
    gqk = kb.sb("cgqk", [P, 2], F32)
    kb.dma(kb.sp, gqk[:], G.c_qk_norm.t.ap()[li], reads=[], writes=[gqk], sbuf=gqk)
    A = kb.sb("Ac", [P, 32, NT], BF16)
    rmsnorm_resident(kb, C, X, 32, gcol, A, "nc")
    sq = Rot([kb.sb(f"csq{i}", [P, 512], BF16) for i in range(2)])
    sd = kb.sb("csd", [P, 512], F32)
    rs = kb.sb("crs", [P, 512], F32)
    st32 = Rot([kb.sb(f"cst32_{i}", [P, 512], F32) for i in range(3)])
    st16 = Rot([kb.sb(f"cst16_{i}", [P, 512], BF16) for i in range(3)])
    jobs = [(m * P, P, ("q", m)) for m in range(32)]
    jobs += [(4096 + m * P, P, ("k", m)) for m in range(4)]
    jobs += [(4608 + m * P, P, ("v", m)) for m in range(4)]
    jobs += [(5120 + m * P, P, ("g", m)) for m in range(32)]
    dd = Buf("dd_dummy")
    kb.dma(kb.sp, G.k_out[li].t.ap()[:, 128:240], G.kcache.t.ap()[li][:, 16:128], reads=[G.kcache], writes=[G.k_out[li]], sbuf=dd)
    kb.dma(kb.sp, G.v_out[li].t.ap()[:, 128:240], G.vcacheT.t.ap()[li][:, 16:128], reads=[G.vcacheT], writes=[G.v_out[li]], sbuf=dd)
    kb.phase_bufs.append(dd)

    def epi(tag, ji, ti, ps, msz):
        t0, n = TT[ti]
        kind, m = tag
        rows = slice(m * P, (m + 1) * P)
        if kind in ("q", "k"):
            b2 = sq.next()
            kb.op(kb.act, lambda: nc.scalar.activation(b2[:, 0:n], ps[:, 0:n], AF.Square), reads=[ps], writes=[b2])
            pss = kb.next_ps()
            kb.op(kb.pe, lambda: nc.tensor.matmul(pss[:, 0:n], bd[:], b2[:, 0:n], start=True, stop=True), reads=[bd, b2], writes=[pss])
            rstd_from_ss(kb, C, pss, n, 1.0 / 64, rs, slice(0, n), sd)
            gc = gqk[:, 0:1] if kind == "q" else gqk[:, 1:2]
            if kind == "q":
                o = st16.next()
                dst = G.QS
            else:
                o = st32.next()
                dst = G.KS
            kb.op(kb.dve, lambda: nc.vector.scalar_tensor_tensor(o[:, 0:n], ps[:, 0:n], gc, rs[:, 0:n], ALU.mult, ALU.mult),
                  reads=[ps, gqk, rs], writes=[o])
            kb.dma(kb.sp, dst.t.ap()[rows, t0:t0 + n], o[:, 0:n], reads=[o], writes=[dst], sbuf=o)
            outw = G.k_out[li]
        elif kind == "v":
            o = st32.next()
            kb.op(kb.act, lambda: nc.scalar.copy(o[:, 0:n], ps[:, 0:n]), reads=[ps], writes=[o])
            kb.dma(kb.sp, G.VS.t.ap()[rows, t0:t0 + n], o[:, 0:n], reads=[o], writes=[G.VS], sbuf=o)
            outw = G.v_out[li]
        else:
            o = st16.next()
            kb.op(kb.act, lambda: nc.scalar.activation(o[:, 0:n], ps[:, 0:n], AF.Silu), reads=[ps], writes=[o])
            kb.dma(kb.sp, G.GC.t.ap()[rows, t0:t0 + n], o[:, 0:n], reads=[o], writes=[G.GC], sbuf=o)
        if kind in ("k", "v"):
            if ti == 3:
                kb.dma(kb.sp, outw.t.ap()[rows, 0:128], o[:, 384:512], reads=[o], writes=[outw], sbuf=o)
            if ti == 4:
                kb.dma(kb.sp, outw.t.ap()[rows, 240:256], o[:, 0:16], reads=[o], writes=[outw], sbuf=o)

    linear(kb, A, 32, G.c_w_in.t.ap()[li], jobs, epi, "lc")
    kb.phase_end()


LB = 384
NCOPY = 144


def rel_bucket_np(rel):
    nb = 16
    max_exact = 8
    ret = np.where(rel > 0, nb, 0)
    n = np.abs(rel)
    nf = np.maximum(n, 1).astype(np.float32)
    large = max_exact + (np.log(nf / max_exact) / math.log(128 / max_exact) * (nb - max_exact)).astype(np.int32)
    large = np.minimum(large, nb - 1)
    return ret + np.where(n < max_exact, n, large)


def bias_setup(kb, G):
    nc = kb.nc
    kb.phase_begin()
    tab = kb.sb("btab", [32, 64], F32)
    oh = kb.sb("boh", [32, LB], F32)
    kb.dma(kb.sp, tab[:], G.rel_table.t.ap(), reads=[], writes=[tab], sbuf=tab)
    kb.dma(kb.sp, oh[:], G.onehot.t.ap(), reads=[], writes=[oh], sbuf=oh)
    ps = kb.next_ps()
    kb.op(kb.pe, lambda: nc.tensor.matmul(ps[0:64, 0:LB], tab[0:32, 0:64], oh[0:32, 0:LB], start=True, stop=True),
          reads=[tab, oh], writes=[ps])
    gsb = kb.sb("bg", [64, LB], F32)
    kb.op(kb.dve, lambda: nc.vector.tensor_copy(gsb[:], ps[0:64, 0:LB]), reads=[ps], writes=[gsb])
    kb.dma(kb.sp, G.GD.t.ap(), gsb[:], reads=[gsb], writes=[G.GD], sbuf=gsb)
    dd = Buf("dd_bias")
    kb.phase_bufs.append(dd)
    for h4 in range(0, 64, 4):
        src = bass.AP(tensor=G.GD.t, offset=h4 * LB, ap=[[LB, 4], [0, NCOPY], [1, LB]])
        kb.dma(kb.sp, G.GT.t.ap()[h4:h4 + 4], src, reads=[G.GD], writes=[G.GT], sbuf=dd)
    tp = Rot([kb.sb(f"btp{i}", [P, 256], F32) for i in range(3)])
    ts = Rot([kb.sb(f"bts{i}", [P, 32], F32) for i in range(3)])
    for h in range(64):
        t = tp.next()
        src = bass.AP(tensor=G.GT.t, offset=h * NCOPY * LB + 127, ap=[[LB - 1, 128], [1, 256]])
        kb.dma(kb.sp, t[:], src, reads=[G.GT], writes=[t], sbuf=t)
        kb.op(kb.act, lambda: nc.scalar.activation(t[:], t[:], AF.Exp), reads=[t], writes=[t])
        kb.op(kb.dve, lambda: nc.vector.memset(t[0:64, 192:256], 0.0), writes=[t])
        kb.op(kb.dve, lambda: nc.vector.memset(t[64:128, 0:64], 0.0), writes=[t])
        kb.dma(kb.sp, G.EBP.t.ap()[h], t[:], reads=[t], writes=[G.EBP], sbuf=t)
        u = ts.next()
        srcA = bass.AP(tensor=G.GT.t, offset=h * NCOPY * LB + 255, ap=[[LB - 1, 128], [1, 16]])
        srcB = bass.AP(tensor=G.GT.t, offset=h * NCOPY * LB + 255 + 128 * (LB - 1), ap=[[LB - 1, 16], [1, 16]])
        kb.dma(kb.sp, u[:, 0:16], srcA, reads=[G.GT], writes=[u], sbuf=u)
        kb.dma(kb.sp, u[0:16, 16:32], srcB, reads=[G.GT], writes=[u], sbuf=u)
        kb.op(kb.act, lambda: nc.scalar.activation(u[:, 0:16], u[:, 0:16], AF.Exp), reads=[u], writes=[u])
        kb.op(kb.act, lambda: nc.scalar.activation(u[0:16, 16:32], u[0:16, 16:32], AF.Exp), reads=[u], writes=[u])
        kb.dma(kb.sp, G.EBS.t.ap()[h], u[:], reads=[u], writes=[G.EBS], sbuf=u)
    kb.phase_end()


def c_phase2(kb, G, li):
    nc = kb.nc
    kb.phase_begin()
    C = NS()
    load_consts(kb, C)
    SCALE = 64.0 ** -0.5
    ident = kb.sb("ident", [P, P], F32)
    kb.dma(kb.sp, ident[:], G.ident.t.ap(), reads=[], writes=[ident], sbuf=ident)
    ES = kb.sb("ES", [P, 64], F32)
    kb.dma(kb.sp, ES[:], G.sinks.t.ap()[li], reads=[], writes=[ES], sbuf=ES)
    kb.op(kb.act, lambda: nc.scalar.activation(ES[:], ES[:], AF.Exp), reads=[ES], writes=[ES])
    psW = Rot([kb.psum[i] for i in range(4)])
    psO = Rot([kb.psum[4], kb.psum[5]])
    psL = Rot([kb.psum[6], kb.psum[7]])
    V2 = [kb.sb(f"V2_{c}", [P, 16, P], BF16) for c in range(4)]
    VsA = kb.sb("VsA", [P, 512], BF16)
    VsB = kb.sb("VsB", [16, 512], BF16)
    kb.dma(kb.pool, VsA[:], G.vcache.t.ap()[li], reads=[], writes=[VsA], sbuf=VsA)
    vin = Rot([kb.sb(f"vin{i}", [P, NT], F32) for i in range(2)])
    for c in range(4):
        v_ = vin.next()
        kb.dma(kb.sp, v_[:], G.VS.t.ap()[c * P:(c + 1) * P, :], reads=[G.VS], writes=[v_], sbuf=v_)
        for kt4 in range(0, 16, 4):
            ps = psW.next()
            for a in range(4):
                kt = kt4 + a
                kb.op(kb.pe, lambda: nc.tensor.transpose(ps[:, a * P:(a + 1) * P], v_[:, kt * P:(kt + 1) * P], ident[:]),
                      reads=[v_, ident], writes=[ps], inc=(a == 3))
            kb.op(kb.act, lambda: nc.scalar.copy(V2[c][:, kt4:kt4 + 4, :], ps[:, :].rearrange("p (a b) -> p a b", b=P)),
                  reads=[ps], writes=[V2[c]])
        ps = psW.next()
        kb.op(kb.pe, lambda: nc.tensor.transpose(ps[0:16, 0:P], v_[:, SEQ:NT], ident[:]), reads=[v_, ident], writes=[ps])
        kb.op(kb.act, lambda: nc.scalar.copy(VsB[0:16, c * P:(c + 1) * P], ps[0:16, 0:P]), reads=[ps], writes=[VsB])
    KT2 = Rot([kb.sb(f"KT2_{i}", [P, SEQ], BF16) for i in range(2)])
    KTs = Rot([kb.sb(f"KTs_{i}", [P, 144], BF16) for i in range(2)])
    Q2 = Rot([kb.sb(f"Q2_{i}", [P, NT], BF16) for i in range(2)])
    GCt = Rot([kb.sb(f"GCt_{i}", [P, NT], BF16) for i in range(2)])
    EBp = Rot([kb.sb(f"EBp{i}", [P, 256], F32) for i in range(2)])
    EBs = Rot([kb.sb(f"EBs{i}", [P, 32], F32) for i in range(2)])
    Et = Rot([kb.sb(f"Et{i}", [P, 256], F32) for i in range(3)])
    PTt = Rot([kb.sb(f"PTt{i}", [P, 256], BF16) for i in range(3)])
    lt = kb.sb("clt", [P, 512], F32)
    ot = kb.sb("cot", [P, 512], F32)
    aot = Rot([kb.sb(f"caot{i}", [P, 512], BF16) for i in range(2)])

    def finish(Ops, Lps, rows, ncol, h, q0, gct):
        kb.op(kb.dve, lambda: nc.vector.tensor_scalar(lt[rows, 0:ncol], Lps[rows, 0:ncol], ES[rows, h:h + 1], None, ALU.add),
              reads=[Lps, ES], writes=[lt])
        kb.op(kb.dve, lambda: nc.vector.reciprocal(lt[rows, 0:ncol], lt[rows, 0:ncol]), reads=[lt], writes=[lt])
        kb.op(kb.dve, lambda: nc.vector.tensor_tensor(ot[rows, 0:ncol], Ops[rows, 0:ncol], lt[rows, 0:ncol], ALU.mult), reads=[Ops, lt], writes=[ot])
        a_ = aot.next()
        kb.op(kb.pool, lambda: nc.gpsimd.tensor_tensor(a_[rows, 0:ncol], ot[rows, 0:ncol], gct[rows, q0:q0 + ncol], ALU.mult),
              reads=[ot, gct], writes=[a_])
        kb.dma(kb.sp, G.AO.t.ap()[h * 64:(h + 1) * 64, q0:q0 + ncol], a_[rows, 0:ncol], reads=[a_], writes=[G.AO], sbuf=a_)

    for g in range(8):
        kt2, kts = KT2.next(), KTs.next()
        for hh in range(2):
            r_ = slice(64 * hh, 64 * hh + 64)
            kb.dma(kb.pool, kt2[r_, :], G.KS.t.ap()[g * 64:(g + 1) * 64, 0:SEQ], reads=[G.KS], writes=[kt2], sbuf=kt2)
            kb.dma(kb.pool, kts[r_, 0:128], G.kcache.t.ap()[li][g * 64:(g + 1) * 64, :], reads=[G.kcache], writes=[kts], sbuf=kts)
            kb.dma(kb.pool, kts[r_, 128:144], G.KS.t.ap()[g * 64:(g + 1) * 64, SEQ:NT], reads=[G.KS], writes=[kts], sbuf=kts)
        vc, vo = g // 2, (g % 2) * 64
        for hp in range(4):
            ch = g * 4 + hp
            q2, gct = Q2.next(), GCt.next()
            kb.dma(kb.sp, q2[:], G.QS.t.ap()[ch * P:(ch + 1) * P, :], reads=[G.QS], writes=[q2], sbuf=q2)
            kb.dma(kb.sp, gct[:], G.GC.t.ap()[ch * P:(ch + 1) * P, :], reads=[G.GC], writes=[gct], sbuf=gct)
            for hh in range(2):
                h = 2 * ch + hh
                rows = slice(64 * hh, 64 * hh + 64)
                ebp, ebs = EBp.next(), EBs.next()
                kb.dma(kb.sp, ebp[:], G.EBP.t.ap()[h], reads=[G.EBP], writes=[ebp], sbuf=ebp)
                kb.dma(kb.sp, ebs[:], G.EBS.t.ap()[h], reads=[G.EBS], writes=[ebs], sbuf=ebs)
                Ops, Lps = psO.next(), psL.next()
                for j in range(16):
                    nq = min(256, SEQ - P * j)
                    Sps = psW.next()
                    kb.op(kb.pe, lambda: nc.tensor.matmul(Sps[:, 0:nq], kt2[rows, j * P:(j + 1) * P], q2[rows, j * P:j * P + nq], start=True, stop=True),
                          reads=[kt2, q2], writes=[Sps])
                    e_ = Et.next()
                    kb.op(kb.act, lambda: nc.scalar.activation(e_[:, 0:nq], Sps[:, 0:nq], AF.Exp, scale=SCALE), reads=[Sps], writes=[e_])
                    pt = PTt.next()
                    kb.op(kb.dve, lambda: nc.vector.tensor_tensor(pt[:, 0:nq], e_[:, 0:nq], ebp[:, 0:nq], ALU.mult), reads=[e_, ebp], writes=[pt])
                    cb = (j % 4) * P
                    kb.op(kb.pe, lambda: nc.tensor.matmul(Ops[rows, cb:cb + P], V2[vc][:, j, vo:vo + 64], pt[:, 0:P], start=(j == 0), stop=True),
                          reads=[V2[vc], pt], writes=[Ops], inc=False)
                    kb.op(kb.pe, lambda: nc.tensor.matmul(Lps[rows, cb:cb + P], C.ones[:, 0:64], pt[:, 0:P], start=(j == 0), stop=True),
                          reads=[C.ones, pt], writes=[Lps], inc=True)
                    if j % 4 == 3:
                        Ops_done, Lps_done = Ops, Lps
                        if j < 15:
                            Ops, Lps = psO.next(), psL.next()
                    if j < 15:
                        cb2 = ((j + 1) % 4) * P
                        kb.op(kb.pe, lambda: nc.tensor.matmul(Ops[rows, cb2:cb2 + P], V2[vc][:, j, vo:vo + 64], pt[:, P:2 * P], start=True, stop=False),
                              reads=[V2[vc], pt], writes=[Ops], inc=False)
                        kb.op(kb.pe, lambda: nc.tensor.matmul(Lps[rows, cb2:cb2 + P], C.ones[:, 0:64], pt[:, P:2 * P], start=True, stop=False),
                              reads=[C.ones, pt], writes=[Lps], inc=True)
                    if j % 4 == 3:
                        finish(Ops_done, Lps_done, rows, 512, h, (j - 3) * P, gct)
                Ops, Lps = psO.next(), psL.next()
                SA, SB = psW.next(), psW.next()
                kb.op(kb.pe, lambda: nc.tensor.matmul(SA[:, 0:TS], kts[rows, 0:128], q2[rows, SEQ:NT], start=True, stop=True), reads=[kts, q2], writes=[SA])
                kb.op(kb.pe, lambda: nc.tensor.matmul(SB[0:16, 0:TS], kts[rows, 128:144], q2[rows, SEQ:NT], start=True, stop=True), reads=[kts, q2], writes=[SB])
                e_ = Et.next()
                kb.op(kb.act, lambda: nc.scalar.activation(e_[:, 0:TS], SA[:, 0:TS], AF.Exp, scale=SCALE), reads=[SA], writes=[e_])
                kb.op(kb.act, lambda: nc.scalar.activation(e_[0:16, 16:32], SB[0:16, 0:TS], AF.Exp, scale=SCALE), reads=[SB], writes=[e_])
                pt = PTt.next()
                kb.op(kb.dve, lambda: nc.vector.tensor_tensor(pt[:, 0:TS], e_[:, 0:TS], ebs[:, 0:TS], ALU.mult), reads=[e_, ebs], writes=[pt])
                kb.op(kb.dve, lambda: nc.vector.tensor_tensor(pt[0:16, 16:32], e_[0:16, 16:32], ebs[0:16, 16:32], ALU.mult), reads=[e_, ebs], writes=[pt])
                kb.op(kb.pe, lambda: nc.tensor.matmul(Ops[rows, 0:TS], VsA[:, g * 64:(g + 1) * 64], pt[:, 0:TS], start=True, stop=False), reads=[VsA, pt], writes=[Ops], inc=False)
                kb.op(kb.pe, lambda: nc.tensor.matmul(Ops[rows, 0:TS], VsB[0:16, g * 64:(g + 1) * 64], pt[0:16, 16:32], start=False, stop=True), reads=[VsB, pt], writes=[Ops], inc=False)
                kb.op(kb.pe, lambda: nc.tensor.matmul(Lps[rows, 0:TS], C.ones[:, 0:64], pt[:, 0:TS], start=True, stop=False), reads=[C.ones, pt], writes=[Lps], inc=False)
                kb.op(kb.pe, lambda: nc.tensor.matmul(Lps[rows, 0:TS], C.ones[0:16, 0:64], pt[0:16, 16:32], start=False, stop=True), reads=[C.ones, pt], writes=[Lps], inc=True)
                finish(Ops, Lps, rows, TS, h, SEQ, gct)
    kb.phase_end()


_CACHE = {}


def kernel(**inputs):
    inp = {k: np.asarray(v) for k, v in inputs.items()}
    if "kb" not in _CACHE:
        _CACHE["kb"] = build()
    kb = _CACHE["kb"]
    shared = prep_shared(inp)
    used = set(k for k, b in kb.dram.items())
    in_maps = []
    for c in range(8):
        m = prep_core(inp, c, shared)
        in_maps.append({k: np.ascontiguousarray(v, dtype=np.float32) for k, v in m.items() if k in used})
    res = run_bass_kernel_spmd(kb.nc, in_maps, core_ids=list(range(8)))
    R = res.results
    f32 = np.float32
    y_p = np.stack([R[b]["y_out"][:, :SEQ].T for b in range(4)]).astype(f32)
    y_s = np.stack([R[c]["y_out"][:, SEQ:].T for c in range(8)]).astype(f32)

    def tok(name, lo, hi, cores):
        return np.stack([np.stack([R[c][f"{name}{i}"][:, lo:hi].T for c in cores]) for i in range(2)]).astype(f32)

    def st(name, k, cores):
        return np.stack([np.stack([s_unlayout(R[c][f"{name}{i}"][k]) for c in cores]) for i in range(2)]).astype(f32)

    def win(name, lo, hi, cores):
        return np.stack([np.stack([R[c][f"{name}{i}"][:, lo:hi].T.reshape(128, 8, 64) for c in cores]) for i in range(2)]).astype(f32)

    pc, sc = list(range(4)), list(range(8))
    return (np.ascontiguousarray(y_p), np.ascontiguousarray(y_s),
            tok("lat_out", 0, SEQ, pc), tok("kr_out", 0, SEQ, pc), st("hp_out", 0, pc), st("hp_out", 1, pc),
            win("k_out", 0, 128, pc), win("v_out", 0, 128, pc),
            tok("lat_out", SEQ, NT, sc), tok("kr_out", SEQ, NT, sc), st("hs_out", 0, sc), st("hs_out", 1, sc),
            win("k_out", 128, 256, sc), win("v_out", 128, 256, sc))
```

```python
import math
from contextlib import ExitStack

import numpy as np
import concourse.bass as bass
import concourse.mybir as mybir
from concourse.bass_utils import run_bass_kernel_spmd

F32 = mybir.dt.float32
BF16 = mybir.dt.bfloat16
AF = mybir.ActivationFunctionType
ALU = mybir.AluOpType

P = 128
D = 4096
SEQ = 2048
TS = 16
NT = SEQ + TS
DEPTH = 4
EPS = 1e-6
TT = [(0, 512), (512, 512), (1024, 512), (1536, 512), (2048, 16)]
GROUPS = [[0, 1], [2, 3, 4]]
AB_IN = 7488
C_IN = 9216
MAGIC = 12582912.0
S5_PIPE = True


class Buf:
    __slots__ = ("name", "t", "wr", "rd", "dsem", "merge")

    def __init__(self, name, t=None, merge=False):
        self.name = name
        self.t = t
        self.wr = {}
        self.rd = {}
        self.dsem = None
        self.merge = merge

    def __getitem__(self, idx):
        return self.t[idx]


class Eng:
    def __init__(self, kb, name, h, compute=True):
        self.kb = kb
        self.name = name
        self.h = h
        self.sem = kb.new_sem("e_" + name) if compute else None
        self.cnt = 0
        self.waited = {}
        self.pend_r = []
        self.pend_w = []

    def wait(self, sem, val):
        k = id(sem)
        if self.waited.get(k, 0) >= val:
            return
        self.h.wait_ge(sem, val)
        self.waited[k] = val


class KB:
    def __init__(self):
        self.nc = bass.Bass("TRN2", target_bir_lowering=False)
        nc = self.nc
        self.stack = ExitStack()
        self.sems = {}
        self.pe = Eng(self, "pe", nc.tensor)
        self.act = Eng(self, "act", nc.scalar)
        self.dve = Eng(self, "dve", nc.vector)
        self.pool = Eng(self, "pool", nc.gpsimd)
        self.sp = Eng(self, "sp", nc.sync, compute=False)
        self.engs = [self.pe, self.act, self.dve, self.pool, self.sp]
        self.dma_sems = [[self.new_sem(f"d{i}"), 0] for i in range(84)]
        self.dma_free = list(range(len(self.dma_sems)))
        self.phase_bufs = []
        self.all_events = {}
        self.psum = [Buf(f"ps{i}", self.stack.enter_context(nc.psum_tensor(f"ps{i}", [P, 512], F32)))
                     for i in range(8)]
        self.ps_i = 0
        self.uid = 0
        self.dram = {}

    def new_sem(self, name):
        s = self.stack.enter_context(self.nc.semaphore(name))
        self.sems[id(s)] = s
        return s

    def next_ps(self):
        b = self.psum[self.ps_i % 8]
        self.ps_i += 1
        return b

    def dram_in(self, name, shape, dtype=F32):
        t = self.nc.dram_tensor(name, list(shape), dtype, kind="ExternalInput")
        self.dram[name] = Buf(name, t, merge=True)
        return self.dram[name]

    def dram_out(self, name, shape, dtype=F32):
        t = self.nc.dram_tensor(name, list(shape), dtype, kind="ExternalOutput")
        self.dram[name] = Buf(name, t, merge=True)
        return self.dram[name]

    def dram_tmp(self, name, shape, dtype):
        t = self.nc.dram_tensor(name, list(shape), dtype, kind="Internal")
        self.dram[name] = Buf(name, t, merge=True)
        return self.dram[name]

    def phase_begin(self):
        self.pstack = ExitStack()
        self.phase_bufs = []

    def sb(self, name, shape, dtype):
        self.uid += 1
        stk = self.scope[0] if getattr(self, "scope", None) else self.pstack
        t = stk.enter_context(self.nc.sbuf_tensor(f"{name}_{self.uid}", list(shape), dtype))
        b = Buf(name, t)
        if getattr(self, "scope", None):
            self.scope[1].append(b)
        else:
            self.phase_bufs.append(b)
        return b

    def push_scope(self):
        self.scope = (ExitStack(), [])

    def pop_scope(self):
        self.barrier()
        stk, bufs = self.scope
        for b in bufs:
            if b.dsem is not None:
                self.dma_free.append(b.dsem)
                b.dsem = None
        stk.close()
        self.scope = None

    def phase_end(self):
        self.barrier()
        for b in self.phase_bufs:
            if b.dsem is not None:
                self.dma_free.append(b.dsem)
                b.dsem = None
        self.pstack.close()
        self.phase_bufs = []

    def barrier(self):
        for e in self.engs:
            assert not e.pend_r and not e.pend_w, f"pending accesses on {e.name} at barrier"
        for e in self.engs:
            for k, (sem, val) in self.all_events.items():
                if e.sem is not None and sem is e.sem and e is self.pe:
                    continue
                e.wait(sem, val)

    def _note(self, sem, val):
        self.all_events[id(sem)] = (sem, val)

    def _waits(self, eng, reads, writes):
        for e2 in self.engs:
            if e2 is eng:
                continue
            if e2.pend_r or e2.pend_w:
                for b in list(reads) + list(writes):
                    if any(b is x for x in e2.pend_w):
                        raise RuntimeError(f"buffer {b.name} has uncommitted write on {e2.name}")
                for b in writes:
                    if any(b is x for x in e2.pend_r):
                        raise RuntimeError(f"buffer {b.name} has uncommitted read on {e2.name}")
        for b in reads:
            for k, (sem, val) in b.wr.items():
                if eng is self.pe and sem is eng.sem:
                    continue
                eng.wait(sem, val)
        for b in writes:
            if not b.merge:
                for k, (sem, val) in b.wr.items():
                    if eng is self.pe and sem is eng.sem:
                        continue
                    eng.wait(sem, val)
            for k, (sem, val) in b.rd.items():
                if eng is self.pe and sem is eng.sem:
                    continue
                eng.wait(sem, val)

    def _commit(self, sem, val, reads, writes):
        for b in reads:
            b.rd[id(sem)] = (sem, val)
        for b in writes:
            if b.merge:
                b.wr[id(sem)] = (sem, val)
            else:
                b.wr = {id(sem): (sem, val)}
                b.rd = {}
        self._note(sem, val)

    def op(self, eng, fn, reads=(), writes=(), inc=True):
        self._waits(eng, reads, writes)
        ins = fn()
        if not inc:
            eng.pend_r.extend(reads)
            eng.pend_w.extend(writes)
            return ins
        eng.cnt += 1
        ins.then_inc(eng.sem, 1)
        self._commit(eng.sem, eng.cnt, list(reads) + eng.pend_r, list(writes) + eng.pend_w)
        eng.pend_r = []
        eng.pend_w = []
        return ins

    def dma(self, q, out_ap, in_ap, reads, writes, sbuf):
        if sbuf.dsem is None:
            sbuf.dsem = self.dma_free.pop()
        ent = self.dma_sems[sbuf.dsem]
        saved = []
        for b in writes:
            if not b.merge and id(ent[0]) in b.wr:
                saved.append((b, b.wr.pop(id(ent[0]))))
        self._waits(q, reads, writes)
        for b, v in saved:
            b.wr[id(ent[0])] = v
        ent[1] += 16
        q.h.dma_start(out=out_ap, in_=in_ap).then_inc(ent[0], 16)
        sem, val = ent
        for b in reads:
            b.rd[id(sem)] = (sem, val)
        for b in writes:
            if not b.merge:
                stale = [k for k in b.wr if k != id(sem)]
                for k in stale:
                    del b.wr[k]
                b.rd = {}
            b.wr[id(sem)] = (sem, val)
        self._note(sem, val)

    def _is_dma_sem(self, k):
        return any(id(e[0]) == k for e in self.dma_sems)


class Rot:
    def __init__(self, bufs):
        self.bufs = bufs
        self.i = 0

    def next(self):
        b = self.bufs[self.i % len(self.bufs)]
        self.i += 1
        return b


def load_consts(kb, C):
    nc = kb.nc
    C.ones = kb.sb("ones_bf", [P, P], BF16)
    kb.op(kb.dve, lambda: nc.vector.memset(C.ones[:], 1.0), writes=[C.ones])
    C.eps = kb.sb("eps_col", [P, 1], F32)
    kb.op(kb.dve, lambda: nc.vector.memset(C.eps[:], EPS), writes=[C.eps])
    C.halfpi = kb.sb("halfpi", [P, 1], F32)
    kb.op(kb.dve, lambda: nc.vector.memset(C.halfpi[:], math.pi / 2), writes=[C.halfpi])
    C.magic = kb.sb("magic", [P, 3], F32)
    kb.op(kb.dve, lambda: nc.vector.memset(C.magic[:, 0:1], MAGIC), writes=[C.magic])
    kb.op(kb.dve, lambda: nc.vector.memset(C.magic[:, 1:2], -MAGIC), writes=[C.magic])
    kb.op(kb.dve, lambda: nc.vector.memset(C.magic[:, 2:3], 1.0), writes=[C.magic])


class NS:
    pass


def rstd_from_ss(kb, C, ss_ps, n, inv_count, out_buf, out_sl, tmp):
    nc = kb.nc
    kb.op(kb.act, lambda: nc.scalar.activation(tmp[:, 0:n], ss_ps[:, 0:n], AF.Sqrt, bias=C.eps[:, 0:1],
                                               scale=inv_count), reads=[ss_ps, C.eps], writes=[tmp])
    kb.op(kb.dve, lambda: nc.vector.reciprocal(out_buf[:, out_sl], tmp[:, 0:n]), reads=[tmp], writes=[out_buf])


def rmsnorm_resident(kb, C, X, KC, gcol, A, pre):
    nc = kb.nc
    xa = X.t.ap().rearrange("(c p) t -> p c t", p=P)
    kb.push_scope()
    SUB = 128
    xt = Rot([kb.sb(f"{pre}_xt{i}", [P, KC, SUB], F32) for i in range(2)])
    sq = Rot([kb.sb(f"{pre}_sq{i}", [P, KC, SUB], BF16) for i in range(2)])
    rstd = Rot([kb.sb(f"{pre}_rstd{i}", [P, SUB], F32) for i in range(2)])
    tmp = kb.sb(f"{pre}_sd", [P, SUB], F32)
    flip = 0
    for t0 in range(0, NT, SUB):
        m = min(SUB, NT - t0)
        x_ = xt.next()
        kb.dma(kb.sp, x_[:, :, 0:m], xa[:, :, t0:t0 + m], reads=[X], writes=[x_], sbuf=x_)
        s_ = sq.next()
        kb.op(kb.act, lambda: nc.scalar.activation(s_[:, :, 0:m], x_[:, :, 0:m], AF.Square), reads=[x_], writes=[s_])
        ss = kb.next_ps()
        for c in range(KC):
            kb.op(kb.pe, lambda: nc.tensor.matmul(ss[:, 0:m], C.ones[:], s_[:, c, 0:m], start=(c == 0), stop=(c == KC - 1)),
                  reads=[C.ones, s_], writes=[ss], inc=(c == KC - 1))
        r_ = rstd.next()
        rstd_from_ss(kb, C, ss, m, 1.0 / (KC * P), r_, slice(0, m), tmp)
        for c in range(KC):
            flip ^= 1
            if flip:
                kb.op(kb.dve, lambda: nc.vector.scalar_tensor_tensor(A[:, c, t0:t0 + m], x_[:, c, 0:m], gcol[:, c:c + 1],
                                                                      r_[:, 0:m], ALU.mult, ALU.mult),
                      reads=[x_, gcol, r_], writes=[A], inc=True)
            else:
                kb.op(kb.dve, lambda: nc.vector.scalar_tensor_tensor(A[:, c, t0:t0 + m], x_[:, c, 0:m], gcol[:, c:c + 1],
                                                                      r_[:, 0:m], ALU.mult, ALU.mult),
                      reads=[x_, gcol, r_], writes=[A], inc=True)
    kb.pop_scope()


def linear(kb, A, KC, W_ap, jobs, epilogue, pre, nslab=3):
    nc = kb.nc
    wv = W_ap.rearrange("(k p) m -> p k m", p=P)
    slabs = Rot([kb.sb(f"{pre}_w{i}", [P, KC, P], BF16) for i in range(nslab)])
    for ji, (m0, msz, tag) in enumerate(jobs):
        sl = slabs.next()
        kb.dma(kb.pool, sl[:, :, 0:msz], wv[:, :, m0:m0 + msz], reads=[], writes=[sl], sbuf=sl)
        for grp in GROUPS:
            pss = {ti: kb.next_ps() for ti in grp}
            for k in range(KC):
                for ti in grp:
                    t0, n = TT[ti]
                    last = (k == KC - 1)
                    kb.op(kb.pe, lambda: nc.tensor.matmul(pss[ti][0:msz, 0:n], sl[:, k, 0:msz], A[:, k, t0:t0 + n],
                                                          start=(k == 0), stop=last),
                          reads=[sl, A], writes=[pss[ti]], inc=last)
            for ti in grp:
                epilogue(tag, ji, ti, pss[ti], msz)


def ab_jobs():
    jobs = []
    for i in range(6):
        jobs.append((i * 128, 128, ("cq", i)))
    for i in range(4):
        jobs.append((768 + i * 128, 128, ("ckv", i)))
    jobs.append((1280, 64, ("kr", 0)))
    for i in range(16):
        jobs.append((1344 + i * 128, 128, ("ga", i)))
    for i in range(16):
        jobs.append((3392 + i * 128, 128, ("u", i)))
    for i in range(16):
        jobs.append((5440 + i * 128, 128, ("gb", i)))
    return jobs


def ab_phase1(kb, G, li, X):
    nc = kb.nc
    kb.phase_begin()
    C = NS()
    load_consts(kb, C)
    gcol = kb.sb("gcol", [P, 32], F32)
    kb.dma(kb.sp, gcol[:], G.ab_norm.t.ap()[li], reads=[G.ab_norm], writes=[gcol], sbuf=gcol)
    A = kb.sb("A", [P, 32, NT], BF16)
    rmsnorm_resident(kb, C, X, 32, gcol, A, "n1")
    st32 = Rot([kb.sb(f"st32_{i}", [P, 512], F32) for i in range(3)])
    st16 = Rot([kb.sb(f"st16_{i}", [P, 512], BF16) for i in range(3)])
    flip = [0]

    def epi(tag, ji, ti, ps, msz):
        t0, n = TT[ti]
        kind, i = tag
        if kind in ("cq", "ckv", "kr"):
            dst = {"cq": G.CQ, "ckv": G.CKV, "kr": G.KRAW}[kind]
            s = st32.next()
            flip[0] ^= 1
            if flip[0]:
                kb.op(kb.dve, lambda: nc.vector.tensor_copy(s[0:msz, 0:n], ps[0:msz, 0:n]), reads=[ps], writes=[s])
            else:
                kb.op(kb.act, lambda: nc.scalar.copy(s[0:msz, 0:n], ps[0:msz, 0:n]), reads=[ps], writes=[s])
            kb.dma(kb.sp, dst.t.ap()[i * 128:i * 128 + msz, t0:t0 + n], s[0:msz, 0:n], reads=[s], writes=[dst], sbuf=s)
        else:
            dst = {"ga": G.GA, "u": G.U, "gb": G.GB}[kind]
            s = st16.next()
            if kind == "u":
                kb.op(kb.dve, lambda: nc.vector.tensor_copy(s[:, 0:n], ps[:, 0:n]), reads=[ps], writes=[s])
            else:
                kb.op(kb.act, lambda: nc.scalar.activation(s[:, 0:n], ps[:, 0:n], AF.Silu), reads=[ps], writes=[s])
            kb.dma(kb.sp, dst.t.ap()[i * 128:(i + 1) * 128, t0:t0 + n], s[:, 0:n], reads=[s], writes=[dst], sbuf=s)

    linear(kb, A, 32, G.ab_w_in.t.ap()[li], ab_jobs(), epi, "l1")
    kb.phase_end()

    kb.phase_begin()
    C = NS()
    load_consts(kb, C)
    wuq = kb.sb("wuq", [P, 6, 3072], BF16)
    wv = G.ab_w_uq.t.ap()[li].rearrange("(k p) m -> p k m", p=P)
    for k in range(6):
        kb.dma(kb.pool, wuq[:, k, :], wv[:, k, :], reads=[], writes=[wuq], sbuf=wuq)
    glq = kb.sb("glq", [P, 6], F32)
    kb.dma(kb.sp, glq[:], G.ab_q_lora_norm.t.ap()[li], reads=[], writes=[glq], sbuf=glq)
    glkv = kb.sb("glkv", [P, 4], F32)
    kb.dma(kb.sp, glkv[:], G.ab_kv_lora_norm.t.ap()[li], reads=[], writes=[glkv], sbuf=glkv)
    gq = kb.sb("gq", [P, 2], F32)
    kb.dma(kb.sp, gq[:], G.ab_q_norm.t.ap()[li], reads=[], writes=[gq], sbuf=gq)
    cs = kb.sb("ropecs", [64, NT], F32)
    sn = kb.sb("ropesn", [64, NT], F32)
    kb.dma(kb.sp, cs[:], G.rope_cos.t.ap(), reads=[], writes=[cs], sbuf=cs)
    kb.dma(kb.sp, sn[:], G.rope_sin.t.ap(), reads=[], writes=[sn], sbuf=sn)

    cq = Rot([kb.sb(f"cq{i}", [P, 6, 512], F32) for i in range(2)])
    cqsq = kb.sb("cqsq", [P, 6, 512], BF16)
    cqn = Rot([kb.sb(f"cqn{i}", [P, 6, 512], BF16) for i in range(2)])
    ckv = Rot([kb.sb(f"ckv{i}", [P, 4, 512], F32) for i in range(2)])
    ckvsq = kb.sb("ckvsq", [P, 4, 512], BF16)
    latn = Rot([kb.sb(f"latn{i}", [P, 4, 512], F32) for i in range(2)])
    kraw = Rot([kb.sb(f"kraw{i}", [64, 512], F32) for i in range(2)])
    krt = kb.sb("krt", [64, 512], F32)
    kro = Rot([kb.sb(f"kro{i}", [64, 512], F32) for i in range(2)])
    rsq = kb.sb("rsq", [P, 512], F32)
    rskv = kb.sb("rskv", [P, 512], F32)
    tmp = kb.sb("tmp", [P, 512], F32)
    sqa = Rot([kb.sb(f"sqa{i}", [P, 512], BF16) for i in range(2)])
    sqb = Rot([kb.sb(f"sqb{i}", [64, 512], BF16) for i in range(2)])
    rsh = Rot([kb.sb(f"rsh{i}", [P, 512], F32) for i in range(2)])
    rt = Rot([kb.sb(f"rt{i}", [64, 512], F32) for i in range(2)])
    rr = Rot([kb.sb(f"rr{i}", [64, 512], F32) for i in range(2)])
    qn = Rot([kb.sb(f"qn{i}", [P, 512], BF16) for i in range(3)])
    qr = Rot([kb.sb(f"qr{i}", [64, 512], BF16) for i in range(3)])

    def rope(src, sl_src, dst_t, dst, n, t0, src_bufs):
        kb.op(kb.dve, lambda: nc.vector.tensor_tensor(dst_t[0:32, 0:n], src[32:64, sl_src], sn[32:64, t0:t0 + n], ALU.mult),
              reads=src_bufs + [sn], writes=[dst_t])
        kb.op(kb.dve, lambda: nc.vector.tensor_tensor(dst_t[32:64, 0:n], src[0:32, sl_src], sn[0:32, t0:t0 + n], ALU.mult),
              reads=src_bufs + [sn], writes=[dst_t])
        kb.op(kb.dve, lambda: nc.vector.tensor_tensor(dst[0:64, 0:n], src[0:64, sl_src], cs[0:64, t0:t0 + n], ALU.mult),
              reads=src_bufs + [cs], writes=[dst])
        kb.op(kb.dve, lambda: nc.vector.tensor_tensor(dst[0:64, 0:n], dst[0:64, 0:n], dst_t[0:64, 0:n], ALU.add),
              reads=[dst, dst_t], writes=[dst])

    cqa = G.CQ.t.ap().rearrange("(c p) t -> p c t", p=P)
    ckva = G.CKV.t.ap().rearrange("(c p) t -> p c t", p=P)
    lata = G.lat_out[li].t.ap().rearrange("(c p) t -> p c t", p=P)
    for (t0, n) in TT:
        kv = ckv.next()
        kb.dma(kb.sp, kv[:, :, 0:n], ckva[:, :, t0:t0 + n], reads=[G.CKV], writes=[kv], sbuf=kv)
        kb.op(kb.act, lambda: nc.scalar.activation(ckvsq[:, :, 0:n], kv[:, :, 0:n], AF.Square), reads=[kv], writes=[ckvsq])
        ss = kb.next_ps()
        for c in range(4):
            kb.op(kb.pe, lambda: nc.tensor.matmul(ss[:, 0:n], C.ones[:], ckvsq[:, c, 0:n], start=(c == 0), stop=(c == 3)),
                  reads=[C.ones, ckvsq], writes=[ss], inc=(c == 3))
        rstd_from_ss(kb, C, ss, n, 1.0 / 512, rskv, slice(0, n), tmp)
        ln = latn.next()
        for c in range(4):
            kb.op(kb.dve, lambda: nc.vector.scalar_tensor_tensor(ln[:, c, 0:n], kv[:, c, 0:n], glkv[:, c:c + 1],
                                                                  rskv[:, 0:n], ALU.mult, ALU.mult),
                  reads=[kv, glkv, rskv], writes=[ln])
        kb.dma(kb.sp, lata[:, :, t0:t0 + n], ln[:, :, 0:n], reads=[ln], writes=[G.lat_out[li]], sbuf=ln)
        kr = kraw.next()
        kb.dma(kb.sp, kr[:, 0:n], G.KRAW.t.ap()[:, t0:t0 + n], reads=[G.KRAW], writes=[kr], sbuf=kr)
        ko = kro.next()
        rope(kr, slice(0, n), krt, ko, n, t0, [kr])
        kb.dma(kb.sp, G.kr_out[li].t.ap()[:, t0:t0 + n], ko[:, 0:n], reads=[ko], writes=[G.kr_out[li]], sbuf=ko)
        q_ = cq.next()
        kb.dma(kb.sp, q_[:, :, 0:n], cqa[:, :, t0:t0 + n], reads=[G.CQ], writes=[q_], sbuf=q_)
        kb.op(kb.act, lambda: nc.scalar.activation(cqsq[:, :, 0:n], q_[:, :, 0:n], AF.Square), reads=[q_], writes=[cqsq])
        ss = kb.next_ps()
        for c in range(6):
            kb.op(kb.pe, lambda: nc.tensor.matmul(ss[:, 0:n], C.ones[:], cqsq[:, c, 0:n], start=(c == 0), stop=(c == 5)),
                  reads=[C.ones, cqsq], writes=[ss], inc=(c == 5))
        rstd_from_ss(kb, C, ss, n, 1.0 / 768, rsq, slice(0, n), tmp)
        qn_ = cqn.next()
        for c in range(6):
            kb.op(kb.dve, lambda: nc.vector.scalar_tensor_tensor(qn_[:, c, 0:n], q_[:, c, 0:n], glq[:, c:c + 1],
                                                                  rsq[:, 0:n], ALU.mult, ALU.mult),
                  reads=[q_, glq, rsq], writes=[qn_])
        for h in range(16):
            psA = kb.next_ps()
            psB = kb.next_ps()
            for k in range(6):
                kb.op(kb.pe, lambda: nc.tensor.matmul(psA[:, 0:n], wuq[:, k, h * 192:h * 192 + 128], qn_[:, k, 0:n],
                                                      start=(k == 0), stop=(k == 5)),
                      reads=[wuq, qn_], writes=[psA], inc=(k == 5))
            for k in range(6):
                kb.op(kb.pe, lambda: nc.tensor.matmul(psB[0:64, 0:n], wuq[:, k, h * 192 + 128:h * 192 + 192], qn_[:, k, 0:n],
                                                      start=(k == 0), stop=(k == 5)),
                      reads=[wuq, qn_], writes=[psB], inc=(k == 5))
            a2 = sqa.next()
            b2 = sqb.next()
            kb.op(kb.act, lambda: nc.scalar.activation(a2[:, 0:n], psA[:, 0:n], AF.Square), reads=[psA], writes=[a2])
            kb.op(kb.act, lambda: nc.scalar.activation(b2[0:64, 0:n], psB[0:64, 0:n], AF.Square), reads=[psB], writes=[b2])
            ss = kb.next_ps()
            kb.op(kb.pe, lambda: nc.tensor.matmul(ss[:, 0:n], C.ones[:], a2[:, 0:n], start=True, stop=False),
                  reads=[C.ones, a2], writes=[ss], inc=False)
            kb.op(kb.pe, lambda: nc.tensor.matmul(ss[:, 0:n], C.ones[0:64, :], b2[0:64, 0:n], start=False, stop=True),
                  reads=[C.ones, b2], writes=[ss], inc=True)
            rh = rsh.next()
            rstd_from_ss(kb, C, ss, n, 1.0 / 192, rh, slice(0, n), tmp)
            t_ = rt.next()
            r_ = rr.next()
            rope(psB, slice(0, n), t_, r_, n, t0, [psB])
            o1 = qn.next()
            kb.op(kb.dve, lambda: nc.vector.scalar_tensor_tensor(o1[:, 0:n], psA[:, 0:n], gq[:, 0:1], rh[:, 0:n],
                                                                  ALU.mult, ALU.mult),
                  reads=[psA, gq, rh], writes=[o1])
            kb.dma(kb.sp, G.QT.t.ap()[h * 192:h * 192 + 128, t0:t0 + n], o1[:, 0:n], reads=[o1], writes=[G.QT], sbuf=o1)
            o2 = qr.next()
            kb.op(kb.dve, lambda: nc.vector.scalar_tensor_tensor(o2[0:64, 0:n], r_[0:64, 0:n], gq[0:64, 1:2], rh[0:64, 0:n],
                                                                  ALU.mult, ALU.mult),
                  reads=[r_, gq, rh], writes=[o2])
            kb.dma(kb.sp, G.QT.t.ap()[h * 192 + 128:h * 192 + 192, t0:t0 + n], o2[0:64, 0:n], reads=[o2], writes=[G.QT],
                   sbuf=o2)
    kb.phase_end()


def declare(kb):
    G = NS()
    G.x0 = kb.dram_in("x0", [D, NT])
    G.ab_norm = kb.dram_in("ab_norm", [2, P, 32])
    G.ab_w_in = kb.dram_in("ab_w_in", [2, D, AB_IN])
    G.ab_q_lora_norm = kb.dram_in("ab_q_lora_norm", [2, P, 6])
    G.ab_kv_lora_norm = kb.dram_in("ab_kv_lora_norm", [2, P, 4])
    G.ab_w_uq = kb.dram_in("ab_w_uq", [2, 768, 3072])
    G.ab_q_norm = kb.dram_in("ab_q_norm", [2, P, 2])
    G.rope_cos = kb.dram_in("rope_cos", [64, NT])
    G.rope_sin = kb.dram_in("rope_sin", [64, NT])
    G.lat_out = [kb.dram_out(f"lat_out{i}", [512, NT]) for i in range(2)]
    G.kr_out = [kb.dram_out(f"kr_out{i}", [64, NT]) for i in range(2)]
    G.CQ = kb.dram_tmp("CQ", [768, NT], F32)
    G.CKV = kb.dram_tmp("CKV", [512, NT], F32)
    G.KRAW = kb.dram_tmp("KRAW", [64, NT], F32)
    G.GA = kb.dram_tmp("GA", [2048, NT], BF16)
    G.U = kb.dram_tmp("U", [2048, NT], BF16)
    G.GB = kb.dram_tmp("GB", [2048, NT], BF16)
    G.QT = kb.dram_tmp("QT", [3072, NT], BF16)
    G.s5B = kb.dram_in("s5B", [2, 5, P, 4096])
    G.s5S = kb.dram_in("s5S", [2, 3, P, 64])
    G.s5C = kb.dram_in("s5C", [2, 2, P, 4096])
    G.ssm_d = kb.dram_in("ssm_d", [2, P, 16])
    G.iota512 = kb.dram_in("iota512", [P, 512])
    G.h0 = kb.dram_in("h0", [2, 2, P, 64])
    G.SPAR = kb.dram_tmp("SPAR", [6, P, 64], F32)
    G.BWRE = kb.dram_tmp("BWRE", [P, 4096], BF16)
    G.BWIM = kb.dram_tmp("BWIM", [P, 4096], BF16)
    G.CW = kb.dram_tmp("CW", [3, P, 4096], BF16)
    G.YG = kb.dram_tmp("YG", [2048, NT], BF16)
    G.hp_out = [kb.dram_out(f"hp_out{i}", [2, P, 64]) for i in range(2)]
    G.hs_out = [kb.dram_out(f"hs_out{i}", [2, P, 64]) for i in range(2)]
    G.ab_k_norm = kb.dram_in("ab_k_norm", [2, P, 2])
    G.latc = kb.dram_in("latc", [2, 512, SEQ])
    G.krc = kb.dram_in("krc", [2, 64, SEQ])
    G.ab_w_ukv = kb.dram_in("ab_w_ukv", [2, 512, 4096])
    G.ssm_w_glu = kb.dram_in("ssm_w_glu", [2, 2048, 4096])
    G.ab_w_out = kb.dram_in("ab_w_out", [2, D, D])
    G.AO = kb.dram_tmp("AO", [D, NT], BF16)
    G.KNS = kb.dram_tmp("KNS", [16, P, NKA], BF16)
    G.KRS = kb.dram_tmp("KRS", [16, 64, NKA], BF16)
    G.VSC = kb.dram_tmp("VSC", [16, P, 33, P], BF16)
    G.XA = kb.dram_tmp("XA", [D, NT], F32)
    G.XB = kb.dram_tmp("XB", [D, NT], F32)
    G.y_out = kb.dram_out("y_out", [D, NT])
    G.c_norm = kb.dram_in("c_norm", [2, P, 32])
    G.c_qk_norm = kb.dram_in("c_qk_norm", [2, P, 2])
    G.c_w_in = kb.dram_in("c_w_in", [2, D, C_IN])
    G.c_w_out = kb.dram_in("c_w_out", [2, D, D])
    G.kcache = kb.dram_in("kcache", [2, 512, 128])
    G.vcacheT = kb.dram_in("vcacheT", [2, 512, 128])
    G.vcache = kb.dram_in("vcache", [2, 128, 512])
    G.QS = kb.dram_tmp("QS", [D, NT], BF16)
    G.KS = kb.dram_tmp("KS", [512, NT], F32)
    G.VS = kb.dram_tmp("VS", [512, NT], F32)
    G.GC = kb.dram_tmp("GC", [D, NT], BF16)
    G.k_out = [kb.dram_out(f"k_out{i}", [512, 256]) for i in range(2)]
    G.v_out = [kb.dram_out(f"v_out{i}", [512, 256]) for i in range(2)]
    G.rel_table = kb.dram_in("rel_table", [32, 64])
    G.onehot = kb.dram_in("onehot", [32, LB])
    G.ident = kb.dram_in("ident", [P, P])
    G.sinks = kb.dram_in("sinks", [2, P, 64])
    G.GD = kb.dram_tmp("GD", [64, LB], F32)
    G.GT = kb.dram_tmp("GT", [64, NCOPY, LB], F32)
    G.EBP = kb.dram_tmp("EBP", [64, P, 256], F32)
    G.EBS = kb.dram_tmp("EBS", [64, P, 32], F32)
    return G


def build(parts=None, depth=DEPTH):
    kb = KB()
    G = declare(kb)
    if parts is not None:
        if "p1" in parts:
            ab_phase1(kb, G, 0, G.x0)
        if "s5s" in parts:
            s5_setup(kb, G, 0)
        if "s5m" in parts:
            s5_main(kb, G, 0)
        if "mla" in parts:
            mla_phase(kb, G, 0)
        if "glu" in parts:
            glu_phase(kb, G, 0)
        if "out" in parts:
            out_proj_phase(kb, G, G.ab_w_out.t.ap()[0], G.x0, G.XA)
        if "c1" in parts:
            c_phase1(kb, G, 0, G.XA)
        if "bias" in parts:
            bias_setup(kb, G)
        if "c2" in parts:
            c_phase2(kb, G, 0)
        if "cout" in parts:
            out_proj_phase(kb, G, G.c_w_out.t.ap()[0], G.XA, G.XB)
        kb.barrier()
        return kb
    bias_setup(kb, G)
    chain = [G.x0, G.XA, G.XB, G.XA, G.y_out]
    for layer in range(depth):
        li = layer // 2
        Xin, Xout = chain[layer], chain[layer + 1]
        if layer == depth - 1:
            Xout = G.y_out
        if layer % 2 == 0:
            ab_phase1(kb, G, li, Xin)
            mla_phase(kb, G, li)
            s5_setup(kb, G, li)
            s5_main(kb, G, li)
            glu_phase(kb, G, li)
            out_proj_phase(kb, G, G.ab_w_out.t.ap()[li], Xin, Xout)
        else:
            c_phase1(kb, G, li, Xin)
            c_phase2(kb, G, li)
            out_proj_phase(kb, G, G.c_w_out.t.ap()[li], Xin, Xout)
    kb.barrier()
    return kb


def colmajor(v, nchunk):
    return np.ascontiguousarray(v.reshape(nchunk, P).T)


def rope_tables():
    half = 32
    inv = 10000.0 ** (-np.arange(half, dtype=np.float32) / half)
    ang = np.arange(NT, dtype=np.float32)[None, :] * inv[:, None].astype(np.float32)
    ang = ang.astype(np.float32)
    cos = np.cos(ang).astype(np.float32)
    sin = np.sin(ang).astype(np.float32)
    cs = np.concatenate([cos, cos], 0)
    sn = np.concatenate([sin, -sin], 0)
    return np.ascontiguousarray(cs), np.ascontiguousarray(sn)


def prep_core(inp, c, shared):
    pb = c % 4
    m = dict(shared)
    m["x0"] = np.ascontiguousarray(np.concatenate([inp["x_prompt"][pb].T, inp["x_sample"][c].T], axis=1))
    m["latc"] = np.ascontiguousarray(inp["cache_mla_latent"][:, c].transpose(0, 2, 1))
    m["krc"] = np.ascontiguousarray(inp["cache_mla_krope"][:, c].transpose(0, 2, 1))
    kc = inp["cache_swa_k"][:, c].reshape(2, 128, 512)
    vc = inp["cache_swa_v"][:, c].reshape(2, 128, 512)
    m["kcache"] = np.ascontiguousarray(kc.transpose(0, 2, 1))
    m["vcacheT"] = np.ascontiguousarray(vc.transpose(0, 2, 1))
    m["vcache"] = np.ascontiguousarray(vc)
    m["h0"] = np.stack([np.stack([s_layout(inp["state_ssm_re"][i, c]), s_layout(inp["state_ssm_im"][i, c])]) for i in range(2)])
    return m


def prep_shared(inp):
    m = {}
    m["ab_norm"] = np.stack([colmajor(inp["ab_norm"][i], 32) for i in range(2)])
    m["ab_w_in"] = inp["ab_w_in"]
    m["ab_q_lora_norm"] = np.stack([colmajor(inp["ab_q_lora_norm"][i], 6) for i in range(2)])
    m["ab_kv_lora_norm"] = np.stack([colmajor(inp["ab_kv_lora_norm"][i], 4) for i in range(2)])
    m["ab_w_uq"] = inp["ab_w_uq"]
    gq = np.zeros((2, P, 2), np.float32)
    gq[:, :, 0] = inp["ab_q_norm"][:, 0:128]
    gq[:, 0:64, 1] = inp["ab_q_norm"][:, 128:192]
    m["ab_q_norm"] = gq
    m["rope_cos"], m["rope_sin"] = rope_tables()
    m.update(s5_layouts(inp))
    gk = np.zeros((2, P, 2), np.float32)
    gk[:, :, 0] = inp["ab_k_norm"][:, 0:128]
    gk[:, 0:64, 1] = inp["ab_k_norm"][:, 128:192]
    m["ab_k_norm"] = gk
    for nm in ("ab_w_ukv", "ssm_w_glu", "ab_w_out", "c_w_in", "c_w_out"):
        m[nm] = inp[nm]
    m["c_norm"] = np.stack([colmajor(inp["c_norm"][i], 32) for i in range(2)])
    qk = np.zeros((2, P, 2), np.float32)
    qk[:, :, 0] = np.tile(inp["c_q_norm"], (1, 2))
    qk[:, :, 1] = np.tile(inp["c_k_norm"], (1, 2))
    m["c_qk_norm"] = qk
    m["rel_table"] = inp["rel_bias_table"]
    rel = 127 - np.arange(LB)
    bk = rel_bucket_np(rel)
    m["onehot"] = np.ascontiguousarray((bk[None, :] == np.arange(32)[:, None]).astype(np.float32))
    m["ident"] = np.eye(P, dtype=np.float32)
    m["sinks"] = np.ascontiguousarray(np.broadcast_to(inp["c_sinks"][:, None, :], (2, P, 64)))
    m["ssm_d"] = np.stack([colmajor(inp["ssm_d"][i], 16) for i in range(2)])
    m["iota512"] = np.ascontiguousarray(np.broadcast_to(np.arange(512, dtype=np.float32), (P, 512)))
    return m


def s_layout(a):
    return np.ascontiguousarray(a.reshape(64, 2, 64).transpose(1, 2, 0).reshape(P, 64))


def s_unlayout(a):
    return np.ascontiguousarray(a.reshape(2, 64, 64).transpose(2, 0, 1).reshape(P, 64))


def s5_layouts(inp):
    B = np.zeros((2, 5, P, 32, P), np.float32)
    S = np.zeros((2, 3, P, 64), np.float32)
    Cc = np.zeros((2, 2, P, 64, 64), np.float32)
    for i in range(2):
        lre, lim, ldt = inp["ssm_lambda_re"][i], inp["ssm_lambda_im"][i], inp["ssm_log_dt"][i]
        bre, bim = inp["ssm_b_re"][i], inp["ssm_b_im"][i]
        cre, cim = inp["ssm_c_re"][i], inp["ssm_c_im"][i]
        ldtb = np.broadcast_to(ldt[:, None], (128, 64))
        S[i, 0], S[i, 1], S[i, 2] = s_layout(lre), s_layout(lim), s_layout(ldtb)
        for hf in range(2):
            rows = slice(64 * hf, 64 * hf + 64)
            for j in range(16):
                for s_ in range(2):
                    js = 2 * j + s_
                    gb = 8 * j + 2 * (2 * hf + s_)
                    for g1 in range(2):
                        cols = slice(64 * g1, 64 * g1 + 64)
                        B[i, 0, rows, js, cols] = lre[gb + g1][None, :]
                        B[i, 1, rows, js, cols] = lim[gb + g1][None, :]
                        B[i, 2, rows, js, cols] = ldt[gb + g1]
                        r0 = 64 * hf + 32 * s_ + 16 * g1
                        B[i, 3, r0:r0 + 16, js, cols] = bre[gb + g1].T
                        B[i, 4, r0:r0 + 16, js, cols] = bim[gb + g1].T
        for pi in range(64):
            s_ = pi % 2
            for g1 in range(2):
                g = 2 * pi + g1
                c0 = 32 * s_ + 16 * g1
                Cc[i, 0, 64 * g1:64 * g1 + 64, pi, c0:c0 + 16] = cre[g].T
                Cc[i, 1, 64 * g1:64 * g1 + 64, pi, c0:c0 + 16] = cim[g].T
    return {"s5B": B.reshape(2, 5, P, 4096), "s5S": S, "s5C": Cc.reshape(2, 2, P, 4096)}


def cis_tables(kb, C, x, n, COS, SIN, W, rows=P, xscale=None, xscale_buf=None):
    nc = kb.nc
    t1, k_, fr = W
    r = slice(0, rows)
    xr = [xscale_buf] if xscale_buf is not None else []
    sc = xscale if xscale is not None else 1.0
    kb.op(kb.act, lambda: nc.scalar.activation(t1[r, 0:n], x[r, 0:n], AF.Identity, bias=C.magic[r, 0:1], scale=sc),
          reads=[x, C.magic] + xr, writes=[t1])
    kb.op(kb.act, lambda: nc.scalar.activation(k_[r, 0:n], t1[r, 0:n], AF.Identity, bias=C.magic[r, 1:2], scale=1.0),
          reads=[t1, C.magic], writes=[k_])
    if xscale is not None:
        kb.op(kb.dve, lambda: nc.vector.scalar_tensor_tensor(fr[r, 0:n], x[r, 0:n], xscale, k_[r, 0:n], ALU.mult, ALU.subtract),
              reads=[x, k_] + xr, writes=[fr])
    else:
        kb.op(kb.dve, lambda: nc.vector.tensor_tensor(fr[r, 0:n], x[r, 0:n], k_[r, 0:n], ALU.subtract), reads=[x, k_], writes=[fr])
    kb.op(kb.act, lambda: nc.scalar.activation(t1[r, 0:n], fr[r, 0:n], AF.Sin, scale=math.pi), reads=[fr], writes=[t1])
    kb.op(kb.act, lambda: nc.scalar.activation(k_[r, 0:n], fr[r, 0:n], AF.Sin, bias=C.halfpi[r, 0:1], scale=math.pi),
          reads=[fr, C.halfpi], writes=[k_])
    kb.op(kb.act, lambda: nc.scalar.activation(fr[r, 0:n], t1[r, 0:n], AF.Square, scale=math.sqrt(2.0)), reads=[t1], writes=[fr])
    kb.op(kb.dve, lambda: nc.vector.scalar_tensor_tensor(SIN[r, 0:n], t1[r, 0:n], 2.0, k_[r, 0:n], ALU.mult, ALU.mult),
          reads=[t1, k_], writes=[SIN])
    kb.op(kb.dve, lambda: nc.vector.tensor_scalar(COS[r, 0:n], fr[r, 0:n], -1.0, 1.0, ALU.mult, ALU.add), reads=[fr], writes=[COS])


def s5_setup(kb, G, li):
    nc = kb.nc
    INV2PI = 1.0 / (2.0 * math.pi)
    for which in ("B", "S"):
        kb.phase_begin()
        C = NS()
        load_consts(kb, C)
        N = 4096 if which == "B" else 64
        pre = "sb" if which == "B" else "ss"
        src = G.s5B if which == "B" else G.s5S
        T = {nm: kb.sb(f"{pre}_{nm}", [P, N], F32) for nm in
             ("lre", "lim", "ldt", "mag", "x", "cos", "sin", "w0", "w1", "w2", "den")}

        def ld(dst, idx):
            kb.dma(kb.sp, dst[:], src.t.ap()[li, idx], reads=[src], writes=[dst], sbuf=dst)
        ld(T["lre"], 0)
        ld(T["lim"], 1)
        ld(T["ldt"], 2)
        dt = T["ldt"]
        kb.op(kb.act, lambda: nc.scalar.activation(dt[:], dt[:], AF.Exp), reads=[dt], writes=[dt])
        lr = T["lre"]
        kb.op(kb.dve, lambda: nc.vector.tensor_scalar(lr[:], lr[:], -1e-4, None, ALU.min), reads=[lr], writes=[lr])
        mag = T["mag"]
        kb.op(kb.dve, lambda: nc.vector.tensor_tensor(mag[:], lr[:], dt[:], ALU.mult), reads=[lr, dt], writes=[mag])
        kb.op(kb.act, lambda: nc.scalar.activation(mag[:], mag[:], AF.Exp), reads=[mag], writes=[mag])
        x = T["x"]
        kb.op(kb.dve, lambda: nc.vector.scalar_tensor_tensor(x[:], T["lim"][:], INV2PI, dt[:], ALU.mult, ALU.mult),
              reads=[T["lim"], dt], writes=[x])
        cis_tables(kb, C, x, N, T["cos"], T["sin"], (T["w0"], T["w1"], T["w2"]))
        if which == "S":
            def st(srcb, idx):
                kb.dma(kb.sp, G.SPAR.t.ap()[idx], srcb[:], reads=[srcb], writes=[G.SPAR], sbuf=srcb)
            st(mag, 0)
            st(x, 1)
            st(T["cos"], 2)
            st(T["sin"], 3)
            x5 = T["den"]
            kb.op(kb.dve, lambda: nc.vector.tensor_scalar(x5[:], x[:], 512.0, None, ALU.mult), reads=[x], writes=[x5])
            c5 = kb.sb("ss_c5", [P, N], F32)
            s5 = kb.sb("ss_s5", [P, N], F32)
            w3 = [kb.sb(f"ss_w3{i}", [P, N], F32) for i in range(3)]
            cis_tables(kb, C, x5, N, c5, s5, w3)
            st(c5, 4)
            st(s5, 5)
            kb.phase_end()
            continue
        lbre, lbim = T["cos"], T["sin"]
        kb.op(kb.dve, lambda: nc.vector.tensor_tensor(lbre[:], lbre[:], mag[:], ALU.mult), reads=[lbre, mag], writes=[lbre])
        kb.op(kb.dve, lambda: nc.vector.tensor_tensor(lbim[:], lbim[:], mag[:], ALU.mult), reads=[lbim, mag], writes=[lbim])
        kb.op(kb.dve, lambda: nc.vector.tensor_scalar(lbre[:], lbre[:], -1.0, None, ALU.add), reads=[lbre], writes=[lbre])
        lim = T["lim"]
        den, w0, w1, w2 = T["den"], T["w0"], T["w1"], T["w2"]
        kb.op(kb.dve, lambda: nc.vector.tensor_tensor(den[:], lr[:], lr[:], ALU.mult), reads=[lr], writes=[den])
        kb.op(kb.dve, lambda: nc.vector.tensor_tensor(w0[:], lim[:], lim[:], ALU.mult), reads=[lim], writes=[w0])
        kb.op(kb.dve, lambda: nc.vector.tensor_tensor(den[:], den[:], w0[:], ALU.add), reads=[den, w0], writes=[den])
        kb.op(kb.dve, lambda: nc.vector.reciprocal(den[:], den[:]), reads=[den], writes=[den])
        kb.op(kb.dve, lambda: nc.vector.tensor_tensor(w0[:], lbre[:], lr[:], ALU.mult), reads=[lbre, lr], writes=[w0])
        kb.op(kb.dve, lambda: nc.vector.tensor_tensor(w2[:], lbim[:], lim[:], ALU.mult), reads=[lbim, lim], writes=[w2])
        kb.op(kb.dve, lambda: nc.vector.tensor_tensor(w0[:], w0[:], w2[:], ALU.add), reads=[w0, w2], writes=[w0])
        kb.op(kb.dve, lambda: nc.vector.tensor_tensor(w0[:], w0[:], den[:], ALU.mult), reads=[w0, den], writes=[w0])
        kb.op(kb.dve, lambda: nc.vector.tensor_tensor(w1[:], lbim[:], lr[:], ALU.mult), reads=[lbim, lr], writes=[w1])
        kb.op(kb.dve, lambda: nc.vector.tensor_tensor(w2[:], lbre[:], lim[:], ALU.mult), reads=[lbre, lim], writes=[w2])
        kb.op(kb.dve, lambda: nc.vector.tensor_tensor(w1[:], w1[:], w2[:], ALU.subtract), reads=[w1, w2], writes=[w1])
        kb.op(kb.dve, lambda: nc.vector.tensor_tensor(w1[:], w1[:], den[:], ALU.mult), reads=[w1, den], writes=[w1])
        bre, bim = T["lre"], T["ldt"]
        ld(bre, 3)
        ld(bim, 4)
        o_re = kb.sb("sb_ore", [P, N], BF16)
        o_im = kb.sb("sb_oim", [P, N], BF16)
        kb.op(kb.dve, lambda: nc.vector.tensor_tensor(w2[:], w0[:], bre[:], ALU.mult), reads=[w0, bre], writes=[w2])
        kb.op(kb.dve, lambda: nc.vector.tensor_tensor(den[:], w1[:], bim[:], ALU.mult), reads=[w1, bim], writes=[den])
        kb.op(kb.dve, lambda: nc.vector.tensor_tensor(o_re[:], w2[:], den[:], ALU.subtract), reads=[w2, den], writes=[o_re])
        kb.op(kb.dve, lambda: nc.vector.tensor_tensor(w2[:], w0[:], bim[:], ALU.mult), reads=[w0, bim], writes=[w2])
        kb.op(kb.dve, lambda: nc.vector.tensor_tensor(den[:], w1[:], bre[:], ALU.mult), reads=[w1, bre], writes=[den])
        kb.op(kb.dve, lambda: nc.vector.tensor_tensor(o_im[:], w2[:], den[:], ALU.add), reads=[w2, den], writes=[o_im])
        kb.dma(kb.sp, G.BWRE.t.ap(), o_re[:], reads=[o_re], writes=[G.BWRE], sbuf=o_re)
        kb.dma(kb.sp, G.BWIM.t.ap(), o_im[:], reads=[o_im], writes=[G.BWIM], sbuf=o_im)
        kb.phase_end()
    kb.phase_begin()
    cre = kb.sb("sc_cre", [P, 4096], F32)
    cim = kb.sb("sc_cim", [P, 4096], F32)
    kb.dma(kb.sp, cre[:], G.s5C.t.ap()[li, 0], reads=[G.s5C], writes=[cre], sbuf=cre)
    kb.dma(kb.sp, cim[:], G.s5C.t.ap()[li, 1], reads=[G.s5C], writes=[cim], sbuf=cim)
    o = [kb.sb(f"sc_o{i}", [P, 4096], BF16) for i in range(3)]
    kb.op(kb.dve, lambda: nc.vector.tensor_copy(o[0][:], cre[:]), reads=[cre], writes=[o[0]])
    kb.op(kb.dve, lambda: nc.vector.tensor_scalar(o[1][:], cre[:], -1.0, None, ALU.mult), reads=[cre], writes=[o[1]])
    kb.op(kb.dve, lambda: nc.vector.tensor_scalar(o[2][:], cim[:], -1.0, None, ALU.mult), reads=[cim], writes=[o[2]])
    for i in range(3):
        kb.dma(kb.sp, G.CW.t.ap()[i], o[i][:], reads=[o[i]], writes=[G.CW], sbuf=o[i])
    kb.phase_end()


def s5_main(kb, G, li):
    nc = kb.nc
    kb.phase_begin()
    C = NS()
    load_consts(kb, C)
    UT = kb.sb("UT", [P, 16, NT], BF16)
    ua = G.U.t.ap().rearrange("(c p) t -> p c t", p=P)
    for c in range(16):
        kb.dma(kb.sp, UT[:, c, :], ua[:, c, :], reads=[G.U], writes=[UT], sbuf=UT)
    BWre = kb.sb("BWre", [P, 32, P], BF16)
    BWim = kb.sb("BWim", [P, 32, P], BF16)
    kb.dma(kb.sp, BWre[:], G.BWRE.t.ap().rearrange("p (a b) -> p a b", b=P), reads=[G.BWRE], writes=[BWre], sbuf=BWre)
    kb.dma(kb.sp, BWim[:], G.BWIM.t.ap().rearrange("p (a b) -> p a b", b=P), reads=[G.BWIM], writes=[BWim], sbuf=BWim)
    CW = [kb.sb(f"CW{i}", [P, 64, 64], BF16) for i in range(3)]
    for i in range(3):
        kb.dma(kb.sp, CW[i][:], G.CW.t.ap()[i].rearrange("p (a b) -> p a b", b=64), reads=[G.CW], writes=[CW[i]], sbuf=CW[i])
    SPR = kb.sb("SPR", [P, 6, 64], F32)
    kb.dma(kb.sp, SPR[:], G.SPAR.t.ap().rearrange("k p n -> p k n"), reads=[G.SPAR], writes=[SPR], sbuf=SPR)
    dcol = kb.sb("dcol", [P, 16], F32)
    kb.dma(kb.sp, dcol[:], G.ssm_d.t.ap()[li], reads=[], writes=[dcol], sbuf=dcol)
    IOTA = kb.sb("iota", [P, 512], F32)
    kb.dma(kb.sp, IOTA[:], G.iota512.t.ap(), reads=[], writes=[IOTA], sbuf=IOTA)
    H0 = kb.sb("h0", [P, 2, 64], F32)
    kb.dma(kb.sp, H0[:], G.h0.t.ap()[li].rearrange("k p n -> p k n"), reads=[], writes=[H0], sbuf=H0)
    INIT = kb.sb("init", [P, 2, 64], F32)
    kb.op(kb.dve, lambda: nc.vector.memset(INIT[:], 0.0), writes=[INIT])
    HP = kb.sb("HP", [P, 2, 64], F32)
    HS = kb.sb("HS", [P, 2, 64], F32)
    SINI = kb.sb("SINI", [P, 2, 64], F32)
    ctmp = kb.sb("ctmp", [P, 4], F32)
    COS = [kb.sb(f"COS{q}", [P, 512], F32) for q in range(8)]
    SIN = [kb.sb(f"SIN{q}", [P, 512], F32) for q in range(8)]
    TW = [kb.sb(f"tw{i}", [P, 512], F32) for i in range(3)]
    bR = Rot([kb.sb(f"bR{i}", [P, 512], F32) for i in range(2)])
    bI = Rot([kb.sb(f"bI{i}", [P, 512], F32) for i in range(2)])
    m1r = Rot([kb.sb(f"m1_{i}", [P, 512], F32) for i in range(2)])
    m2r = Rot([kb.sb(f"m2_{i}", [P, 512], F32) for i in range(2)])
    m3 = kb.sb("m3", [P, 512], F32)
    m4 = kb.sb("m4", [P, 512], F32)
    btre = Rot([kb.sb(f"btre{i}", [P, 512], F32) for i in range(3)])
    btim = Rot([kb.sb(f"btim{i}", [P, 512], F32) for i in range(3)])
    gre = Rot([kb.sb(f"gre{i}", [P, 512], F32) for i in range(2)])
    gim = Rot([kb.sb(f"gim{i}", [P, 512], F32) for i in range(2)])
    PR = [Rot([kb.sb(f"pr{a}_{i}", [P, 512], BF16) for i in range(2)]) for a in range(4)]
    yv = Rot([kb.sb(f"yv{i}", [P, 512], F32) for i in range(2)])
    gw = [kb.sb(f"gw{i}", [P, 512], F32) for i in range(2)]
    yo = Rot([kb.sb(f"yo{i}", [P, 512], BF16) for i in range(2)])
    psW = Rot([kb.psum[i] for i in range(6)])
    psY = Rot([kb.psum[6], kb.psum[7]])

    def cmul(o_re, o_im, a_re, a_im, g_re, g_im, rd, wr):
        kb.op(kb.dve, lambda: nc.vector.tensor_tensor(ctmp[:, 0:1], a_im, g_im, ALU.mult), reads=rd, writes=[ctmp])
        kb.op(kb.dve, lambda: nc.vector.scalar_tensor_tensor(o_re, g_re, a_re, ctmp[:, 0:1], ALU.mult, ALU.subtract),
              reads=rd + [ctmp], writes=wr)
        kb.op(kb.dve, lambda: nc.vector.tensor_tensor(ctmp[:, 1:2], a_im, g_re, ALU.mult), reads=rd, writes=[ctmp])
        kb.op(kb.dve, lambda: nc.vector.scalar_tensor_tensor(o_im, g_im, a_re, ctmp[:, 1:2], ALU.mult, ALU.add),
              reads=rd + [ctmp], writes=wr)

    for pi in range(64):
        cmul(SINI[:, 0, pi:pi + 1], SINI[:, 1, pi:pi + 1], SPR[:, 2, pi:pi + 1], SPR[:, 3, pi:pi + 1],
             H0[:, 0, pi:pi + 1], H0[:, 1, pi:pi + 1], [SPR, H0], [SINI])

    def tables(j):
        for q in range(4):
            pi = 4 * j + q
            cis_tables(kb, C, IOTA, 512, COS[(j % 2) * 4 + q], SIN[(j % 2) * 4 + q], TW,
                       xscale=SPR[:, 1, pi:pi + 1], xscale_buf=SPR)

    def stage_a(j, ti, q):
        t0, n = TT[ti]
        pi = 4 * j + q
        hf, s = q // 2, q % 2
        js = 2 * j + s
        rows = slice(64 * hf, 64 * hf + 64)
        psR = psW.next()
        psI = psW.next()
        kb.op(kb.pe, lambda: nc.tensor.matmul(psR[:, 0:n], BWre[rows, js, :], UT[rows, j, t0:t0 + n], start=True, stop=True),
              reads=[BWre, UT], writes=[psR])
        kb.op(kb.pe, lambda: nc.tensor.matmul(psI[:, 0:n], BWim[rows, js, :], UT[rows, j, t0:t0 + n], start=True, stop=True),
              reads=[BWim, UT], writes=[psI])
        r_ = bR.next()
        i_ = bI.next()
        kb.op(kb.act, lambda: nc.scalar.copy(r_[:, 0:n], psR[:, 0:n]), reads=[psR], writes=[r_])
        kb.op(kb.act, lambda: nc.scalar.copy(i_[:, 0:n], psI[:, 0:n]), reads=[psI], writes=[i_])
        cq_, sq_ = COS[(j % 2) * 4 + q], SIN[(j % 2) * 4 + q]
        tr, tim = btre.next(), btim.next()
        m1, m2 = m1r.next(), m2r.next()
        kb.op(kb.dve, lambda: nc.vector.tensor_tensor(m1[:, 0:n], cq_[:, 0:n], r_[:, 0:n], ALU.mult), reads=[cq_, r_], writes=[m1])
        kb.op(kb.dve, lambda: nc.vector.tensor_tensor(m2[:, 0:n], sq_[:, 0:n], i_[:, 0:n], ALU.mult), reads=[sq_, i_], writes=[m2])
        kb.op(kb.pool, lambda: nc.gpsimd.tensor_tensor(tr[:, 0:n], m1[:, 0:n], m2[:, 0:n], ALU.add), reads=[m1, m2], writes=[tr])
        kb.op(kb.pool, lambda: nc.gpsimd.tensor_tensor(m3[:, 0:n], cq_[:, 0:n], i_[:, 0:n], ALU.mult), reads=[cq_, i_], writes=[m3])
        kb.op(kb.pool, lambda: nc.gpsimd.tensor_tensor(m4[:, 0:n], sq_[:, 0:n], r_[:, 0:n], ALU.mult), reads=[sq_, r_], writes=[m4])
        kb.op(kb.pool, lambda: nc.gpsimd.tensor_tensor(tim[:, 0:n], m3[:, 0:n], m4[:, 0:n], ALU.subtract), reads=[m3, m4], writes=[tim])
        return (tr, tim, cq_, sq_)

    Ycur = [None]

    def stage_b(j, ti, q, st):
        t0, n = TT[ti]
        tr, tim, cq_, sq_ = st
        pi = 4 * j + q
        hf, s = q // 2, q % 2
        rows = slice(64 * hf, 64 * hf + 64)
        if q == 0:
            Ycur[0] = psY.next()
        Yps = Ycur[0]
        g_r, g_i = gre.next(), gim.next()
        rcol = SPR[:, 0, pi:pi + 1]
        if ti == 0:
            ini_r, ini_i, ini_rd = 0.0, 0.0, []
        elif ti < 4:
            ini_r, ini_i, ini_rd = INIT[:, 0, pi:pi + 1], INIT[:, 1, pi:pi + 1], [INIT]
        else:
            ini_r, ini_i, ini_rd = SINI[:, 0, pi:pi + 1], SINI[:, 1, pi:pi + 1], [SINI]
        kb.op(kb.dve, lambda: nc.vector.tensor_tensor_scan(g_r[:, 0:n], rcol.to_broadcast([P, n]), tr[:, 0:n], ini_r,
                                                            ALU.mult, ALU.add), reads=[SPR, tr] + ini_rd, writes=[g_r])
        kb.op(kb.dve, lambda: nc.vector.tensor_tensor_scan(g_i[:, 0:n], rcol.to_broadcast([P, n]), tim[:, 0:n], ini_i,
                                                            ALU.mult, ALU.add), reads=[SPR, tim] + ini_rd, writes=[g_i])
        lr_, li_ = g_r[:, n - 1:n], g_i[:, n - 1:n]
        if ti < 3:
            cmul(INIT[:, 0, pi:pi + 1], INIT[:, 1, pi:pi + 1], SPR[:, 4, pi:pi + 1], SPR[:, 5, pi:pi + 1], lr_, li_,
                 [SPR, g_r, g_i], [INIT])
        elif ti == 3:
            cmul(HP[:, 0, pi:pi + 1], HP[:, 1, pi:pi + 1], cq_[:, 511:512], sq_[:, 511:512], lr_, li_,
                 [cq_, sq_, g_r, g_i], [HP])
        else:
            cmul(HS[:, 0, pi:pi + 1], HS[:, 1, pi:pi + 1], cq_[:, n - 1:n], sq_[:, n - 1:n], lr_, li_,
                 [cq_, sq_, g_r, g_i], [HS])
        p1, p2, p3, p4 = (PR[a].next() for a in range(4))
        kb.op(kb.dve, lambda: nc.vector.tensor_tensor(p1[:, 0:n], cq_[:, 0:n], g_r[:, 0:n], ALU.mult), reads=[cq_, g_r], writes=[p1])
        kb.op(kb.pool, lambda: nc.gpsimd.tensor_tensor(p2[:, 0:n], sq_[:, 0:n], g_i[:, 0:n], ALU.mult), reads=[sq_, g_i], writes=[p2])
        kb.op(kb.dve, lambda: nc.vector.tensor_tensor(p3[:, 0:n], sq_[:, 0:n], g_r[:, 0:n], ALU.mult), reads=[sq_, g_r], writes=[p3])
        kb.op(kb.pool, lambda: nc.gpsimd.tensor_tensor(p4[:, 0:n], cq_[:, 0:n], g_i[:, 0:n], ALU.mult), reads=[cq_, g_i], writes=[p4])
        for a, (pp, cw) in enumerate(((p1, CW[0]), (p2, CW[1]), (p3, CW[2]), (p4, CW[2]))):
            first = (s == 0 and a == 0)
            last = (s == 1 and a == 3)
            kb.op(kb.pe, lambda: nc.tensor.matmul(Yps[rows, 0:n], cw[:, pi, :], pp[:, 0:n], start=first, stop=last),
                  reads=[cw, pp], writes=[Yps], inc=(a == 3))
        if q == 3:
            y = yv.next()
            kb.op(kb.dve, lambda: nc.vector.scalar_tensor_tensor(y[:, 0:n], UT[:, j, t0:t0 + n], dcol[:, j:j + 1], Yps[:, 0:n],
                                                                  ALU.mult, ALU.add), reads=[UT, dcol, Yps], writes=[y])
            kb.op(kb.act, lambda: nc.scalar.activation(gw[0][:, 0:n], y[:, 0:n], AF.Square), reads=[y], writes=[gw[0]])
            kb.op(kb.act, lambda: nc.scalar.activation(gw[0][:, 0:n], gw[0][:, 0:n], AF.Identity, bias=C.magic[:, 2:3], scale=0.044715),
                  reads=[gw[0], C.magic], writes=[gw[0]])
            kb.op(kb.pool, lambda: nc.gpsimd.tensor_tensor(gw[1][:, 0:n], gw[0][:, 0:n], y[:, 0:n], ALU.mult), reads=[gw[0], y], writes=[gw[1]])
            kb.op(kb.act, lambda: nc.scalar.activation(gw[1][:, 0:n], gw[1][:, 0:n], AF.Sigmoid, scale=1.5957691216057308),
                  reads=[gw[1]], writes=[gw[1]])
            o_ = yo.next()
            kb.op(kb.dve, lambda: nc.vector.tensor_tensor(o_[:, 0:n], gw[1][:, 0:n], y[:, 0:n], ALU.mult), reads=[gw[1], y], writes=[o_])
            kb.dma(kb.sp, G.YG.t.ap()[j * P:(j + 1) * P, t0:t0 + n], o_[:, 0:n], reads=[o_], writes=[G.YG], sbuf=o_)

    items = [(j, ti, q) for j in range(16) for ti in range(len(TT)) for q in range(4)]
    tables(0)
    pend = []
    for idx, it in enumerate(items):
        j, ti, q = it
        if ti == 2 and q == 0 and j + 1 < 16:
            tables(j + 1)
        st = stage_a(*it)
        pend.append((it, st))
        if len(pend) > 2:
            it0, st0 = pend.pop(0)
            stage_b(*it0, st0)
    while pend:
        it0, st0 = pend.pop(0)
        stage_b(*it0, st0)
    kb.dma(kb.sp, G.hp_out[li].t.ap().rearrange("k p n -> p k n"), HP[:], reads=[HP], writes=[G.hp_out[li]], sbuf=HP)
    kb.dma(kb.sp, G.hs_out[li].t.ap().rearrange("k p n -> p k n"), HS[:], reads=[HS], writes=[G.hs_out[li]], sbuf=HS)
    kb.phase_end()


NKA = SEQ + NT


def mla_phase(kb, G, li):
    mla_build(kb, G, li)
    mla_attend(kb, G, li)


def mla_build(kb, G, li):
    nc = kb.nc
    kb.phase_begin()
    C = NS()
    load_consts(kb, C)
    gk = kb.sb("gk", [P, 2], F32)
    kb.dma(kb.sp, gk[:], G.ab_k_norm.t.ap()[li], reads=[], writes=[gk], sbuf=gk)
    lat_src = G.lat_out[li].t.ap().rearrange("(c p) t -> p c t", p=P)
    latc_src = G.latc.t.ap()[li].rearrange("(c p) t -> p c t", p=P)
    LAT = kb.sb("LAT", [P, 4, NKA], BF16)
    for c in range(4):
        kb.dma(kb.pool, LAT[:, c, 0:SEQ], lat_src[:, c, 0:SEQ], reads=[G.lat_out[li]], writes=[LAT], sbuf=LAT)
        kb.dma(kb.pool, LAT[:, c, SEQ:2 * SEQ], latc_src[:, c, :], reads=[G.latc], writes=[LAT], sbuf=LAT)
        kb.dma(kb.pool, LAT[:, c, 2 * SEQ:NKA], lat_src[:, c, SEQ:NT], reads=[G.lat_out[li]], writes=[LAT], sbuf=LAT)
    KRF = kb.sb("KRF", [64, NKA], F32)
    kb.dma(kb.sp, KRF[:, 0:SEQ], G.kr_out[li].t.ap()[:, 0:SEQ], reads=[G.kr_out[li]], writes=[KRF], sbuf=KRF)
    kb.dma(kb.sp, KRF[:, SEQ:2 * SEQ], G.krc.t.ap()[li], reads=[G.krc], writes=[KRF], sbuf=KRF)
    kb.dma(kb.sp, KRF[:, 2 * SEQ:NKA], G.kr_out[li].t.ap()[:, SEQ:NT], reads=[G.kr_out[li]], writes=[KRF], sbuf=KRF)
    psR = Rot(kb.psum)
    sq16 = Rot([kb.sb(f"msq{i}", [P, 512], BF16) for i in range(3)])
    SSR = kb.sb("SSR", [P, NKA], F32)
    KRG = kb.sb("KRG", [64, NKA], F32)
    kb.op(kb.dve, lambda: nc.vector.tensor_scalar(KRG[0:64, :], KRF[0:64, :], gk[0:64, 1:2], None, ALU.mult), reads=[KRF, gk], writes=[KRG])
    ktiles = [(k0, 512) for k0 in range(0, 2 * SEQ, 512)] + [(2 * SEQ, 16)]
    for (k0, kn) in ktiles:
        b2 = sq16.next()
        kb.op(kb.act, lambda: nc.scalar.activation(b2[0:64, 0:kn], KRF[0:64, k0:k0 + kn], AF.Square), reads=[KRF], writes=[b2])
        ps = psR.next()
        kb.op(kb.pe, lambda: nc.tensor.matmul(ps[:, 0:kn], C.ones[0:64, :], b2[0:64, 0:kn], start=True, stop=True),
              reads=[C.ones, b2], writes=[ps])
        kb.op(kb.dve, lambda: nc.vector.tensor_copy(SSR[:, k0:k0 + kn], ps[:, 0:kn]), reads=[ps], writes=[SSR])
    Wh = Rot([kb.sb(f"Wh{i}", [P, 4, 256], BF16) for i in range(2)])
    tot = Rot([kb.sb(f"tot{i}", [P, 512], F32) for i in range(2)])
    sd = Rot([kb.sb(f"msd{i}", [P, 512], F32) for i in range(2)])
    rs = Rot([kb.sb(f"mrs{i}", [P, 512], F32) for i in range(2)])
    kno = Rot([kb.sb(f"kno{i}", [P, 512], BF16) for i in range(3)])
    kro = Rot([kb.sb(f"kro{i}", [64, 512], BF16) for i in range(3)])
    vo = Rot([kb.sb(f"vo{i}", [P, 4, P], BF16) for i in range(3)])
    wsrc = G.ab_w_ukv.t.ap()[li].rearrange("(k p) m -> p k m", p=P)
    items = []
    for h in range(16):
        for (k0, kn) in ktiles:
            items.append(("K", h, k0, kn))
        for kt4 in range(0, 33, 4):
            items.append(("V", h, kt4, 0))
    wcur = {}
    A_out = {}

    def st_A(i):
        kind, h, a, b = items[i]
        if h not in wcur:
            w = Wh.next()
            kb.dma(kb.pool, w[:], wsrc[:, :, h * 256:(h + 1) * 256], reads=[], writes=[w], sbuf=w)
            wcur[h] = w
        w = wcur[h]
        if kind == "K":
            k0, kn = a, b
            psK = psR.next()
            for c in range(4):
                kb.op(kb.pe, lambda: nc.tensor.matmul(psK[:, 0:kn], w[:, c, 0:128], LAT[:, c, k0:k0 + kn], start=(c == 0), stop=(c == 3)),
                      reads=[w, LAT], writes=[psK], inc=(c == 3))
            b2 = sq16.next()
            kb.op(kb.act, lambda: nc.scalar.activation(b2[:, 0:kn], psK[:, 0:kn], AF.Square), reads=[psK], writes=[b2])
            A_out[i] = (psK, b2)
        else:
            kt4 = a
            psV = psR.next()
            kts = list(range(kt4, min(kt4 + 4, 33)))
            for kt in kts:
                kk = P if kt < 32 else 16
                for c in range(4):
                    kb.op(kb.pe, lambda: nc.tensor.matmul(psV[0:kk, (kt - kt4) * P:(kt - kt4 + 1) * P], LAT[:, c, kt * P:kt * P + kk],
                                                          w[:, c, 128:256], start=(c == 0), stop=(c == 3)),
                          reads=[LAT, w], writes=[psV], inc=(c == 3 and kt == kts[-1]))
            A_out[i] = (psV, len(kts))

    def st_B(i):
        kind, h, a, b = items[i]
        if kind == "K":
            k0, kn = a, b
            psK, b2 = A_out.pop(i)
            pss = psR.next()
            kb.op(kb.pe, lambda: nc.tensor.matmul(pss[:, 0:kn], C.ones[:], b2[:, 0:kn], start=True, stop=True),
                  reads=[C.ones, b2], writes=[pss])
            t_, s_, r_ = tot.next(), sd.next(), rs.next()
            kb.op(kb.dve, lambda: nc.vector.tensor_tensor(t_[:, 0:kn], pss[:, 0:kn], SSR[:, k0:k0 + kn], ALU.add),
                  reads=[pss, SSR], writes=[t_])
            kb.op(kb.act, lambda: nc.scalar.activation(s_[:, 0:kn], t_[:, 0:kn], AF.Ln, bias=C.eps[:, 0:1], scale=1.0 / 192),
                  reads=[t_, C.eps], writes=[s_])
            kb.op(kb.act, lambda: nc.scalar.activation(r_[:, 0:kn], s_[:, 0:kn], AF.Exp, scale=-0.5), reads=[s_], writes=[r_])
            o1, o2 = kno.next(), kro.next()
            kb.op(kb.dve, lambda: nc.vector.scalar_tensor_tensor(o1[:, 0:kn], psK[:, 0:kn], gk[:, 0:1], r_[:, 0:kn], ALU.mult, ALU.mult),
                  reads=[psK, gk, r_], writes=[o1])
            kb.op(kb.pool, lambda: nc.gpsimd.tensor_tensor(o2[0:64, 0:kn], KRG[0:64, k0:k0 + kn], r_[0:64, 0:kn], ALU.mult),
                  reads=[KRG, r_], writes=[o2])
            kb.dma(kb.sp, G.KNS.t.ap()[h, :, k0:k0 + kn], o1[:, 0:kn], reads=[o1], writes=[G.KNS], sbuf=o1)
            kb.dma(kb.sp, G.KRS.t.ap()[h, :, k0:k0 + kn], o2[0:64, 0:kn], reads=[o2], writes=[G.KRS], sbuf=o2)
        else:
            kt4 = a
            psV, nk = A_out.pop(i)
            v_ = vo.next()
            if nk == 4:
                kb.op(kb.act, lambda: nc.scalar.copy(v_[:, :, :], psV[:, :].rearrange("p (a b) -> p a b", b=P)), reads=[psV], writes=[v_])
                kb.dma(kb.sp, G.VSC.t.ap()[h, :, kt4:kt4 + 4, :], v_[:, :, :], reads=[v_], writes=[G.VSC], sbuf=v_)
            else:
                kb.op(kb.act, lambda: nc.scalar.copy(v_[0:16, 0, :], psV[0:16, 0:P]), reads=[psV], writes=[v_])
                kb.dma(kb.sp, G.VSC.t.ap()[h, 0:16, 32, :], v_[0:16, 0, :], reads=[v_], writes=[G.VSC], sbuf=v_)

    nI = len(items)
    for step in range(nI + 1):
        if step < nI:
            st_A(step)
        if step >= 1:
            st_B(step - 1)
    kb.phase_end()


def mla_attend(kb, G, li):
    nc = kb.nc
    kb.phase_begin()
    C = NS()
    load_consts(kb, C)
    SCALE = 192.0 ** -0.5
    psW = Rot([kb.psum[i] for i in range(4)])
    psO = Rot([kb.psum[4], kb.psum[5]])
    psL = Rot([kb.psum[6], kb.psum[7]])
    KNb = Rot([kb.sb(f"KN{i}", [P, NKA], BF16) for i in range(2)])
    KRb = Rot([kb.sb(f"KR{i}", [64, NKA], BF16) for i in range(2)])
    Vb = Rot([kb.sb(f"V{i}", [P, 33, P], BF16) for i in range(2)])
    QN = Rot([kb.sb(f"QN{i}", [P, NT], BF16) for i in range(2)])
    QR = Rot([kb.sb(f"QR{i}", [64, NT], BF16) for i in range(2)])
    PT = Rot([kb.sb(f"PT{i}", [P, 512], BF16) for i in range(3)])
    rec = kb.sb("rec", [P, 512], F32)
    on = kb.sb("on", [P, 512], F32)
    gat = Rot([kb.sb(f"gat{i}", [P, 512], BF16) for i in range(2)])
    ao = Rot([kb.sb(f"ao{i}", [P, 512], BF16) for i in range(2)])

    def finish(Ops, Lps, ncol, h, q0):
        kb.op(kb.act, lambda: nc.scalar.activation(rec[:, 0:ncol], Lps[:, 0:ncol], AF.Ln), reads=[Lps], writes=[rec])
        kb.op(kb.act, lambda: nc.scalar.activation(rec[:, 0:ncol], rec[:, 0:ncol], AF.Exp, scale=-1.0), reads=[rec], writes=[rec])
        kb.op(kb.dve, lambda: nc.vector.tensor_tensor(on[:, 0:ncol], Ops[:, 0:ncol], rec[:, 0:ncol], ALU.mult), reads=[Ops, rec], writes=[on])
        g_ = gat.next()
        kb.dma(kb.sp, g_[:, 0:ncol], G.GA.t.ap()[h * P:(h + 1) * P, q0:q0 + ncol], reads=[G.GA], writes=[g_], sbuf=g_)
        a_ = ao.next()
        kb.op(kb.pool, lambda: nc.gpsimd.tensor_tensor(a_[:, 0:ncol], on[:, 0:ncol], g_[:, 0:ncol], ALU.mult), reads=[on, g_], writes=[a_])
        kb.dma(kb.sp, G.AO.t.ap()[h * P:(h + 1) * P, q0:q0 + ncol], a_[:, 0:ncol], reads=[a_], writes=[G.AO], sbuf=a_)

    def load_head(h):
        kn, kr, v, qn, qr = KNb.next(), KRb.next(), Vb.next(), QN.next(), QR.next()
        kb.dma(kb.sp, kn[:], G.KNS.t.ap()[h], reads=[G.KNS], writes=[kn], sbuf=kn)
        kb.dma(kb.sp, kr[:], G.KRS.t.ap()[h], reads=[G.KRS], writes=[kr], sbuf=kr)
        kb.dma(kb.sp, v[:], G.VSC.t.ap()[h], reads=[G.VSC], writes=[v], sbuf=v)
        kb.dma(kb.sp, qn[:], G.QT.t.ap()[h * 192:h * 192 + 128, :], reads=[G.QT], writes=[qn], sbuf=qn)
        kb.dma(kb.sp, qr[:], G.QT.t.ap()[h * 192 + 128:h * 192 + 192, :], reads=[G.QT], writes=[qr], sbuf=qr)
        return kn, kr, v, qn, qr

    nxt = load_head(0)
    for h in range(16):
        KN, KR, V, qn, qr = nxt
        if h + 1 < 16:
            nxt = load_head(h + 1)
        items = [("p", qt, kt) for qt in range(4) for kt in range(4 * (qt + 1))] + [("s", 0, kt) for kt in range(17)]
        S_out, F_out, acc = {}, {}, {}

        def st_S(i):
            kind, qt, kt = items[i]
            Sps = psW.next()
            if kind == "p":
                q0 = qt * 512
                col0 = max(0, kt - 4 * qt) * P
                ncols = 512 - col0
                kb.op(kb.pe, lambda: nc.tensor.matmul(Sps[:, 0:ncols], KN[:, kt * P:(kt + 1) * P], qn[:, q0 + col0:q0 + 512], start=True, stop=False),
                      reads=[KN, qn], writes=[Sps], inc=False)
                kb.op(kb.pe, lambda: nc.tensor.matmul(Sps[:, 0:ncols], KR[0:64, kt * P:(kt + 1) * P], qr[0:64, q0 + col0:q0 + 512], start=False, stop=True),
                      reads=[KR, qr], writes=[Sps], inc=True)
            else:
                kk = P if kt < 16 else 16
                kb.op(kb.pe, lambda: nc.tensor.matmul(Sps[0:kk, 0:TS], KN[:, SEQ + kt * P:SEQ + kt * P + kk], qn[:, SEQ:NT], start=True, stop=False),
                      reads=[KN, qn], writes=[Sps], inc=False)
                kb.op(kb.pe, lambda: nc.tensor.matmul(Sps[0:kk, 0:TS], KR[0:64, SEQ + kt * P:SEQ + kt * P + kk], qr[0:64, SEQ:NT], start=False, stop=True),
                      reads=[KR, qr], writes=[Sps], inc=True)
            S_out[i] = Sps

        def st_F(i):
            kind, qt, kt = items[i]
            Sps = S_out.pop(i)
            pt = PT.next()
            if kind == "p":
                col0 = max(0, kt - 4 * qt) * P
                ncols = 512 - col0
                kb.op(kb.act, lambda: nc.scalar.activation(pt[:, 0:ncols], Sps[:, 0:ncols], AF.Exp, scale=SCALE), reads=[Sps], writes=[pt])
                if kt >= 4 * qt:
                    kb.op(kb.pool, lambda: nc.gpsimd.memset(pt[64:128, 0:64], 0.0), writes=[pt])
            else:
                kk = P if kt < 16 else 16
                kb.op(kb.act, lambda: nc.scalar.activation(pt[0:kk, 0:TS], Sps[0:kk, 0:TS], AF.Exp, scale=SCALE), reads=[Sps], writes=[pt])
            F_out[i] = pt

        def st_V(i):
            kind, qt, kt = items[i]
            pt = F_out.pop(i)
            if kt == 0:
                acc["O"], acc["L"] = psO.next(), psL.next()
            Ops, Lps = acc["O"], acc["L"]
            if kind == "p":
                nkt = 4 * (qt + 1)
                col0 = max(0, kt - 4 * qt) * P
                ncols = 512 - col0
                last = (kt == nkt - 1)
                kb.op(kb.pe, lambda: nc.tensor.matmul(Ops[:, col0:512], V[:, kt, :], pt[:, 0:ncols], start=(kt == 0), stop=last),
                      reads=[V, pt], writes=[Ops], inc=False)
                kb.op(kb.pe, lambda: nc.tensor.matmul(Lps[:, col0:512], C.ones[:], pt[:, 0:ncols], start=(kt == 0), stop=last),
                      reads=[C.ones, pt], writes=[Lps], inc=True)
                if last:
                    finish(Ops, Lps, 512, h, qt * 512)
            else:
                kk = P if kt < 16 else 16
                kb.op(kb.pe, lambda: nc.tensor.matmul(Ops[:, 0:TS], V[0:kk, 16 + kt, :], pt[0:kk, 0:TS], start=(kt == 0), stop=(kt == 16)),
                      reads=[V, pt], writes=[Ops], inc=False)
                kb.op(kb.pe, lambda: nc.tensor.matmul(Lps[:, 0:TS], C.ones[0:kk, :], pt[0:kk, 0:TS], start=(kt == 0), stop=(kt == 16)),
                      reads=[C.ones, pt], writes=[Lps], inc=True)
                if kt == 16:
                    finish(Ops, Lps, TS, h, SEQ)

        nI = len(items)
        for step in range(nI + 2):
            if step < nI:
                st_S(step)
            if 0 <= step - 1 < nI:
                st_F(step - 1)
            if 0 <= step - 2 < nI:
                st_V(step - 2)
    kb.phase_end()


def load_resident(kb, src, KC, A):
    sa = src.t.ap().rearrange("(c p) t -> p c t", p=P)
    for c in range(KC):
        kb.dma(kb.sp, A[:, c, :], sa[:, c, :], reads=[src], writes=[A], sbuf=A)


def glu_phase(kb, G, li):
    nc = kb.nc
    kb.phase_begin()
    A = kb.sb("Aglu", [P, 16, NT], BF16)
    load_resident(kb, G.YG, 16, A)
    hold = [kb.sb(f"ghold{i}", [P, 512], F32) for i in range(5)]
    sig = Rot([kb.sb(f"gsig{i}", [P, 512], F32) for i in range(2)])
    gbt = Rot([kb.sb(f"ggb{i}", [P, 512], BF16) for i in range(2)])
    outb = Rot([kb.sb(f"gout{i}", [P, 512], BF16) for i in range(2)])
    jobs = []
    for m in range(16):
        jobs.append((m * P, P, ("a", m)))
        jobs.append((2048 + m * P, P, ("b", m)))

    def epi(tag, ji, ti, ps, msz):
        t0, n = TT[ti]
        kind, m = tag
        if kind == "a":
            kb.op(kb.dve, lambda: nc.vector.tensor_copy(hold[ti][:, 0:n], ps[:, 0:n]), reads=[ps], writes=[hold[ti]])
            return
        s = sig.next()
        kb.op(kb.act, lambda: nc.scalar.activation(s[:, 0:n], ps[:, 0:n], AF.Sigmoid), reads=[ps], writes=[s])
        g_ = gbt.next()
        kb.dma(kb.sp, g_[:, 0:n], G.GB.t.ap()[m * P:(m + 1) * P, t0:t0 + n], reads=[G.GB], writes=[g_], sbuf=g_)
        kb.op(kb.pool, lambda: nc.gpsimd.tensor_tensor(s[:, 0:n], s[:, 0:n], hold[ti][:, 0:n], ALU.mult), reads=[s, hold[ti]], writes=[s])
        o = outb.next()
        kb.op(kb.dve, lambda: nc.vector.tensor_tensor(o[:, 0:n], s[:, 0:n], g_[:, 0:n], ALU.mult), reads=[s, g_], writes=[o])
        kb.dma(kb.sp, G.AO.t.ap()[2048 + m * P:2048 + (m + 1) * P, t0:t0 + n], o[:, 0:n], reads=[o], writes=[G.AO], sbuf=o)

    linear(kb, A, 16, G.ssm_w_glu.t.ap()[li], jobs, epi, "lg")
    kb.phase_end()


def out_proj_phase(kb, G, W_ap, Xin, Xout):
    nc = kb.nc
    kb.phase_begin()
    A = kb.sb("Aout", [P, 32, NT], BF16)
    load_resident(kb, G.AO, 32, A)
    xt = Rot([kb.sb(f"oxt{i}", [P, 512], F32) for i in range(3)])
    ot = Rot([kb.sb(f"oot{i}", [P, 512], F32) for i in range(3)])
    jobs = [(m * P, P, ("o", m)) for m in range(32)]

    def epi(tag, ji, ti, ps, msz):
        t0, n = TT[ti]
        m = tag[1]
        x_ = xt.next()
        kb.dma(kb.sp, x_[:, 0:n], Xin.t.ap()[m * P:(m + 1) * P, t0:t0 + n], reads=[Xin], writes=[x_], sbuf=x_)
        o = ot.next()
        kb.op(kb.dve, lambda: nc.vector.tensor_tensor(o[:, 0:n], ps[:, 0:n], x_[:, 0:n], ALU.add), reads=[ps, x_], writes=[o])
        kb.dma(kb.sp, Xout.t.ap()[m * P:(m + 1) * P, t0:t0 + n], o[:, 0:n], reads=[o], writes=[Xout], sbuf=o)

    linear(kb, A, 32, W_ap, jobs, epi, "lo")
    kb.phase_end()


def c_phase1(kb, G, li, X):
    nc = kb.nc
    kb.phase_begin()
    C = NS()
    load_consts(kb, C)
    bd = kb.sb("bdones", [P, P], BF16)
    kb.op(kb.dve, lambda: nc.vector.memset(bd[:], 0.0), writes=[bd])
    kb.op(kb.dve, lambda: nc.vector.memset(bd[0:64, 0:64], 1.0), writes=[bd])
    kb.op(kb.dve, lambda: nc.vector.memset(bd[64:128, 64:128], 1.0), writes=[bd])
    gcol = kb.sb("cgcol", [P, 32], F32)
    kb.dma(kb.sp, gcol[:], G.c_norm.t.ap()[li], reads=[], writes=[gcol], sbuf=gcol)
    gqk = kb.sb("cgqk", [P, 2], F32)
    kb.dma(kb.sp, gqk[:], G.c_qk_norm.t.ap()[li], reads=[], writes=[gqk], sbuf=gqk)
    A = kb.sb("Ac", [P, 32, NT], BF16)
    rmsnorm_resident(kb, C, X, 32, gcol, A, "nc")
    sq = Rot([kb.sb(f"csq{i}", [P, 512], BF16) for i in range(2)])
    sd = kb.sb("csd", [P, 512], F32)
    rs = kb.sb("crs", [P, 512], F32)
    st32 = Rot([kb.sb(f"cst32_{i}", [P, 512], F32) for i in range(3)])
    st16 = Rot([kb.sb(f"cst16_{i}", [P, 512], BF16) for i in range(3)])
    jobs = [(m * P, P, ("q", m)) for m in range(32)]
    jobs += [(4096 + m * P, P, ("k", m)) for m in range(4)]
    jobs += [(4608 + m * P, P, ("v", m)) for m in range(4)]
    jobs += [(5120 + m * P, P, ("g", m)) for m in range(32)]
    dd = Buf("dd_dummy")
    kb.dma(kb.sp, G.k_out[li].t.ap()[:, 128:240], G.kcache.t.ap()[li][:, 16:128], reads=[G.kcache], writes=[G.k_out[li]], sbuf=dd)
    kb.dma(kb.sp, G.v_out[li].t.ap()[:, 128:240], G.vcacheT.t.ap()[li][:, 16:128], reads=[G.vcacheT], writes=[G.v_out[li]], sbuf=dd)
    kb.phase_bufs.append(dd)

    def epi(tag, ji, ti, ps, msz):
        t0, n = TT[ti]
        kind, m = tag
        rows = slice(m * P, (m + 1) * P)
        if kind in ("q", "k"):
            b2 = sq.next()
            kb.op(kb.act, lambda: nc.scalar.activation(b2[:, 0:n], ps[:, 0:n], AF.Square), reads=[ps], writes=[b2])
            pss = kb.next_ps()
            kb.op(kb.pe, lambda: nc.tensor.matmul(pss[:, 0:n], bd[:], b2[:, 0:n], start=True, stop=True), reads=[bd, b2], writes=[pss])
            rstd_from_ss(kb, C, pss, n, 1.0 / 64, rs, slice(0, n), sd)
            gc = gqk[:, 0:1] if kind == "q" else gqk[:, 1:2]
            if kind == "q":
                o = st16.next()
                dst = G.QS
            else:
                o = st32.next()
                dst = G.KS
            kb.op(kb.dve, lambda: nc.vector.scalar_tensor_tensor(o[:, 0:n], ps[:, 0:n], gc, rs[:, 0:n], ALU.mult, ALU.mult),
                  reads=[ps, gqk, rs], writes=[o])
            kb.dma(kb.sp, dst.t.ap()[rows, t0:t0 + n], o[:, 0:n], reads=[o], writes=[dst], sbuf=o)
            outw = G.k_out[li]
        elif kind == "v":
            o = st32.next()
            kb.op(kb.act, lambda: nc.scalar.copy(o[:, 0:n], ps[:, 0:n]), reads=[ps], writes=[o])
            kb.dma(kb.sp, G.VS.t.ap()[rows, t0:t0 + n], o[:, 0:n], reads=[o], writes=[G.VS], sbuf=o)
            outw = G.v_out[li]
        else:
            o = st16.next()
            kb.op(kb.act, lambda: nc.scalar.activation(o[:, 0:n], ps[:, 0:n], AF.Silu), reads=[ps], writes=[o])
            kb.dma(kb.sp, G.GC.t.ap()[rows, t0:t0 + n], o[:, 0:n], reads=[o], writes=[G.GC], sbuf=o)
        if kind in ("k", "v"):
            if ti == 3:
                kb.dma(kb.sp, outw.t.ap()[rows, 0:128], o[:, 384:512], reads=[o], writes=[outw], sbuf=o)
            if ti == 4:
                kb.dma(kb.sp, outw.t.ap()[rows, 240:256], o[:, 0:16], reads=[o], writes=[outw], sbuf=o)

    linear(kb, A, 32, G.c_w_in.t.ap()[li], jobs, epi, "lc")
    kb.phase_end()


LB = 384
NCOPY = 144


def rel_bucket_np(rel):
    nb = 16
    max_exact = 8
    ret = np.where(rel > 0, nb, 0)
    n = np.abs(rel)
    nf = np.maximum(n, 1).astype(np.float32)
    large = max_exact + (np.log(nf / max_exact) / math.log(128 / max_exact) * (nb - max_exact)).astype(np.int32)
    large = np.minimum(large, nb - 1)
    return ret + np.where(n < max_exact, n, large)


def bias_setup(kb, G):
    nc = kb.nc
    kb.phase_begin()
    tab = kb.sb("btab", [32, 64], F32)
    oh = kb.sb("boh", [32, LB], F32)
    kb.dma(kb.sp, tab[:], G.rel_table.t.ap(), reads=[], writes=[tab], sbuf=tab)
    kb.dma(kb.sp, oh[:], G.onehot.t.ap(), reads=[], writes=[oh], sbuf=oh)
    ps = kb.next_ps()
    kb.op(kb.pe, lambda: nc.tensor.matmul(ps[0:64, 0:LB], tab[0:32, 0:64], oh[0:32, 0:LB], start=True, stop=True),
          reads=[tab, oh], writes=[ps])
    gsb = kb.sb("bg", [64, LB], F32)
    kb.op(kb.dve, lambda: nc.vector.tensor_copy(gsb[:], ps[0:64, 0:LB]), reads=[ps], writes=[gsb])
    kb.dma(kb.sp, G.GD.t.ap(), gsb[:], reads=[gsb], writes=[G.GD], sbuf=gsb)
    dd = Buf("dd_bias")
    kb.phase_bufs.append(dd)
    for h4 in range(0, 64, 4):
        src = bass.AP(tensor=G.GD.t, offset=h4 * LB, ap=[[LB, 4], [0, NCOPY], [1, LB]])
        kb.dma(kb.sp, G.GT.t.ap()[h4:h4 + 4], src, reads=[G.GD], writes=[G.GT], sbuf=dd)
    tp = Rot([kb.sb(f"btp{i}", [P, 256], F32) for i in range(3)])
    ts = Rot([kb.sb(f"bts{i}", [P, 32], F32) for i in range(3)])
    for h in range(64):
        t = tp.next()
        src = bass.AP(tensor=G.GT.t, offset=h * NCOPY * LB + 127, ap=[[LB - 1, 128], [1, 256]])
        kb.dma(kb.sp, t[:], src, reads=[G.GT], writes=[t], sbuf=t)
        kb.op(kb.act, lambda: nc.scalar.activation(t[:], t[:], AF.Exp), reads=[t], writes=[t])
        kb.op(kb.dve, lambda: nc.vector.memset(t[0:64, 192:256], 0.0), writes=[t])
        kb.op(kb.dve, lambda: nc.vector.memset(t[64:128, 0:64], 0.0), writes=[t])
        kb.dma(kb.sp, G.EBP.t.ap()[h], t[:], reads=[t], writes=[G.EBP], sbuf=t)
        u = ts.next()
        srcA = bass.AP(tensor=G.GT.t, offset=h * NCOPY * LB + 255, ap=[[LB - 1, 128], [1, 16]])
        srcB = bass.AP(tensor=G.GT.t, offset=h * NCOPY * LB + 255 + 128 * (LB - 1), ap=[[LB - 1, 16], [1, 16]])
        kb.dma(kb.sp, u[:, 0:16], srcA, reads=[G.GT], writes=[u], sbuf=u)
        kb.dma(kb.sp, u[0:16, 16:32], srcB, reads=[G.GT], writes=[u], sbuf=u)
        kb.op(kb.act, lambda: nc.scalar.activation(u[:, 0:16], u[:, 0:16], AF.Exp), reads=[u], writes=[u])
        kb.op(kb.act, lambda: nc.scalar.activation(u[0:16, 16:32], u[0:16, 16:32], AF.Exp), reads=[u], writes=[u])
        kb.dma(kb.sp, G.EBS.t.ap()[h], u[:], reads=[u], writes=[G.EBS], sbuf=u)
    kb.phase_end()


def c_phase2(kb, G, li):
    nc = kb.nc
    kb.phase_begin()
    C = NS()
    load_consts(kb, C)
    SCALE = 64.0 ** -0.5
    ident = kb.sb("ident", [P, P], F32)
    kb.dma(kb.sp, ident[:], G.ident.t.ap(), reads=[], writes=[ident], sbuf=ident)
    ES = kb.sb("ES", [P, 64], F32)
    kb.dma(kb.sp, ES[:], G.sinks.t.ap()[li], reads=[], writes=[ES], sbuf=ES)
    kb.op(kb.act, lambda: nc.scalar.activation(ES[:], ES[:], AF.Exp), reads=[ES], writes=[ES])
    psW = Rot([kb.psum[i] for i in range(4)])
    psO = Rot([kb.psum[4], kb.psum[5]])
    psL = Rot([kb.psum[6], kb.psum[7]])
    V2 = [kb.sb(f"V2_{c}", [P, 16, P], BF16) for c in range(4)]
    VsA = kb.sb("VsA", [P, 512], BF16)
    VsB = kb.sb("VsB", [16, 512], BF16)
    kb.dma(kb.pool, VsA[:], G.vcache.t.ap()[li], reads=[], writes=[VsA], sbuf=VsA)
    vin = Rot([kb.sb(f"vin{i}", [P, NT], F32) for i in range(2)])
    for c in range(4):
        v_ = vin.next()
        kb.dma(kb.sp, v_[:], G.VS.t.ap()[c * P:(c + 1) * P, :], reads=[G.VS], writes=[v_], sbuf=v_)
        for kt4 in range(0, 16, 4):
            ps = psW.next()
            for a in range(4):
                kt = kt4 + a
                kb.op(kb.pe, lambda: nc.tensor.transpose(ps[:, a * P:(a + 1) * P], v_[:, kt * P:(kt + 1) * P], ident[:]),
                      reads=[v_, ident], writes=[ps], inc=(a == 3))
            kb.op(kb.act, lambda: nc.scalar.copy(V2[c][:, kt4:kt4 + 4, :], ps[:, :].rearrange("p (a b) -> p a b", b=P)),
                  reads=[ps], writes=[V2[c]])
        ps = psW.next()
        kb.op(kb.pe, lambda: nc.tensor.transpose(ps[0:16, 0:P], v_[:, SEQ:NT], ident[:]), reads=[v_, ident], writes=[ps])
        kb.op(kb.act, lambda: nc.scalar.copy(VsB[0:16, c * P:(c + 1) * P], ps[0:16, 0:P]), reads=[ps], writes=[VsB])
    KT2 = Rot([kb.sb(f"KT2_{i}", [P, SEQ], BF16) for i in range(2)])
    KTs = Rot([kb.sb(f"KTs_{i}", [P, 144], BF16) for i in range(2)])
    Q2 = Rot([kb.sb(f"Q2_{i}", [P, NT], BF16) for i in range(2)])
    GCt = Rot([kb.sb(f"GCt_{i}", [P, NT], BF16) for i in range(2)])
    EBp = Rot([kb.sb(f"EBp{i}", [P, 256], F32) for i in range(2)])
    EBs = Rot([kb.sb(f"EBs{i}", [P, 32], F32) for i in range(2)])
    Et = Rot([kb.sb(f"Et{i}", [P, 256], F32) for i in range(3)])
    PTt = Rot([kb.sb(f"PTt{i}", [P, 256], BF16) for i in range(3)])
    lt = kb.sb("clt", [P, 512], F32)
    ot = kb.sb("cot", [P, 512], F32)
    aot = Rot([kb.sb(f"caot{i}", [P, 512], BF16) for i in range(2)])

    def finish(Ops, Lps, rows, ncol, h, q0, gct):
        kb.op(kb.dve, lambda: nc.vector.tensor_scalar(lt[rows, 0:ncol], Lps[rows, 0:ncol], ES[rows, h:h + 1], None, ALU.add),
              reads=[Lps, ES], writes=[lt])
        kb.op(kb.act, lambda: nc.scalar.activation(lt[rows, 0:ncol], lt[rows, 0:ncol], AF.Ln), reads=[lt], writes=[lt])
        kb.op(kb.act, lambda: nc.scalar.activation(lt[rows, 0:ncol], lt[rows, 0:ncol], AF.Exp, scale=-1.0), reads=[lt], writes=[lt])
        kb.op(kb.dve, lambda: nc.vector.tensor_tensor(ot[rows, 0:ncol], Ops[rows, 0:ncol], lt[rows, 0:ncol], ALU.mult), reads=[Ops, lt], writes=[ot])
        a_ = aot.next()
        kb.op(kb.pool, lambda: nc.gpsimd.tensor_tensor(a_[rows, 0:ncol], ot[rows, 0:ncol], gct[rows, q0:q0 + ncol], ALU.mult),
              reads=[ot, gct], writes=[a_])
        kb.dma(kb.sp, G.AO.t.ap()[h * 64:(h + 1) * 64, q0:q0 + ncol], a_[rows, 0:ncol], reads=[a_], writes=[G.AO], sbuf=a_)

    for g in range(8):
        kt2, kts = KT2.next(), KTs.next()
        for hh in range(2):
            r_ = slice(64 * hh, 64 * hh + 64)
            kb.dma(kb.pool, kt2[r_, :], G.KS.t.ap()[g * 64:(g + 1) * 64, 0:SEQ], reads=[G.KS], writes=[kt2], sbuf=kt2)
            kb.dma(kb.pool, kts[r_, 0:128], G.kcache.t.ap()[li][g * 64:(g + 1) * 64, :], reads=[G.kcache], writes=[kts], sbuf=kts)
            kb.dma(kb.pool, kts[r_, 128:144], G.KS.t.ap()[g * 64:(g + 1) * 64, SEQ:NT], reads=[G.KS], writes=[kts], sbuf=kts)
        vc, vo = g // 2, (g % 2) * 64
        items = [(hp, hh, it) for hp in range(4) for hh in range(2) for it in (list(range(16)) + ["s"])]
        ctx = {}
        S_out, F_out = {}, {}

        def head_ctx(hp, hh):
            key = (hp, hh)
            if key in ctx:
                return ctx[key]
            ch = g * 4 + hp
            if hh == 0:
                q2, gct = Q2.next(), GCt.next()
                kb.dma(kb.sp, q2[:], G.QS.t.ap()[ch * P:(ch + 1) * P, :], reads=[G.QS], writes=[q2], sbuf=q2)
                kb.dma(kb.sp, gct[:], G.GC.t.ap()[ch * P:(ch + 1) * P, :], reads=[G.GC], writes=[gct], sbuf=gct)
            else:
                q2, gct = ctx[(hp, 0)]["q2"], ctx[(hp, 0)]["gct"]
            h = 2 * ch + hh
            ebp, ebs = EBp.next(), EBs.next()
            kb.dma(kb.sp, ebp[:], G.EBP.t.ap()[h], reads=[G.EBP], writes=[ebp], sbuf=ebp)
            kb.dma(kb.sp, ebs[:], G.EBS.t.ap()[h], reads=[G.EBS], writes=[ebs], sbuf=ebs)
            ctx[key] = dict(q2=q2, gct=gct, ebp=ebp, ebs=ebs, h=h, rows=slice(64 * hh, 64 * hh + 64), O=None, L=None)
            return ctx[key]

        def st_S(i):
            hp, hh, it = items[i]
            c = head_ctx(hp, hh)
            rows, q2 = c["rows"], c["q2"]
            if it == "s":
                SA, SB = psW.next(), psW.next()
                kb.op(kb.pe, lambda: nc.tensor.matmul(SA[:, 0:TS], kts[rows, 0:128], q2[rows, SEQ:NT], start=True, stop=True), reads=[kts, q2], writes=[SA])
                kb.op(kb.pe, lambda: nc.tensor.matmul(SB[0:16, 0:TS], kts[rows, 128:144], q2[rows, SEQ:NT], start=True, stop=True), reads=[kts, q2], writes=[SB])
                S_out[i] = (SA, SB)
                return
            j = it
            nq = min(256, SEQ - P * j)
            Sps = psW.next()
            kb.op(kb.pe, lambda: nc.tensor.matmul(Sps[:, 0:nq], kt2[rows, j * P:(j + 1) * P], q2[rows, j * P:j * P + nq], start=True, stop=True),
                  reads=[kt2, q2], writes=[Sps])
            S_out[i] = Sps

        def st_F(i):
            hp, hh, it = items[i]
            c = head_ctx(hp, hh)
            ebp, ebs = c["ebp"], c["ebs"]
            e_ = Et.next()
            pt = PTt.next()
            if it == "s":
                SA, SB = S_out.pop(i)
                kb.op(kb.act, lambda: nc.scalar.activation(e_[:, 0:TS], SA[:, 0:TS], AF.Exp, scale=SCALE), reads=[SA], writes=[e_])
                kb.op(kb.act, lambda: nc.scalar.activation(e_[0:16, 16:32], SB[0:16, 0:TS], AF.Exp, scale=SCALE), reads=[SB], writes=[e_])
                kb.op(kb.dve, lambda: nc.vector.tensor_tensor(pt[:, 0:TS], e_[:, 0:TS], ebs[:, 0:TS], ALU.mult), reads=[e_, ebs], writes=[pt])
                kb.op(kb.dve, lambda: nc.vector.tensor_tensor(pt[0:16, 16:32], e_[0:16, 16:32], ebs[0:16, 16:32], ALU.mult), reads=[e_, ebs], writes=[pt])
                F_out[i] = pt
                return
            j = it
            nq = min(256, SEQ - P * j)
            Sps = S_out.pop(i)
            kb.op(kb.act, lambda: nc.scalar.activation(e_[:, 0:nq], Sps[:, 0:nq], AF.Exp, scale=SCALE), reads=[Sps], writes=[e_])
            kb.op(kb.dve, lambda: nc.vector.tensor_tensor(pt[:, 0:nq], e_[:, 0:nq], ebp[:, 0:nq], ALU.mult), reads=[e_, ebp], writes=[pt])
            F_out[i] = pt

        def st_V(i):
            hp, hh, it = items[i]
            c = head_ctx(hp, hh)
            rows, h, gct = c["rows"], c["h"], c["gct"]
            pt = F_out.pop(i)
            if it == "s":
                Ops, Lps = psO.next(), psL.next()
                kb.op(kb.pe, lambda: nc.tensor.matmul(Ops[rows, 0:TS], VsA[:, g * 64:(g + 1) * 64], pt[:, 0:TS], start=True, stop=False), reads=[VsA, pt], writes=[Ops], inc=False)
                kb.op(kb.pe, lambda: nc.tensor.matmul(Ops[rows, 0:TS], VsB[0:16, g * 64:(g + 1) * 64], pt[0:16, 16:32], start=False, stop=True), reads=[VsB, pt], writes=[Ops], inc=False)
                kb.op(kb.pe, lambda: nc.tensor.matmul(Lps[rows, 0:TS], C.ones[:, 0:64], pt[:, 0:TS], start=True, stop=False), reads=[C.ones, pt], writes=[Lps], inc=False)
                kb.op(kb.pe, lambda: nc.tensor.matmul(Lps[rows, 0:TS], C.ones[0:16, 0:64], pt[0:16, 16:32], start=False, stop=True), reads=[C.ones, pt], writes=[Lps], inc=True)
                finish(Ops, Lps, rows, TS, h, SEQ, gct)
                return
            j = it
            if j == 0:
                c["O"], c["L"] = psO.next(), psL.next()
            Ops, Lps = c["O"], c["L"]
            cb = (j % 4) * P
            kb.op(kb.pe, lambda: nc.tensor.matmul(Ops[rows, cb:cb + P], V2[vc][:, j, vo:vo + 64], pt[:, 0:P], start=(j == 0), stop=True),
                  reads=[V2[vc], pt], writes=[Ops], inc=False)
            kb.op(kb.pe, lambda: nc.tensor.matmul(Lps[rows, cb:cb + P], C.ones[:, 0:64], pt[:, 0:P], start=(j == 0), stop=True),
                  reads=[C.ones, pt], writes=[Lps], inc=True)
            Ops_done, Lps_done = Ops, Lps
            if j % 4 == 3 and j < 15:
                c["O"], c["L"] = psO.next(), psL.next()
                Ops, Lps = c["O"], c["L"]
            if j < 15:
                cb2 = ((j + 1) % 4) * P
                kb.op(kb.pe, lambda: nc.tensor.matmul(Ops[rows, cb2:cb2 + P], V2[vc][:, j, vo:vo + 64], pt[:, P:2 * P], start=True, stop=False),
                      reads=[V2[vc], pt], writes=[Ops], inc=False)
                kb.op(kb.pe, lambda: nc.tensor.matmul(Lps[rows, cb2:cb2 + P], C.ones[:, 0:64], pt[:, P:2 * P], start=True, stop=False),
                      reads=[C.ones, pt], writes=[Lps], inc=True)
            if j % 4 == 3:
                finish(Ops_done, Lps_done, rows, 512, h, (j - 3) * P, gct)

        nI = len(items)
        for step in range(nI + 2):
            if step < nI:
                st_S(step)
            if 0 <= step - 1 < nI:
                st_F(step - 1)
            if 0 <= step - 2 < nI:
                st_V(step - 2)
    kb.phase_end()


_CACHE = {}


def kernel(**inputs):
    inp = {k: np.asarray(v) for k, v in inputs.items()}
    if "kb" not in _CACHE:
        _CACHE["kb"] = build()
    kb = _CACHE["kb"]
    shared = prep_shared(inp)
    used = set(k for k, b in kb.dram.items())
    in_maps = []
    for c in range(8):
        m = prep_core(inp, c, shared)
        in_maps.append({k: np.ascontiguousarray(v, dtype=np.float32) for k, v in m.items() if k in used})
    res = run_bass_kernel_spmd(kb.nc, in_maps, core_ids=list(range(8)))
    R = res.results
    f32 = np.float32
    y_p = np.stack([R[b]["y_out"][:, :SEQ].T for b in range(4)]).astype(f32)
    y_s = np.stack([R[c]["y_out"][:, SEQ:].T for c in range(8)]).astype(f32)

    def tok(name, lo, hi, cores):
        return np.stack([np.stack([R[c][f"{name}{i}"][:, lo:hi].T for c in cores]) for i in range(2)]).astype(f32)

    def st(name, k, cores):
        return np.stack([np.stack([s_unlayout(R[c][f"{name}{i}"][k]) for c in cores]) for i in range(2)]).astype(f32)

    def win(name, lo, hi, cores):
        return np.stack([np.stack([R[c][f"{name}{i}"][:, lo:hi].T.reshape(128, 8, 64) for c in cores]) for i in range(2)]).astype(f32)

    pc, sc = list(range(4)), list(range(8))
    return (np.ascontiguousarray(y_p), np.ascontiguousarray(y_s),
            tok("lat_out", 0, SEQ, pc), tok("kr_out", 0, SEQ, pc), st("hp_out", 0, pc), st("hp_out", 1, pc),
            win("k_out", 0, 128, pc), win("v_out", 0, 128, pc),
            tok("lat_out", SEQ, NT, sc), tok("kr_out", SEQ, NT, sc), st("hs_out", 0, sc), st("hs_out", 1, sc),
            win("k_out", 128, 256, sc), win("v_out", 128, 256, sc))
```
